# Optimizing a Trainium2 kernel written in Bass

```python
import math
import jax, jax.numpy as jnp
from jax import lax
import numpy as np

D_MODEL = 2048
BATCH = 4
SEQ = 2048
DEPTH = 4

CHUNK = 64
N_MIXERS = 3
N_POOL_LAYERS = (DEPTH + 2) // 3
N_SB_LAYERS = (DEPTH + 1) // 3
N_SSM_LAYERS = DEPTH // 3

POOL_WINDOWS = (2, 4, 8, 16)
N_POOL_GROUPS = len(POOL_WINDOWS)
POOL_GROUP_DIM = D_MODEL // N_POOL_GROUPS

SB_HEAD_DIM = 128
SB_HEADS = D_MODEL // SB_HEAD_DIM
Q_BLOCK = 128

SSM_GROUP_CH = 16
SSM_GROUPS = D_MODEL // SSM_GROUP_CH
SSM_STATE = 64
SSM_DT_MIN = 1e-3
SSM_DT_MAX = 1e-1

D_FF = 5632
CONV_WIDTH = 3

RMS_EPS = 1e-6

kernel_name = "hybrid_pool_stickbreak_s5_convffn_trunk"


def rms_norm(x, g):
    xf = x.astype(jnp.float32)
    y = xf * lax.rsqrt(jnp.mean(xf * xf, axis=-1, keepdims=True) + RMS_EPS)
    return (y * g.astype(jnp.float32)).astype(x.dtype)


def multiscale_pool_mixer(h, w, b, scale):
    bsz, seq, _ = h.shape
    hf = h.astype(jnp.float32).reshape(bsz, seq, N_POOL_GROUPS, POOL_GROUP_DIM)
    cs = jnp.cumsum(hf, axis=1)
    cs = jnp.concatenate([jnp.zeros_like(cs[:, :1]), cs], axis=1)
    t = jnp.arange(seq)[:, None]
    win = jnp.array(POOL_WINDOWS, dtype=jnp.int32)[None, :]
    lo = jnp.maximum(t + 1 - win, 0)
    cnt = (t + 1 - lo).astype(jnp.float32)
    grp = jnp.arange(N_POOL_GROUPS)[None, :]
    lower = cs[:, lo, grp]
    mean = (cs[:, 1:] - lower) / cnt[None, :, :, None]
    pooled = mean - hf
    y = jnp.einsum('bsgc,gcd->bsgd', pooled, w.astype(jnp.float32))
    y = y.reshape(bsz, seq, D_MODEL) + b.astype(jnp.float32)
    return (y * scale.astype(jnp.float32)).astype(h.dtype)


def stick_breaking_attention(h, w_qkv, q_gain, k_gain, w_o):
    bsz, seq, _ = h.shape
    qkv = (h @ w_qkv).reshape(bsz, seq, 3, SB_HEADS, SB_HEAD_DIM)
    q = rms_norm(qkv[:, :, 0], q_gain).astype(jnp.float32).transpose(0, 2, 1, 3)
    k = rms_norm(qkv[:, :, 1], k_gain).astype(jnp.float32).transpose(0, 2, 1, 3)
    v = qkv[:, :, 2].transpose(0, 2, 1, 3)
    inv_sqrt_d = 1.0 / math.sqrt(SB_HEAD_DIM)
    outs = []
    for blk in range(seq // Q_BLOCK):
        q0 = blk * Q_BLOCK
        kv_len = q0 + Q_BLOCK
        qb = q[:, :, q0:kv_len]
        kb = k[:, :, :kv_len]
        vb = v[:, :, :kv_len]
        z = jnp.einsum('bhqd,bhkd->bhqk', qb, kb) * inv_sqrt_d
        t_idx = q0 + jnp.arange(Q_BLOCK)[:, None]
        s_idx = jnp.arange(kv_len)[None, :]
        mask = s_idx < t_idx
        log_beta = jax.nn.log_sigmoid(z)
        log_1m_beta = jnp.where(mask, jax.nn.log_sigmoid(-z), 0.0)
        log_remain = lax.cumsum(log_1m_beta, axis=3, reverse=True) - log_1m_beta
        attn = jnp.where(mask, jnp.exp(log_beta + log_remain), 0.0)
        outs.append(jnp.einsum('bhqk,bhkd->bhqd', attn.astype(vb.dtype), vb))
    o = jnp.concatenate(outs, axis=2)
    o = o.transpose(0, 2, 1, 3).reshape(bsz, seq, D_MODEL)
    return o @ w_o


def _ssm_combine(e1, e2):
    a1r, a1i, b1r, b1i = e1
    a2r, a2i, b2r, b2i = e2
    return (a2r * a1r - a2i * a1i,
            a2r * a1i + a2i * a1r,
            a2r * b1r - a2i * b1i + b2r,
            a2r * b1i + a2i * b1r + b2i)


def s5_mixer(h, lam_re, lam_im, log_step, b_re, b_im, c_re, c_im, d_skip, w_glu, b_glu):
    bsz, seq, _ = h.shape
    u = h.astype(jnp.float32).reshape(bsz, seq, SSM_GROUPS, SSM_GROUP_CH)
    lr = lam_re.astype(jnp.float32)
    li = lam_im.astype(jnp.float32)
    step = jnp.exp(log_step.astype(jnp.float32))[:, None]
    mag = jnp.exp(lr * step)
    lb_re = mag * jnp.cos(li * step)
    lb_im = mag * jnp.sin(li * step)
    den = lr * lr + li * li
    f_re = ((lb_re - 1.0) * lr + lb_im * li) / den
    f_im = (lb_im * lr - (lb_re - 1.0) * li) / den
    br = b_re.astype(jnp.float32)
    bi = b_im.astype(jnp.float32)
    bb_re = f_re[..., None] * br - f_im[..., None] * bi
    bb_im = f_re[..., None] * bi + f_im[..., None] * br
    bu_re = jnp.einsum('bsgh,gph->bsgp', u, bb_re)
    bu_im = jnp.einsum('bsgh,gph->bsgp', u, bb_im)
    a_re = jnp.broadcast_to(lb_re, bu_re.shape)
    a_im = jnp.broadcast_to(lb_im, bu_im.shape)
    _, _, xs_re, xs_im = lax.associative_scan(_ssm_combine, (a_re, a_im, bu_re, bu_im), axis=1)
    y = (jnp.einsum('bsgp,ghp->bsgh', xs_re, c_re.astype(jnp.float32))
         - jnp.einsum('bsgp,ghp->bsgh', xs_im, c_im.astype(jnp.float32))
         + d_skip.astype(jnp.float32).reshape(SSM_GROUPS, SSM_GROUP_CH) * u)
    y = jax.nn.gelu(y.reshape(bsz, seq, D_MODEL)).astype(h.dtype)
    gv = y @ w_glu + b_glu
    val, gate = jnp.split(gv, 2, axis=-1)
    return val * jax.nn.sigmoid(gate)


def conv_ffn(h, w_up, conv_w, conv_b, w_down):
    seq = h.shape[1]
    up = h @ w_up
    padded = jnp.pad(up, ((0, 0), (CONV_WIDTH - 1, 0), (0, 0)))
    c = conv_b + sum(conv_w[j] * padded[:, j:j + seq] for j in range(CONV_WIDTH))
    val, gate = jnp.split(c, 2, axis=-1)
    return (jax.nn.silu(gate) * val) @ w_down


def setup_inputs(seed: int = 0) -> dict:
    key = jax.random.key(seed)
    ks = jax.random.split(key, 26)
    f32 = jnp.float32
    nrm = lambda k, shape, s: jax.random.normal(k, shape, f32) * s
    lam_im_base = jnp.pi * jnp.arange(SSM_STATE, dtype=f32)
    return {
        "x": jax.random.normal(ks[0], (BATCH, SEQ, D_MODEL), f32),
        "norm_mix_g": 1.0 + nrm(ks[1], (DEPTH, D_MODEL), 0.02),
        "norm_ffn_g": 1.0 + nrm(ks[2], (DEPTH, D_MODEL), 0.02),
        "pool_w": nrm(ks[3], (N_POOL_LAYERS, N_POOL_GROUPS, POOL_GROUP_DIM, POOL_GROUP_DIM), POOL_GROUP_DIM ** -0.5),
        "pool_b": nrm(ks[4], (N_POOL_LAYERS, D_MODEL), 0.01),
        "pool_scale": 1.0 + nrm(ks[5], (N_POOL_LAYERS, D_MODEL), 0.02),
        "sb_w_qkv": nrm(ks[6], (N_SB_LAYERS, D_MODEL, 3 * D_MODEL), D_MODEL ** -0.5),
        "sb_q_gain": 1.0 + nrm(ks[7], (N_SB_LAYERS, SB_HEAD_DIM), 0.02),
        "sb_k_gain": 1.0 + nrm(ks[8], (N_SB_LAYERS, SB_HEAD_DIM), 0.02),
        "sb_w_o": nrm(ks[9], (N_SB_LAYERS, D_MODEL, D_MODEL), D_MODEL ** -0.5),
        "ssm_lam_re": -0.5 + nrm(ks[10], (N_SSM_LAYERS, SSM_GROUPS, SSM_STATE), 0.01),
        "ssm_lam_im": lam_im_base + nrm(ks[11], (N_SSM_LAYERS, SSM_GROUPS, SSM_STATE), 0.01),
        "ssm_log_step": jax.random.uniform(ks[12], (N_SSM_LAYERS, SSM_GROUPS), f32,
                                           math.log(SSM_DT_MIN), math.log(SSM_DT_MAX)),
        "ssm_b_re": nrm(ks[13], (N_SSM_LAYERS, SSM_GROUPS, SSM_STATE, SSM_GROUP_CH), (2 * SSM_GROUP_CH) ** -0.5),
        "ssm_b_im": nrm(ks[14], (N_SSM_LAYERS, SSM_GROUPS, SSM_STATE, SSM_GROUP_CH), (2 * SSM_GROUP_CH) ** -0.5),
        "ssm_c_re": nrm(ks[15], (N_SSM_LAYERS, SSM_GROUPS, SSM_GROUP_CH, SSM_STATE), (2 * SSM_STATE) ** -0.5),
        "ssm_c_im": nrm(ks[16], (N_SSM_LAYERS, SSM_GROUPS, SSM_GROUP_CH, SSM_STATE), (2 * SSM_STATE) ** -0.5),
        "ssm_d": nrm(ks[17], (N_SSM_LAYERS, D_MODEL), 1.0),
        "ssm_w_glu": nrm(ks[18], (N_SSM_LAYERS, D_MODEL, 2 * D_MODEL), D_MODEL ** -0.5),
        "ssm_b_glu": nrm(ks[19], (N_SSM_LAYERS, 2 * D_MODEL), 0.01),
        "ffn_w_up": nrm(ks[20], (DEPTH, D_MODEL, 2 * D_FF), D_MODEL ** -0.5),
        "ffn_conv_w": nrm(ks[21], (DEPTH, CONV_WIDTH, 2 * D_FF), CONV_WIDTH ** -0.5),
        "ffn_conv_b": nrm(ks[22], (DEPTH, 2 * D_FF), 0.01),
        "ffn_w_down": nrm(ks[23], (DEPTH, D_FF, D_MODEL), D_FF ** -0.5),
    }


def reference(x, norm_mix_g, norm_ffn_g, pool_w, pool_b, pool_scale,
              sb_w_qkv, sb_q_gain, sb_k_gain, sb_w_o,
              ssm_lam_re, ssm_lam_im, ssm_log_step, ssm_b_re, ssm_b_im,
              ssm_c_re, ssm_c_im, ssm_d, ssm_w_glu, ssm_b_glu,
              ffn_w_up, ffn_conv_w, ffn_conv_b, ffn_w_down):
    for i in range(DEPTH):
        kind = i % N_MIXERS
        j = i // N_MIXERS
        h = rms_norm(x, norm_mix_g[i])
        if kind == 0:
            m = multiscale_pool_mixer(h, pool_w[j], pool_b[j], pool_scale[j])
        elif kind == 1:
            m = stick_breaking_attention(h, sb_w_qkv[j], sb_q_gain[j], sb_k_gain[j], sb_w_o[j])
        else:
            m = s5_mixer(h, ssm_lam_re[j], ssm_lam_im[j], ssm_log_step[j], ssm_b_re[j], ssm_b_im[j],
                         ssm_c_re[j], ssm_c_im[j], ssm_d[j], ssm_w_glu[j], ssm_b_glu[j])
        x = x + m
        x = x + conv_ffn(rms_norm(x, norm_ffn_g[i]), ffn_w_up[i], ffn_conv_w[i], ffn_conv_b[i], ffn_w_down[i])
    return x
```

```python
import numpy as np
import concourse.bass as bass
import concourse.mybir as mybir
from concourse.bass_utils import run_bass_kernel_spmd

F32 = mybir.dt.float32
F32R = mybir.dt.float32r
AF = mybir.ActivationFunctionType
ALU = mybir.AluOpType

D = 2048
NCH = 16
SEQ = 2048
BATCH = 4
T = 1024
HX = 16
XW = HX + T
DFF = 5632
NFP = 44
EPS = 1e-6
DEPTH = 4
SAME_ENGINE_SYNC = True


def r_(ap):
    return ap.bitcast(F32R)


class Op:
    __slots__ = ("eng", "fn", "deps", "cls", "inc", "count", "needed")

    def __init__(self, eng, fn, deps, cls, inc):
        self.eng = eng
        self.fn = fn
        self.deps = deps
        self.cls = cls
        self.inc = inc
        self.count = 0
        self.needed = False


class Ring:
    def __init__(self, n):
        self.n = n
        self.i = -1

    def next(self):
        self.i = (self.i + 1) % self.n
        return self.i


class Prog:
    ENGINES = ("pe", "act", "dve", "pool", "sp")

    def __init__(self, nc):
        self.nc = nc
        self.ops = []
        self.last_w = {}
        self.readers = {}

    def _deps(self, reads, writes, me):
        deps = set()
        for k in reads:
            w = self.last_w.get(k)
            if w is not None:
                deps.add(w)
        for k in writes:
            w = self.last_w.get(k)
            if w is not None:
                deps.add(w)
            deps.update(self.readers.get(k, ()))
        for k in reads:
            self.readers.setdefault(k, []).append(me)
        for k in writes:
            self.last_w[k] = me
            self.readers[k] = []
        deps.discard(me)
        return deps

    def op(self, eng, fn, reads=(), writes=(), cls=None, inc=16):
        me = len(self.ops)
        deps = self._deps(reads, writes, me)
        self.ops.append(Op(eng, fn, deps, cls, inc))
        return me

    def barrier(self):
        n = len(self.ops)
        last = {}
        for i in range(n - 1, -1, -1):
            o = self.ops[i]
            key = o.cls if o.cls is not None else o.eng
            if key not in last:
                last[key] = i
        deps = set(last.values())
        for e in self.ENGINES:
            self.ops.append(Op(e, None, set(deps), None, 0))

    def emit(self, block, sems):
        ops = self.ops
        self.barrier()
        for o in ops:
            for d in o.deps:
                ops[d].needed = True
        cnt = {}
        for o in ops:
            if o.cls is not None:
                cnt[o.cls] = cnt.get(o.cls, 0) + o.inc
                o.count = cnt[o.cls]
            elif o.needed and o.fn is not None:
                cnt[o.eng] = cnt.get(o.eng, 0) + 1
                o.count = cnt[o.eng]
            elif o.fn is None:
                o.count = -1
        def semkey(o):
            return o.cls if o.cls is not None else o.eng

        streams = {e: [] for e in self.ENGINES}
        for i, o in enumerate(ops):
            streams[o.eng].append(i)

        def make(engname):
            idxs = streams[engname]

            def body(e):
                seen = {}
                for i in idxs:
                    o = ops[i]
                    waits = {}
                    for d in o.deps:
                        od = ops[d]
                        if od.fn is None:
                            continue
                        k = semkey(od)
                        if od.cls is None and od.eng == engname and not SAME_ENGINE_SYNC:
                            continue
                        if od.cls is None and od.eng == engname and engname == "pe":
                            continue
                        if od.count > waits.get(k, 0):
                            waits[k] = od.count
                    for k, v in waits.items():
                        if v > seen.get(k, 0):
                            e.wait_ge(sems[k], v)
                            seen[k] = v
                    if o.fn is None:
                        continue
                    ins = o.fn(e)
                    if o.cls is not None:
                        ins.then_inc(sems[o.cls], o.inc)
                    elif o.needed:
                        ins.then_inc(sems[o.eng], 1)

            return body

        block.tensor(make("pe"))
        block.scalar(make("act"))
        block.vector(make("dve"))
        block.gpsimd(make("pool"))
        block.sync(make("sp"))

    def classes(self):
        s = set()
        for o in self.ops:
            if o.cls is not None:
                s.add(o.cls)
        return sorted(s)


class Builder:
    def __init__(self, nc, layers, first, last):
        self.nc = nc
        self.layers = layers
        self.first = first
        self.last = last
        self.P = Prog(nc)

    def declare_dram(self):
        nc = self.nc
        self.in_names = []

        def dt(name, shape, kind="ExternalInput"):
            if kind == "ExternalInput":
                self.in_names.append(name)
            return nc.dram_tensor(name, shape, F32, kind=kind).ap()
        self.d_x = dt("xT", [128, NCH, XW])
        self.d_out = dt("outT", [128, NCH, T], "ExternalOutput")
        self.d_gain = dt("gains", [128, 2 * DEPTH * NCH])
        self.d_gate = dt("gate", [128, 1])
        self.d_invc = dt("invc", [128, 4 * 16])
        self.d_wu = {}
        self.d_wd = {}
        self.d_cw = {}
        self.d_pw = {}
        self.d_pbs = {}
        for l in self.layers:
            self.d_wu[l] = dt(f"wu{l}", [2 * NFP, 128, NCH * 128])
            self.d_wd[l] = dt(f"wd{l}", [NFP, 128, D])
            self.d_cw[l] = dt(f"cw{l}", [128, 2 * NFP * 4])
            if l % 3 == 0:
                self.d_pw[l] = dt(f"pw{l}", [4, 128, 4 * 512])
                self.d_pbs[l] = dt(f"pbs{l}", [128, 2 * NCH])
            if l % 3 == 2:
                self.d_sprm = dt(f"sprm{l}", [128, 3 * 64])
                self.d_sbt = dt(f"sbt{l}", [64, 128, 256])
                self.d_sct = dt(f"sct{l}", [64, 128, 256])
                self.d_scst = dt(f"scst{l}", [128, 16 + 32])
                self.d_wglu = dt(f"wglu{l}", [32, 128, NCH * 128])
                if not hasattr(self, "d_tpos"):
                    self.d_tpos = dt("tpos", [128, T])
                self.yd = nc.dram_tensor("yd", [NCH, 128, T], F32).ap()
                self.c_ins = nc.dram_tensor("c_ins", [128, 128], F32).ap()
                self.c_outs = nc.dram_tensor("c_outs", [256, 128], F32).ap()
            if l % 3 == 1:
                self.d_wqk = dt(f"wqk{l}", [32, 128, NCH * 128])
                self.d_wv = dt(f"wv{l}", [D, D])
                self.d_wo = dt(f"wo{l}", [NCH, 128, NCH * 128])
                self.d_tri = dt("tri", [128, 128])
                self.d_acst = dt(f"acst{l}", [128, 18])
                if not hasattr(self, "d_tpos"):
                    self.d_tpos = dt("tpos", [128, T])
                self.qT_d = nc.dram_tensor("qT_d", [NCH, 128, T], F32).ap()
                self.kin = nc.dram_tensor("kin", [D, T], F32).ap()
                self.kall = nc.dram_tensor("kall", [4, 1024, T], F32).ap()
                self.vin = nc.dram_tensor("vin", [T, D], F32).ap()
                self.vall = nc.dram_tensor("vall", [4, 512, D], F32).ap()
        self.c_in2 = nc.dram_tensor("c_in2", [128, NCH * 2], F32).ap()
        self.c_out2 = nc.dram_tensor("c_out2", [256, NCH * 2], F32).ap()
        self.c_in16 = nc.dram_tensor("c_in16", [128, NCH * 16], F32).ap()
        self.c_out16 = nc.dram_tensor("c_out16", [256, NCH * 16], F32).ap()

    def mm_group(self, ps_ap, pairs, reads, writes):
        def fn(e, ps_ap=ps_ap, pairs=pairs):
            n = len(pairs)
            ins = None
            for i, (l, r) in enumerate(pairs):
                ins = e.matmul(ps_ap, lhsT=l, rhs=r, start=(i == 0), stop=(i == n - 1))
            return ins

        self.P.op("pe", fn, reads, writes)

    def dma(self, out, in_, reads, writes, cls, eng="sp"):
        self.P.op(eng, lambda e, out=out, in_=in_: e.dma_start(out=out, in_=in_), reads, writes, cls=cls, inc=16)

    def allgather(self, cin, cout, reads, writes):
        pairs = [[0, 1], [2, 3], [4, 5], [6, 7]][: self.ncores // 2]

        def fn(e, cin=cin, cout=cout):
            return e.collective_compute("AllGather", ALU.bypass, replica_groups=pairs, ins=[cin], outs=[cout])

        self.P.op("pool", fn, reads, writes, cls="cc", inc=1)

    def ew(self, eng, fn, reads, writes):
        self.P.op(eng, fn, reads, writes)

    def rmsnorm(self, gidx, xc0, blocks):
        X, HT, SQ, RSTD, ONES, GAIN = self.X, self.HT, self.SQ, self.RSTD, self.ONES, self.GAIN
        PS = self.PS
        bank = 7
        for (b0, w) in blocks:
            for c in range(NCH):
                s = self.sq_ring.next()
                self.ew("act", lambda e, s=s, c=c, b0=b0, w=w: e.activation(
                    out=r_(SQ[:, s, 0:w]), in_=X[:, c, xc0 + b0:xc0 + b0 + w], func=AF.Square),
                    reads=[("X", c)], writes=[("sq", s)])
                self.P.op("pe", lambda e, s=s, c=c, w=w: e.matmul(
                    PS[bank][:, 0:w], lhsT=r_(ONES[:, :]), rhs=r_(SQ[:, s, 0:w]),
                    start=(c == 0), stop=(c == NCH - 1)),
                    reads=[("sq", s), ("ones",)], writes=[("ps", bank)])
            self.ew("dve", lambda e, b0=b0, w=w: e.tensor_scalar(
                out=RSTD[:, b0:b0 + w], in0=PS[bank][:, 0:w], scalar1=1.0 / D, scalar2=float(EPS),
                op0=ALU.mult, op1=ALU.add),
                reads=[("ps", bank)], writes=[("rstd",)])
            self.ew("act", lambda e, b0=b0, w=w: e.activation(
                out=RSTD[:, b0:b0 + w], in_=RSTD[:, b0:b0 + w], func=AF.Sqrt),
                reads=[("rstd",)], writes=[("rstd",)])
            self.ew("dve", lambda e, b0=b0, w=w: e.reciprocal(
                out=RSTD[:, b0:b0 + w], in_=RSTD[:, b0:b0 + w]),
                reads=[("rstd",)], writes=[("rstd",)])
            for c in range(NCH):
                if c % 2 == 0:
                    self.ew("dve", lambda e, c=c, b0=b0, w=w: e.scalar_tensor_tensor(
                        out=r_(HT[:, c, b0:b0 + w]), in0=X[:, c, xc0 + b0:xc0 + b0 + w],
                        scalar=GAIN[:, gidx * NCH + c:gidx * NCH + c + 1], in1=RSTD[:, b0:b0 + w],
                        op0=ALU.mult, op1=ALU.mult),
                        reads=[("X", c), ("rstd",), ("gain",)], writes=[("HT", c)])
                else:
                    self.ew("pool", lambda e, c=c, b0=b0, w=w: e.tensor_scalar(
                        out=r_(HT[:, c, b0:b0 + w]), in0=X[:, c, xc0 + b0:xc0 + b0 + w],
                        scalar1=GAIN[:, gidx * NCH + c:gidx * NCH + c + 1], scalar2=None, op0=ALU.mult),
                        reads=[("X", c), ("gain",)], writes=[("HT", c)])
                    self.ew("pool", lambda e, c=c, b0=b0, w=w: e.tensor_tensor(
                        out=r_(HT[:, c, b0:b0 + w]), in0=HT[:, c, b0:b0 + w], in1=RSTD[:, b0:b0 + w], op=ALU.mult),
                        reads=[("HT", c), ("rstd",)], writes=[("HT", c)])

    def exchange_tail(self, ncols):
        X, TB = self.X, self.TB
        cin = self.c_in2 if ncols == 2 else self.c_in16
        cout = self.c_out2 if ncols == 2 else self.c_out16
        tag = f"t{ncols}"
        xkeys = [("X", c) for c in range(NCH)]
        self.dma(out=cin.rearrange("p (c n) -> p c n", n=ncols), in_=X[:, :, XW - ncols:XW],
                 reads=xkeys, writes=[("cin", tag)], cls="tst")
        self.allgather(cin, cout, reads=[("cin", tag)], writes=[("cout", tag)])
        self.dma(out=TB[:, :, 0:ncols], in_=cout[0:128, :].rearrange("p (c n) -> p c n", n=ncols),
                 reads=[("cout", tag)], writes=[("TB",)], cls="tld")
        GATE = self.GATE
        self.ew("dve", lambda e: e.tensor_scalar(
            out=X[:, :, HX - ncols:HX], in0=TB[:, :, 0:ncols], scalar1=GATE[:, 0:1], scalar2=None,
            op0=ALU.mult), reads=[("TB",), ("gate",)], writes=xkeys)

    def pool_mixer(self, l):
        P = self.P
        X, HT, PS = self.X, self.HT, self.PS
        PW, PL, TA, TBF, T16, TMP, PBS, INVC = self.PW, self.PL, self.TA, self.TBF, self.T16, self.TMP, self.PBS, self.INVC
        self.dma(out=PBS[:, 0:2 * NCH], in_=self.d_pbs[l], reads=[], writes=[("pbs",)], cls="c_pbs")
        self.ew("dve", lambda e: e.tensor_tensor(out=PBS[:, 0:NCH], in0=PBS[:, 0:NCH], in1=PBS[:, NCH:2 * NCH], op=ALU.mult),
                reads=[("pbs",)], writes=[("pbs",)])
        self.rmsnorm(l, 0, [(0, 346), (346, 346), (692, 348)])
        for g in range(4):
            w = 2 << g
            ws = self.pw_ring.next()
            self.dma(out=r_(PW[:, ws, :]), in_=r_(self.d_pw[l][g]), reads=[], writes=[("pw", ws)], cls=f"pw{ws}")
            for kk in range(4):
                c = 4 * g + kk
                eng = "dve" if kk % 2 == 0 else "pool"
                src = HT[:, c, :]
                eng = "dve"
                bufs = [TA[:, 0, :], TA[:, 1, :]]
                bkeys = [("ta", 0), ("ta", 1)]
                skey = ("HT", c)
                sh = 1
                lo = 0
                for st in range(g + 1):
                    dst = bufs[st % 2]
                    dkey = bkeys[st % 2]
                    lo2 = lo + sh
                    self.ew(eng, lambda e, dst=dst, src=src, lo2=lo2, sh=sh: e.tensor_tensor(
                        out=dst[:, lo2:XW], in0=src[:, lo2:XW], in1=src[:, lo2 - sh:XW - sh], op=ALU.add),
                        reads=[skey], writes=[dkey])
                    src, skey = dst, dkey
                    lo = lo2
                    sh *= 2
                self.ew(eng, lambda e, src=src, kk=kk, c=c, w=w: e.scalar_tensor_tensor(
                    out=r_(PL[:, kk, :]), in0=src[:, HX:XW], scalar=1.0 / w, in1=HT[:, c, HX:XW],
                    op0=ALU.mult, op1=ALU.subtract),
                    reads=[skey, ("HT", c)], writes=[("pl", kk)])
                self.ew(eng, lambda e, src=src, kk=kk, g=g: e.tensor_tensor(
                    out=T16[:, kk, :], in0=src[:, HX:HX + 16], in1=INVC[:, g * 16:(g + 1) * 16], op=ALU.mult),
                    reads=[skey, ("invc",)], writes=[("t16", kk)])
                self.ew(eng, lambda e, kk=kk, c=c: e.tensor_tensor(
                    out=r_(PL[:, kk, 0:16]), in0=T16[:, kk, :], in1=HT[:, c, HX:HX + 16], op=ALU.subtract),
                    reads=[("t16", kk), ("HT", c)], writes=[("pl", kk)])
            for mm in range(4):
                m = 4 * g + mm
                for th in range(2):
                    bank = 4 + self.dn_ring.next()
                    pairs = [(r_(PW[:, ws, kk * 512 + mm * 128:kk * 512 + (mm + 1) * 128]),
                              r_(PL[:, kk, th * 512:(th + 1) * 512])) for kk in range(4)]
                    self.mm_group(PS[bank][:, :], pairs,
                                  reads=[("pw", ws)] + [("pl", kk) for kk in range(4)], writes=[("ps", bank)])
                    ts = self.tmp_ring.next()
                    self.ew("act", lambda e, bank=bank, ts=ts, m=m: e.activation(
                        out=TMP[:, ts, :], in_=PS[bank][:, :], func=AF.Identity,
                        bias=PBS[:, m:m + 1], scale=PBS[:, NCH + m:NCH + m + 1]),
                        reads=[("ps", bank), ("pbs",)], writes=[("tmp", ts)])
                    self.ew("dve", lambda e, ts=ts, m=m, th=th: e.tensor_tensor(
                        out=X[:, m, HX + th * 512:HX + (th + 1) * 512], in0=X[:, m, HX + th * 512:HX + (th + 1) * 512],
                        in1=TMP[:, ts, :], op=ALU.add),
                        reads=[("tmp", ts), ("X", m)], writes=[("X", m)])

    def attention(self, l):
        P = self.P
        X, HT, PS, WU, SQ, RSTD, ONES, ONESF = self.X, self.HT, self.PS, self.WU, self.SQ, self.RSTD, self.ONES, self.ONESF
        TRI = self.ACST[:, 0:128]
        KPOS = self.PBS[:, 0:16]
        QKG = self.PBS[:, 16:18]
        htkeys = [("HT", k) for k in range(NCH)]
        self.dma(out=r_(TRI), in_=r_(self.d_tri), reads=[], writes=[("tri",)], cls="c_tri")
        self.dma(out=self.PBS[:, 0:18], in_=self.d_acst, reads=[], writes=[("acst",)], cls="c_pbs")
        self.ew("dve", lambda e: e.tensor_scalar(out=QKG[:, 0:1], in0=QKG[:, 0:1], scalar1=float(128 ** -0.5), scalar2=None,
                                                  op0=ALU.mult), reads=[("acst",)], writes=[("acst",)])
        self.rmsnorm(l, HX, [(0, 512), (512, 512)])
        QKS = self.STG
        VST = self.CG[:, :].rearrange("p (a n) -> p a n", a=2)
        for oc in range(32):
            qk = oc // 16
            hh = oc % 16
            wsl = []
            for kh in range(2):
                s_ = self.wu_ring.next()
                self.dma(out=r_(WU[:, s_, :]), in_=r_(self.d_wqk[oc][:, kh * 1024:(kh + 1) * 1024]),
                         reads=[], writes=[("wu", s_)], cls=f"wu{s_}")
                wsl.append(s_)
            st = oc % 2
            for th in range(2):
                bank = self.up_ring.next()
                pairs = [(r_(WU[:, wsl[k // 8], (k % 8) * 128:(k % 8 + 1) * 128]),
                          r_(HT[:, k, th * 512:(th + 1) * 512])) for k in range(NCH)]
                self.mm_group(PS[bank][:, :], pairs, reads=[("wu", wsl[0]), ("wu", wsl[1])] + htkeys, writes=[("ps", bank)])
                sq = self.sq_ring.next()
                self.ew("act", lambda e, bank=bank, sq=sq: e.activation(out=r_(SQ[:, sq, :]), in_=PS[bank][:, :], func=AF.Square),
                        reads=[("ps", bank)], writes=[("sq", sq)])
                self.P.op("pe", lambda e, sq=sq: e.matmul(PS[7][:, :], lhsT=r_(ONES[:, :]), rhs=r_(SQ[:, sq, :]), start=True, stop=True),
                          reads=[("sq", sq), ("ones",)], writes=[("ps", 7)])
                self.ew("dve", lambda e: e.tensor_scalar(out=RSTD[:, 0:512], in0=PS[7][:, :], scalar1=1.0 / 128, scalar2=float(EPS),
                                                          op0=ALU.mult, op1=ALU.add), reads=[("ps", 7)], writes=[("rstd",)])
                self.ew("act", lambda e: e.activation(out=RSTD[:, 0:512], in_=RSTD[:, 0:512], func=AF.Sqrt),
                        reads=[("rstd",)], writes=[("rstd",)])
                self.ew("dve", lambda e: e.reciprocal(out=RSTD[:, 0:512], in_=RSTD[:, 0:512]), reads=[("rstd",)], writes=[("rstd",)])
                self.ew("dve", lambda e, bank=bank, st=st, th=th, qk=qk: e.scalar_tensor_tensor(
                    out=QKS[:, st, th * 512:(th + 1) * 512], in0=PS[bank][:, :], scalar=QKG[:, qk:qk + 1], in1=RSTD[:, 0:512],
                    op0=ALU.mult, op1=ALU.mult), reads=[("ps", bank), ("rstd",), ("acst",)], writes=[("qks", st)])
            dst = self.qT_d[hh] if qk == 0 else self.kin[hh * 128:(hh + 1) * 128, :]
            self.dma(out=dst, in_=QKS[:, st, 0:T], reads=[("qks", st)], writes=[("qkd", oc)], cls=f"qks{st}")
        for tg in range(2):
            for cb in range(4):
                for k in range(NCH):
                    s_ = self.wu_ring.next()
                    self.dma(out=r_(WU[:, s_, 0:512]), in_=r_(self.d_wv[k * 128:(k + 1) * 128, cb * 512:(cb + 1) * 512]),
                             reads=[], writes=[("wu", s_)], cls=f"wu{s_}")
                    for tb in range(4):
                        self.P.op("pe", lambda e, tb=tb, k=k, s_=s_, tg=tg: e.matmul(
                            PS[tb][:, :], lhsT=r_(HT[:, k, tg * 512 + tb * 128:tg * 512 + (tb + 1) * 128]), rhs=r_(WU[:, s_, 0:512]),
                            start=(k == 0), stop=(k == NCH - 1)), reads=[("wu", s_), ("HT", k)], writes=[("ps", tb)])
                for tb in range(4):
                    vs = self.tmp_ring.next()
                    self.ew("act", lambda e, tb=tb, vs=vs: e.activation(out=VST[:, vs, :], in_=PS[tb][:, :], func=AF.Copy),
                            reads=[("ps", tb)], writes=[("vst", vs)])
                    r0 = tg * 512 + tb * 128
                    self.dma(out=self.vin[r0:r0 + 128, cb * 512:(cb + 1) * 512], in_=VST[:, vs, :],
                             reads=[("vst", vs)], writes=[("vind", tg, cb, tb)], cls=f"vst{vs}")
        allk = [("qkd", oc) for oc in range(16, 32)]
        allv = [("vind", tg, cb, tb) for tg in range(2) for cb in range(4) for tb in range(4)]
        for g in range(4):
            self.allgather(self.kin[g * 512:(g + 1) * 512, :], self.kall[g], reads=allk, writes=[("kall", g)])
        for g in range(4):
            self.allgather(self.vin[g * 256:(g + 1) * 256, :], self.vall[g], reads=allv, writes=[("vall", g)])
        P.barrier()
        TPOS = RSTD[:, 0:T]
        self.dma(out=TPOS, in_=self.d_tpos, reads=[], writes=[("rstd",)], cls="c_tpos")
        HTf = HT[:, :, :].rearrange("p c n -> p (c n)")
        KT = HTf[:, 0:4096].rearrange("p (a n) -> p a n", a=2)
        VS = HTf[:, 4096:8192].rearrange("p (a k d) -> p a k d", a=2, k=16)
        OT = HTf[:, 8192:16384].rearrange("p (h n) -> p h n", h=16)
        Gf = self.G[:, :, :].rearrange("p a n -> p (a n)")
        SPM = Gf[:, 0:1024].rearrange("p (a n) -> p a n", a=2)
        ATT = Gf[:, 1024:2048].rearrange("p (a n) -> p a n", a=2)
        LSUM = Gf[:, 2048:2560]
        WDf = self.WD[:, :, :].rearrange("p a n -> p (a n)")
        QT = WDf[:, 0:1024].rearrange("p (a n) -> p a n", a=2)
        NQT = WDf[:, 1024:2048].rearrange("p (a n) -> p a n", a=2)
        STf = self.STG[:, :, :].rearrange("p a n -> p (a n)")
        EB = STf[:, 0:1024].rearrange("p (a n) -> p a n", a=2)
        AR = STf[:, 1024:2048].rearrange("p (a n) -> p a n", a=2)
        MASK = self.CG[:, :].rearrange("p (a n) -> p a n", a=2)
        z_ring, c_ring = Ring(2), Ring(2)
        for c in range(2):
            kbmax = 12 if c == 0 else 16
            tiles = []
            for h in range(16):
                for i, kb in enumerate(range(kbmax - 1, -1, -1)):
                    tiles.append(dict(h=h, kb=kb, i=i, last=(i == kbmax - 1)))
            n = len(tiles)

            def stage1(tl, tau):
                h, kb = tl["h"], tl["kb"]
                hs = h % 2
                if tl["i"] == 0:
                    for half in range(2):
                        r0 = half * 512 + (h % 4) * 128
                        self.dma(out=r_(KT[:, hs, half * T:(half + 1) * T]),
                                 in_=r_(self.kall[h // 4][r0:r0 + 128, :]),
                                 reads=[("kall", h // 4)], writes=[("kt", hs, half)], cls=f"kt{hs}{half}")
                    for half in range(2):
                        for g in range(4):
                            pc = half * 4 + g
                            if pc * 2 >= kbmax:
                                continue
                            self.dma(out=r_(VS[:, hs, pc * 2:pc * 2 + 2, :]),
                                     in_=r_(self.vall[g][half * 256:(half + 1) * 256, h * 128:(h + 1) * 128].rearrange("(k p) d -> p k d", p=128)),
                                     reads=[("vall", g)], writes=[("vs", hs, pc)], cls=f"vs{hs}{pc}")
                    self.dma(out=r_(QT[:, hs, :]), in_=r_(self.qT_d[h][:, c * 512:(c + 1) * 512]),
                             reads=[("qkd", h)], writes=[("qt", hs)], cls=f"qt{hs}")
                    self.ew("dve", lambda e, hs=hs: e.tensor_scalar(out=r_(NQT[:, hs, :]), in0=QT[:, hs, :], scalar1=-1.0, scalar2=None,
                                                                   op0=ALU.mult), reads=[("qt", hs)], writes=[("nqt", hs)])
                zb = z_ring.next()
                tl["zb"] = zb
                sl = tau % 2
                tl["sl"] = sl
                self.P.op("pe", lambda e, zb=zb, hs=hs, kb=kb: e.matmul(PS[zb][:, :], lhsT=r_(KT[:, hs, kb * 128:(kb + 1) * 128]),
                                                                     rhs=r_(QT[:, hs, :]), start=True, stop=True),
                          reads=[("kt", hs, kb // 8), ("qt", hs)], writes=[("ps", zb)])
                self.ew("act", lambda e, zb=zb, sl=sl: e.activation(out=EB[:, sl, :], in_=PS[zb][:, :], func=AF.Exp),
                        reads=[("ps", zb)], writes=[("eb", sl)])
                self.ew("act", lambda e, sl=sl: e.activation(out=EB[:, sl, :], in_=EB[:, sl, :], func=AF.Ln, bias=ONESF[:, 0:1]),
                        reads=[("eb", sl), ("onesf",)], writes=[("eb", sl)])
                self.ew("dve", lambda e, sl=sl, kb=kb, c=c: e.tensor_scalar(out=MASK[:, sl, :], in0=TPOS[:, c * 512:(c + 1) * 512],
                                                                     scalar1=KPOS[:, kb:kb + 1], scalar2=None, op0=ALU.is_gt),
                        reads=[("rstd",), ("acst",)], writes=[("mask", sl)])
                self.ew("pool", lambda e, sl=sl: e.tensor_tensor(out=r_(SPM[:, sl, :]), in0=EB[:, sl, :], in1=MASK[:, sl, :], op=ALU.mult),
                        reads=[("eb", sl), ("mask", sl)], writes=[("spm", sl)])

            def stage2(tl):
                h, kb, sl = tl["h"], tl["kb"], tl["sl"]
                hs = h % 2
                cb_ = 2 + c_ring.next()
                first = tl["i"] == 0

                def fn(e, cb_=cb_, sl=sl, hs=hs, kb=kb, first=first):
                    e.matmul(PS[cb_][:, :], lhsT=r_(TRI), rhs=r_(SPM[:, sl, :]), start=True, stop=False)
                    if not first:
                        e.matmul(PS[cb_][:, :], lhsT=r_(ONES[:, :]), rhs=r_(LSUM), start=False, stop=False)
                    return e.matmul(PS[cb_][:, :], lhsT=r_(KT[:, hs, kb * 128:(kb + 1) * 128]), rhs=r_(NQT[:, hs, :]),
                                    start=False, stop=True)

                self.P.op("pe", fn, reads=[("spm", sl), ("lsum",), ("kt", hs, kb // 8), ("nqt", hs), ("tri",), ("ones",)], writes=[("ps", cb_)])
                if first:
                    self.ew("pool", lambda e, sl=sl: e.tensor_copy(out=r_(LSUM), in_=SPM[:, sl, :]),
                            reads=[("spm", sl)], writes=[("lsum",)])
                else:
                    self.ew("pool", lambda e, sl=sl: e.tensor_tensor(out=r_(LSUM), in0=LSUM, in1=SPM[:, sl, :], op=ALU.add),
                            reads=[("spm", sl), ("lsum",)], writes=[("lsum",)])
                self.ew("act", lambda e, cb_=cb_, sl=sl: e.activation(out=AR[:, sl, :], in_=PS[cb_][:, :], func=AF.Exp, scale=-1.0),
                        reads=[("ps", cb_)], writes=[("ar", sl)])
                self.ew("dve", lambda e, sl=sl: e.tensor_tensor(out=r_(ATT[:, sl, :]), in0=AR[:, sl, :], in1=MASK[:, sl, :], op=ALU.mult),
                        reads=[("ar", sl), ("mask", sl)], writes=[("att", sl)])

            def stage3(tl):
                h, kb, sl = tl["h"], tl["kb"], tl["sl"]
                hs = h % 2
                ob = 4 + hs
                self.P.op("pe", lambda e, ob=ob, hs=hs, kb=kb, sl=sl, tl=tl: e.matmul(
                    PS[ob][:, :], lhsT=r_(VS[:, hs, kb, :]), rhs=r_(ATT[:, sl, :]), start=(tl["i"] == 0), stop=tl["last"]),
                    reads=[("vs", hs, kb // 2), ("att", sl)], writes=[("ps", ob)])
                if tl["last"]:
                    self.ew("act", lambda e, ob=ob, h=h: e.activation(out=r_(OT[:, h, :]), in_=PS[ob][:, :], func=AF.Copy),
                            reads=[("ps", ob)], writes=[("ot", h)])

            for tau in range(n + 2):
                if tau < n:
                    stage1(tiles[tau], tau)
                if 0 <= tau - 1 < n:
                    stage2(tiles[tau - 1])
                if 0 <= tau - 2 < n:
                    stage3(tiles[tau - 2])
            for m in range(NCH):
                wsl = []
                for kh in range(2):
                    s_ = self.wu_ring.next()
                    self.dma(out=r_(WU[:, s_, :]), in_=r_(self.d_wo[m][:, kh * 1024:(kh + 1) * 1024]),
                             reads=[], writes=[("wu", s_)], cls=f"wu{s_}")
                    wsl.append(s_)
                pairs = [(r_(WU[:, wsl[h // 8], (h % 8) * 128:(h % 8 + 1) * 128]), r_(OT[:, h, :])) for h in range(16)]
                self.mm_group(PS[6][:, :], pairs, reads=[("wu", wsl[0]), ("wu", wsl[1])] + [("ot", h) for h in range(16)],
                              writes=[("ps", 6)])
                self.ew("dve", lambda e, m=m, c=c: e.tensor_tensor(
                    out=X[:, m, HX + c * 512:HX + (c + 1) * 512], in0=X[:, m, HX + c * 512:HX + (c + 1) * 512],
                    in1=PS[6][:, :], op=ALU.add), reads=[("ps", 6), ("X", m)], writes=[("X", m)])

    def s5(self, l):
        P = self.P
        X, HT, PS, WU, RSTD, ONESF = self.X, self.HT, self.PS, self.WU, self.RSTD, self.ONESF
        SP = self.SPRM
        LR, LI, LS = SP[:, 0, :], SP[:, 1, :], SP[:, 2, :]
        STEP, MAG, U, FRE, FIM, NFIM = SP[:, 3, :], SP[:, 4, :], SP[:, 5, :], SP[:, 6, :], SP[:, 7, :], SP[:, 8, :]
        W1, W2, W3, W4 = SP[:, 9, :], SP[:, 10, :], SP[:, 11, :], SP[:, 12, :]
        NEGPI = self.NEGPI
        SCST = self.PBS
        STATE, INIT = self.STATE, self.INIT
        TWO_PI = float(2 * np.pi)
        RC = 12582912.0
        kp = [("sprm",)]
        self.dma(out=SP[:, 0:3, :], in_=self.d_sprm.rearrange("p (a n) -> p a n", a=3), reads=[], writes=kp, cls="c_sprm")
        self.dma(out=SCST[:, 0:48], in_=self.d_scst, reads=[], writes=[("acst",)], cls="c_pbs")
        self.ew("pool", lambda e: e.memset(NEGPI[:, :], float(-np.pi)), reads=[], writes=[("negpi",)])
        d = lambda fn: self.ew("dve", fn, reads=kp + [("negpi",)], writes=kp)
        a_ = lambda fn: self.ew("act", fn, reads=kp + [("negpi",)], writes=kp)
        a_(lambda e: e.activation(out=STEP, in_=LS, func=AF.Exp))
        d(lambda e: e.tensor_tensor(out=W1, in0=LR, in1=STEP, op=ALU.mult))
        a_(lambda e: e.activation(out=MAG, in_=W1, func=AF.Exp))
        d(lambda e: e.tensor_tensor(out=U, in0=LI, in1=STEP, op=ALU.mult))
        d(lambda e: e.tensor_scalar(out=U, in0=U, scalar1=float(1.0 / (2 * np.pi)), scalar2=None, op0=ALU.mult))
        d(lambda e: e.tensor_scalar(out=W3, in0=U, scalar1=RC, scalar2=None, op0=ALU.add))
        d(lambda e: e.scalar_tensor_tensor(out=W1, in0=W3, scalar=RC, in1=U, op0=ALU.subtract, op1=ALU.subtract))
        d(lambda e: e.tensor_scalar(out=W4, in0=U, scalar1=0.25, scalar2=None, op0=ALU.add))
        d(lambda e: e.tensor_scalar(out=W3, in0=W4, scalar1=RC, scalar2=None, op0=ALU.add))
        d(lambda e: e.scalar_tensor_tensor(out=W2, in0=W3, scalar=RC, in1=W4, op0=ALU.subtract, op1=ALU.subtract))
        a_(lambda e: e.activation(out=W1, in_=W1, func=AF.Sin, scale=TWO_PI))
        a_(lambda e: e.activation(out=W2, in_=W2, func=AF.Sin, scale=TWO_PI))
        d(lambda e: e.scalar_tensor_tensor(out=W3, in0=W2, scalar=-1.0, in1=MAG, op0=ALU.mult, op1=ALU.mult))
        d(lambda e: e.scalar_tensor_tensor(out=W4, in0=W1, scalar=-1.0, in1=MAG, op0=ALU.mult, op1=ALU.mult))
        d(lambda e: e.tensor_scalar(out=W3, in0=W3, scalar1=-1.0, scalar2=None, op0=ALU.add))
        d(lambda e: e.tensor_tensor(out=W1, in0=LR, in1=LR, op=ALU.mult))
        d(lambda e: e.tensor_tensor(out=W2, in0=LI, in1=LI, op=ALU.mult))
        d(lambda e: e.tensor_tensor(out=W1, in0=W1, in1=W2, op=ALU.add))
        d(lambda e: e.reciprocal(out=W1, in_=W1))
        d(lambda e: e.tensor_tensor(out=FRE, in0=W3, in1=LR, op=ALU.mult))
        d(lambda e: e.tensor_tensor(out=W2, in0=W4, in1=LI, op=ALU.mult))
        d(lambda e: e.tensor_tensor(out=FRE, in0=FRE, in1=W2, op=ALU.add))
        d(lambda e: e.tensor_tensor(out=FRE, in0=FRE, in1=W1, op=ALU.mult))
        d(lambda e: e.tensor_tensor(out=FIM, in0=W4, in1=LR, op=ALU.mult))
        d(lambda e: e.tensor_tensor(out=W2, in0=W3, in1=LI, op=ALU.mult))
        d(lambda e: e.tensor_tensor(out=FIM, in0=FIM, in1=W2, op=ALU.subtract))
        d(lambda e: e.tensor_tensor(out=FIM, in0=FIM, in1=W1, op=ALU.mult))
        d(lambda e: e.tensor_scalar(out=NFIM, in0=FIM, scalar1=-1.0, scalar2=None, op0=ALU.mult))
        self.rmsnorm(l, HX, [(0, 512), (512, 512)])
        P.barrier()
        TG = RSTD[:, 0:T]
        self.dma(out=TG, in_=self.d_tpos, reads=[], writes=[("rstd",)], cls="c_tpos")
        STf = self.STG[:, :, :].rearrange("p a n -> p (a n)")
        COSN, SINN, BUR, BUI = STf[:, 0:512], STf[:, 512:1024], STf[:, 1024:1536], STf[:, 1536:2048]
        T1, T2 = self.CG[:, 0:512], self.CG[:, 512:1024]
        Gf = self.G[:, :, :].rearrange("p a n -> p (a n)")
        XR = Gf[:, 0:1024].rearrange("p (a n) -> p a n", a=2)
        XIN = Gf[:, 1024:2048].rearrange("p (a n) -> p a n", a=2)
        YO = Gf[:, 2048:3072].rearrange("p (a n) -> p a n", a=2)
        WDf = self.WD[:, :, :].rearrange("p a n -> p (a n)")
        BW = WDf[:, 0:512].rearrange("p (a n) -> p a n", a=2)
        CWt = WDf[:, 512:1024].rearrange("p (a n) -> p a n", a=2)
        bw_ring, cw_ring, x_ring, b_ring = Ring(2), Ring(2), Ring(2), Ring(2)
        keys_t = [("cosn",), ("sinn",), ("bur",), ("bui",), ("t1",), ("t2",)]

        def tile_pass(ct, second):
            c = ct // 4
            bs = bw_ring.next()
            self.dma(out=r_(BW[:, bs, :]), in_=r_(self.d_sbt[ct]), reads=[], writes=[("bw", bs)], cls=f"bw{bs}")
            if second:
                cs = cw_ring.next()
                self.dma(out=r_(CWt[:, cs, :]), in_=r_(self.d_sct[ct]), reads=[], writes=[("cwt", cs)], cls=f"cwt{cs}")
            for tb in range(2):
                blk = slice(tb * 512, (tb + 1) * 512)
                pb = 2 * b_ring.next()
                self.P.op("pe", lambda e, pb=pb, bs=bs, c=c, blk=blk: e.matmul(PS[pb][:, :], lhsT=r_(BW[:, bs, 0:128]), rhs=r_(HT[:, c, blk]),
                                                                           start=True, stop=True),
                          reads=[("bw", bs), ("HT", c)], writes=[("ps", pb)])
                self.P.op("pe", lambda e, pb=pb, bs=bs, c=c, blk=blk: e.matmul(PS[pb + 1][:, :], lhsT=r_(BW[:, bs, 128:256]), rhs=r_(HT[:, c, blk]),
                                                                           start=True, stop=True),
                          reads=[("bw", bs), ("HT", c)], writes=[("ps", pb + 1)])
                self.ew("dve", lambda e, pb=pb, ct=ct: e.tensor_scalar(out=BUR, in0=PS[pb][:, :], scalar1=FRE[:, ct:ct + 1], scalar2=None, op0=ALU.mult),
                        reads=[("ps", pb)] + kp, writes=[("bur",)])
                self.ew("dve", lambda e, pb=pb, ct=ct: e.scalar_tensor_tensor(out=BUR, in0=PS[pb + 1][:, :], scalar=NFIM[:, ct:ct + 1], in1=BUR,
                                                                          op0=ALU.mult, op1=ALU.add),
                        reads=[("ps", pb + 1), ("bur",)] + kp, writes=[("bur",)])
                self.ew("dve", lambda e, pb=pb, ct=ct: e.tensor_scalar(out=BUI, in0=PS[pb + 1][:, :], scalar1=FRE[:, ct:ct + 1], scalar2=None, op0=ALU.mult),
                        reads=[("ps", pb + 1)] + kp, writes=[("bui",)])
                self.ew("dve", lambda e, pb=pb, ct=ct: e.scalar_tensor_tensor(out=BUI, in0=PS[pb][:, :], scalar=FIM[:, ct:ct + 1], in1=BUI,
                                                                          op0=ALU.mult, op1=ALU.add),
                        reads=[("ps", pb), ("bui",)] + kp, writes=[("bui",)])
                self.ew("dve", lambda e, ct=ct, blk=blk: e.tensor_scalar(out=T1, in0=TG[:, blk], scalar1=U[:, ct:ct + 1], scalar2=None,
                                                                       op0=ALU.mult), reads=[("rstd",)] + kp, writes=[("t1",)])
                self.ew("pool", lambda e: e.tensor_scalar(out=T2, in0=T1, scalar1=RC, scalar2=None, op0=ALU.add),
                        reads=[("t1",)], writes=[("t2",)])
                self.ew("dve", lambda e: e.scalar_tensor_tensor(out=SINN, in0=T2, scalar=RC, in1=T1, op0=ALU.subtract, op1=ALU.subtract),
                        reads=[("t1",), ("t2",)], writes=[("sinn",)])
                self.ew("pool", lambda e: e.tensor_scalar(out=T1, in0=T1, scalar1=0.25, scalar2=None, op0=ALU.add),
                        reads=[("t1",), ("sinn",)], writes=[("t1",)])
                self.ew("pool", lambda e: e.tensor_scalar(out=T2, in0=T1, scalar1=RC, scalar2=None, op0=ALU.add),
                        reads=[("t1",), ("sinn",)], writes=[("t2",)])
                self.ew("dve", lambda e: e.scalar_tensor_tensor(out=COSN, in0=T2, scalar=RC, in1=T1, op0=ALU.subtract, op1=ALU.subtract),
                        reads=[("t1",), ("t2",)], writes=[("cosn",)])
                self.ew("act", lambda e: e.activation(out=SINN, in_=SINN, func=AF.Sin, scale=TWO_PI),
                        reads=[("sinn",)], writes=[("sinn",)])
                self.ew("act", lambda e: e.activation(out=COSN, in_=COSN, func=AF.Sin, scale=TWO_PI),
                        reads=[("cosn",)], writes=[("cosn",)])
                pl = lambda fn, rd, wr: self.ew("pool", fn, reads=rd, writes=wr)
                pl(lambda e: e.tensor_tensor(out=T1, in0=BUR, in1=COSN, op=ALU.mult), [("bur",), ("cosn",)], [("t1",)])
                pl(lambda e: e.tensor_tensor(out=T2, in0=BUI, in1=SINN, op=ALU.mult), [("bui",), ("sinn",)], [("t2",)])
                pl(lambda e: e.tensor_tensor(out=T1, in0=T1, in1=T2, op=ALU.add), [("t1",), ("t2",)], [("t1",)])
                pl(lambda e: e.tensor_tensor(out=T2, in0=BUI, in1=COSN, op=ALU.mult), [("bui",), ("cosn",), ("t1",)], [("t2",)])
                pl(lambda e: e.tensor_tensor(out=BUR, in0=BUR, in1=SINN, op=ALU.mult), [("bur",), ("sinn",)], [("bur",)])
                pl(lambda e: e.tensor_tensor(out=T2, in0=T2, in1=BUR, op=ALU.subtract), [("t2",), ("bur",)], [("t2",)])
                for comp, TT, tk in ((0, T1, ("t1",)), (1, T2, ("t2",))):
                    if tb == 0:
                        init = INIT[:, ct, comp:comp + 1] if second else 0.0
                        rd = [tk, ("init",)] + kp
                    else:
                        init = STATE[:, ct, comp:comp + 1]
                        rd = [tk, ("state", ct)] + kp
                    self.ew("dve", lambda e, TT=TT, init=init, ct=ct: e.tensor_tensor_scan(
                        out=TT, data0=MAG[:, ct:ct + 1].to_broadcast([128, 512]), data1=TT, initial=init, op0=ALU.mult, op1=ALU.add),
                        reads=rd, writes=[tk])
                for comp, TT, tk in ((0, T1, ("t1",)), (1, T2, ("t2",))):
                    self.ew("act", lambda e, TT=TT, ct=ct, comp=comp: e.activation(out=STATE[:, ct, comp:comp + 1], in_=TT[:, 511:512], func=AF.Copy),
                            reads=[tk], writes=[("state", ct)])
                if not second:
                    continue
                xs = x_ring.next()
                pl(lambda e: e.tensor_tensor(out=BUR, in0=T1, in1=COSN, op=ALU.mult), [("t1",), ("cosn",)], [("bur",)])
                pl(lambda e: e.tensor_tensor(out=BUI, in0=T2, in1=SINN, op=ALU.mult), [("t2",), ("sinn",)], [("bui",)])
                pl(lambda e, xs=xs: e.tensor_tensor(out=r_(XR[:, xs, :]), in0=BUR, in1=BUI, op=ALU.subtract), [("bur",), ("bui",)], [("xr", xs)])
                self.ew("dve", lambda e: e.tensor_tensor(out=BUR, in0=T1, in1=SINN, op=ALU.mult), reads=[("t1",), ("sinn",), ("xr", xs)], writes=[("bur",)])
                self.ew("dve", lambda e: e.tensor_tensor(out=BUI, in0=T2, in1=COSN, op=ALU.mult), reads=[("t2",), ("cosn",), ("xr", xs)], writes=[("bui",)])
                self.ew("dve", lambda e, xs=xs: e.scalar_tensor_tensor(out=r_(XIN[:, xs, :]), in0=BUR, scalar=-1.0, in1=BUI, op0=ALU.mult, op1=ALU.subtract),
                        reads=[("bur",), ("bui",)], writes=[("xin", xs)])
                yb = 4 + tb
                q = ct % 4
                self.P.op("pe", lambda e, yb=yb, cs=cs, xs=xs, q=q: e.matmul(PS[yb][:, :], lhsT=r_(CWt[:, cs, 0:128]), rhs=r_(XR[:, xs, :]),
                                                                         start=(q == 0), stop=False),
                          reads=[("cwt", cs), ("xr", xs)], writes=[("ps", yb)])
                self.P.op("pe", lambda e, yb=yb, cs=cs, xs=xs, q=q: e.matmul(PS[yb][:, :], lhsT=r_(CWt[:, cs, 128:256]), rhs=r_(XIN[:, xs, :]),
                                                                         start=False, stop=(q == 3)),
                          reads=[("cwt", cs), ("xin", xs)], writes=[("ps", yb)])
                if q == 3:
                    self.ew("dve", lambda e, yb=yb, c=c, blk=blk: e.scalar_tensor_tensor(out=PS[yb][:, :], in0=HT[:, c, blk], scalar=SCST[:, c:c + 1],
                                                                                     in1=PS[yb][:, :], op0=ALU.mult, op1=ALU.add),
                            reads=[("HT", c), ("acst",), ("ps", yb)], writes=[("ps", yb)])
                    self.ew("act", lambda e, yb=yb: e.activation(out=PS[6][:, :], in_=PS[yb][:, :], func=AF.Square),
                            reads=[("ps", yb)], writes=[("ps", 6)])
                    self.ew("dve", lambda e: e.tensor_scalar(out=PS[6][:, :], in0=PS[6][:, :], scalar1=0.044715, scalar2=1.0, op0=ALU.mult, op1=ALU.add),
                            reads=[("ps", 6)], writes=[("ps", 6)])
                    self.ew("dve", lambda e, yb=yb, tb=tb: e.tensor_copy(out=r_(YO[:, tb, :]), in_=PS[yb][:, :]),
                            reads=[("ps", yb)], writes=[("yo", tb)])
                    self.ew("dve", lambda e, tb=tb: e.tensor_tensor(out=PS[6][:, :], in0=PS[6][:, :], in1=YO[:, tb, :], op=ALU.mult),
                            reads=[("ps", 6), ("yo", tb)], writes=[("ps", 6)])
                    self.ew("act", lambda e: e.activation(out=PS[6][:, :], in_=PS[6][:, :], func=AF.Sigmoid, scale=1.5957691216057308),
                            reads=[("ps", 6)], writes=[("ps", 6)])
                    self.ew("dve", lambda e, tb=tb: e.tensor_tensor(out=r_(YO[:, tb, :]), in0=YO[:, tb, :], in1=PS[6][:, :], op=ALU.mult),
                            reads=[("ps", 6), ("yo", tb)], writes=[("yo", tb)])
                    self.dma(out=self.yd[c][:, blk], in_=YO[:, tb, :], reads=[("yo", tb)], writes=[("yd", c, tb)], cls=f"yo{tb}")

        for ct in range(64):
            tile_pass(ct, False)
        skeys = [("state", ct) for ct in range(64)]
        self.dma(out=self.c_ins, in_=STATE[:, :, :].rearrange("p a b -> p (a b)"), reads=skeys, writes=[("cin", "s")], cls="tst")
        self.allgather(self.c_ins, self.c_outs, reads=[("cin", "s")], writes=[("cout", "s")])
        self.dma(out=INIT[:, :, :].rearrange("p a b -> p (a b)"), in_=self.c_outs[0:128, :], reads=[("cout", "s")], writes=[("init",)], cls="tld")
        GATE = self.GATE
        self.ew("dve", lambda e: e.tensor_scalar(out=INIT[:, :, :], in0=INIT[:, :, :], scalar1=GATE[:, 0:1], scalar2=None, op0=ALU.mult),
                reads=[("init",), ("gate",)], writes=[("init",)])
        for ct in range(64):
            tile_pass(ct, True)
        P.barrier()
        for c in range(NCH):
            self.dma(out=r_(HT[:, c, 0:T]), in_=r_(self.yd[c]), reads=[("yd", c, 0), ("yd", c, 1)], writes=[("HT", c)], cls=f"xin{c}")
        htkeys = [("HT", k) for k in range(NCH)]
        SG = self.STG
        for m in range(NCH):
            banks = {}
            for half in range(2):
                chunk = m + NCH * half
                wsl = []
                for kh in range(2):
                    s_ = self.wu_ring.next()
                    self.dma(out=r_(WU[:, s_, :]), in_=r_(self.d_wglu[chunk][:, kh * 1024:(kh + 1) * 1024]),
                             reads=[], writes=[("wu", s_)], cls=f"wu{s_}")
                    wsl.append(s_)
                for th in range(2):
                    bank = half * 2 + th
                    banks[(half, th)] = bank
                    pairs = [(r_(WU[:, wsl[k // 8], (k % 8) * 128:(k % 8 + 1) * 128]), r_(HT[:, k, th * 512:(th + 1) * 512])) for k in range(NCH)]
                    self.mm_group(PS[bank][:, :], pairs, reads=[("wu", wsl[0]), ("wu", wsl[1])] + htkeys, writes=[("ps", bank)])
            for th in range(2):
                bv, bg = banks[(0, th)], banks[(1, th)]
                self.ew("act", lambda e, bg=bg, th=th, m=m: e.activation(out=SG[:, th, 0:512], in_=PS[bg][:, :], func=AF.Sigmoid,
                                                                       bias=SCST[:, 32 + m:33 + m]),
                        reads=[("ps", bg), ("acst",)], writes=[("sg", th)])
                self.ew("dve", lambda e, bv=bv, th=th, m=m: e.scalar_tensor_tensor(out=SG[:, th, 0:512], in0=PS[bv][:, :], scalar=SCST[:, 16 + m:17 + m],
                                                                               in1=SG[:, th, 0:512], op0=ALU.add, op1=ALU.mult),
                        reads=[("ps", bv), ("sg", th), ("acst",)], writes=[("sg", th)])
                self.ew("dve", lambda e, th=th, m=m: e.tensor_tensor(out=X[:, m, HX + th * 512:HX + (th + 1) * 512],
                                                                   in0=X[:, m, HX + th * 512:HX + (th + 1) * 512], in1=SG[:, th, 0:512], op=ALU.add),
                        reads=[("sg", th), ("X", m)], writes=[("X", m)])

    def ffn(self, l):
        X, HT, PS = self.X, self.HT, self.PS
        WU, WD, STG, CG, G, CW = self.WU, self.WD, self.STG, self.CG, self.G, self.CW
        self.dma(out=CW[:, :], in_=self.d_cw[l], reads=[], writes=[("cw",)], cls="c_cw")
        self.rmsnorm(DEPTH + l, HX - 2, [(0, 342), (342, 342), (684, 342)])
        GF = 2
        ngroups = NFP // GF
        htkeys = [("HT", k) for k in range(NCH)]

        def down(g, gs):
            for ch in range(2):
                slots = []
                for jj in range(GF):
                    j = g * GF + jj
                    s = self.wd_ring.next()
                    self.dma(out=r_(WD[:, s, :]), in_=r_(self.d_wd[l][j][:, ch * 1024:(ch + 1) * 1024]),
                             reads=[], writes=[("wd", s)], cls=f"wd{s}")
                    slots.append(s)
                for mm in range(8):
                    m = ch * 8 + mm
                    for th in range(2):
                        bank = 4 + self.dn_ring.next()
                        pairs = [(r_(WD[:, slots[jj], mm * 128:(mm + 1) * 128]),
                                  r_(G[:, gs * GF + jj, th * 512:(th + 1) * 512])) for jj in range(GF)]
                        self.mm_group(PS[bank][:, :], pairs,
                                      reads=[("wd", s) for s in slots] + [("g", gs * GF + jj) for jj in range(GF)],
                                      writes=[("ps", bank)])
                        self.ew("dve", lambda e, bank=bank, m=m, th=th: e.tensor_tensor(
                            out=X[:, m, HX + th * 512:HX + (th + 1) * 512],
                            in0=X[:, m, HX + th * 512:HX + (th + 1) * 512], in1=PS[bank][:, :], op=ALU.add),
                            reads=[("ps", bank), ("X", m)], writes=[("X", m)])

        for g in range(ngroups):
            gs = g % 2
            for jj in range(GF):
                j = g * GF + jj
                gi = gs * GF + jj
                for half in range(2):
                    chunk = j + NFP * half
                    wsl = []
                    for kh in range(2):
                        s = self.wu_ring.next()
                        self.dma(out=r_(WU[:, s, :]), in_=r_(self.d_wu[l][chunk][:, kh * 1024:(kh + 1) * 1024]),
                                 reads=[], writes=[("wu", s)], cls=f"wu{s}")
                        wsl.append(s)
                    for blk in range(3):
                        bank = self.up_ring.next()
                        pairs = [(r_(WU[:, wsl[k // 8], (k % 8) * 128:(k % 8 + 1) * 128]),
                                  r_(HT[:, k, blk * 342:(blk + 1) * 342])) for k in range(NCH)]
                        self.mm_group(PS[bank][:, 0:342], pairs,
                                      reads=[("wu", wsl[0]), ("wu", wsl[1])] + htkeys, writes=[("ps", bank)])
                        self.ew("act", lambda e, bank=bank, half=half, blk=blk: e.activation(
                            out=STG[:, half, blk * 342:(blk + 1) * 342], in_=PS[bank][:, 0:342], func=AF.Copy),
                            reads=[("ps", bank)], writes=[("stg", half, blk)])
                    eng = "dve" if half == 0 else "pool"
                    dst = r_(G[:, gi, :]) if half == 0 else CG[:, :]
                    dsrc = G[:, gi, :] if half == 0 else CG[:, :]
                    dkey = ("g", gi) if half == 0 else ("cg",)
                    cwb = chunk * 4
                    self.ew(eng, lambda e, dst=dst, half=half, cwb=cwb: e.tensor_scalar(
                        out=dst, in0=STG[:, half, 2:2 + T], scalar1=CW[:, cwb + 2:cwb + 3], scalar2=CW[:, cwb + 3:cwb + 4],
                        op0=ALU.mult, op1=ALU.add), reads=[("stg", half, 0), ("stg", half, 1), ("stg", half, 2), ("cw",)], writes=[dkey])
                    for tap in (1, 0):
                        if eng == "dve":
                            self.ew(eng, lambda e, dst=dst, dsrc=dsrc, half=half, cwb=cwb, tap=tap: e.scalar_tensor_tensor(
                                out=dst, in0=STG[:, half, tap:tap + T], scalar=CW[:, cwb + tap:cwb + tap + 1], in1=dsrc,
                                op0=ALU.mult, op1=ALU.add), reads=[("stg", half, 0), ("stg", half, 1), ("stg", half, 2), ("cw",), dkey], writes=[dkey])
                        else:
                            PT = self.RSTD[:, 0:T]
                            self.ew(eng, lambda e, half=half, cwb=cwb, tap=tap, PT=PT: e.tensor_scalar(
                                out=PT, in0=STG[:, half, tap:tap + T], scalar1=CW[:, cwb + tap:cwb + tap + 1], scalar2=None,
                                op0=ALU.mult), reads=[("stg", half, 0), ("stg", half, 1), ("stg", half, 2), ("cw",)], writes=[("rstd",)])
                            self.ew(eng, lambda e, dst=dst, dsrc=dsrc, PT=PT: e.tensor_tensor(
                                out=dst, in0=dsrc, in1=PT, op=ALU.add), reads=[("rstd",), dkey], writes=[dkey])
                self.ew("act", lambda e: e.activation(out=CG[:, :], in_=CG[:, :], func=AF.Silu),
                        reads=[("cg",)], writes=[("cg",)])
                self.ew("dve", lambda e, gi=gi: e.tensor_tensor(out=r_(G[:, gi, :]), in0=G[:, gi, :], in1=CG[:, :], op=ALU.mult),
                        reads=[("cg",), ("g", gi)], writes=[("g", gi)])
            if g >= 1:
                down(g - 1, (g - 1) % 2)
        down(ngroups - 1, (ngroups - 1) % 2)

    def build(self, ncores):
        nc = self.nc
        self.ncores = ncores
        self.declare_dram()
        P = self.P
        from contextlib import ExitStack
        with ExitStack() as es:
            sb = lambda name, shape: es.enter_context(nc.sbuf_tensor(name, shape, F32))
            self.X = sb("X", [128, NCH, XW])
            self.HT = sb("HT", [128, NCH, XW])
            self.RSTD = sb("RSTD", [128, XW])
            self.SQ = sb("SQ", [128, 2, 512])
            self.ACST = sb("ACST", [128, 128])
            self.SPRM = sb("SPRM", [128, 13, 64])
            self.NEGPI = sb("NEGPI", [128, 1])
            self.STATE = sb("STATE", [128, 64, 2])
            self.INIT = sb("INIT", [128, 64, 2])
            self.ONES = sb("ONES", [128, 128])
            self.ONESF = sb("ONESF", [128, 128])
            self.GAIN = sb("GAIN", [128, 2 * DEPTH * NCH])
            self.GATE = sb("GATE", [128, 1])
            self.INVC = sb("INVC", [128, 64])
            self.TB = sb("TB", [128, NCH, 16])
            self.CW = sb("CW", [128, 2 * NFP * 4])
            self.PBS = sb("PBS", [128, 48])
            self.T16 = sb("T16", [128, 4, 16])
            self.WU = sb("WU", [128, 4, 1024])
            self.WD = sb("WD", [128, 4, 1024])
            self.STG = sb("STG", [128, 2, XW])
            self.CG = sb("CG", [128, T])
            self.G = sb("G", [128, 4, T])
            self.PW = self.WU[:, 0:4, :].rearrange("p (a b) n -> p a (b n)", a=2)
            self.PL = self.G
            self.TA = self.STG
            self.PS = [es.enter_context(nc.psum_tensor(f"ps{i}", [128, 512], F32)) for i in range(8)]
            self.TBF = None
            self.TMP = self.CG[:, :].rearrange("p (a n) -> p a n", a=2)
            self.sq_ring = Ring(2)
            self.wu_ring = Ring(4)
            self.wd_ring = Ring(4)
            self.up_ring = Ring(4)
            self.dn_ring = Ring(3)
            self.pw_ring = Ring(2)
            self.tmp_ring = Ring(2)

            xkeys = [("X", c) for c in range(NCH)]
            for c in range(NCH):
                self.dma(out=self.X[:, c, :], in_=self.d_x[:, c, :], reads=[], writes=[("X", c)], cls=f"xin{c}")
            self.dma(out=self.GAIN[:, :], in_=self.d_gain, reads=[], writes=[("gain",)], cls="c_gain")
            self.dma(out=self.GATE[:, :], in_=self.d_gate, reads=[], writes=[("gate",)], cls="c_gate")
            self.dma(out=self.INVC[:, :], in_=self.d_invc, reads=[], writes=[("invc",)], cls="c_invc")
            self.ew("pool", lambda e: e.memset(self.ONESF[:, :], 1.0), reads=[], writes=[("onesf",)])
            self.ew("dve", lambda e: e.tensor_copy(out=r_(self.ONES[:, :]), in_=self.ONESF[:, :]),
                    reads=[("onesf",)], writes=[("ones",)])
            for l in self.layers:
                kind = l % 3
                if kind == 0:
                    if l != 0:
                        if l != self.first:
                            self.exchange_tail(16)
                    P.barrier()
                    self.pool_mixer(l)
                elif kind == 1:
                    P.barrier()
                    self.attention(l)
                else:
                    P.barrier()
                    self.s5(l)
                P.barrier()
                self.exchange_tail(2)
                self.ffn(l)
                P.barrier()
            for c in range(NCH):
                self.dma(out=self.d_out[:, c, :], in_=self.X[:, c, HX:XW], reads=[("X", c)], writes=[("out", c)], cls=f"xin{c}")

            clss = P.classes()
            names = list(Prog.ENGINES[:4]) + clss
            sems = {}
            for n in names:
                sems[n] = es.enter_context(nc.semaphore("s_" + n))
            block = es.enter_context(nc.Block())
            P.emit(block, sems)
        return nc


def _feat_major(v):
    n = v.shape[-1] // 128
    return np.ascontiguousarray(v.reshape(n, 128).T)


def prep_shared(inp, layers):
    sh = {}
    gains = np.concatenate([inp["norm_mix_g"], inp["norm_ffn_g"]], axis=0)
    sh["gains"] = np.ascontiguousarray(gains.reshape(2 * DEPTH, NCH, 128).transpose(2, 0, 1).reshape(128, -1))
    for l in layers:
        wu = inp["ffn_w_up"][l]
        sh[f"wu{l}"] = np.ascontiguousarray(wu.reshape(NCH, 128, 2 * NFP, 128).transpose(2, 1, 0, 3).reshape(2 * NFP, 128, NCH * 128))
        sh[f"wd{l}"] = np.ascontiguousarray(inp["ffn_w_down"][l].reshape(NFP, 128, D))
        cw = np.concatenate([inp["ffn_conv_w"][l], inp["ffn_conv_b"][l][None]], axis=0)
        sh[f"cw{l}"] = np.ascontiguousarray(cw.reshape(4, 2 * NFP, 128).transpose(2, 1, 0).reshape(128, -1))
        if l % 3 == 2:
            j = l // 3
            ch = lambda v: np.ascontiguousarray(v.reshape(64, 128).T)
            ls = np.repeat(inp["ssm_log_step"][j][:, None], 64, axis=1)
            sh[f"sprm{l}"] = np.ascontiguousarray(np.concatenate([ch(inp["ssm_lam_re"][j]), ch(inp["ssm_lam_im"][j]), ch(ls)], axis=1))
            sbt = np.zeros((64, 128, 256), np.float32)
            sct = np.zeros((64, 128, 256), np.float32)
            for ct in range(64):
                q = ct % 4
                for gg in range(2):
                    g = 2 * ct + gg
                    rows = slice(q * 32 + gg * 16, q * 32 + gg * 16 + 16)
                    cols = slice(gg * 64, gg * 64 + 64)
                    sbt[ct, rows, 0:128][:, cols] = inp["ssm_b_re"][j][g].T
                    sbt[ct, rows, 128:256][:, cols] = inp["ssm_b_im"][j][g].T
                    sct[ct, cols, 0:128][:, rows] = inp["ssm_c_re"][j][g].T
                    sct[ct, cols, 128:256][:, rows] = inp["ssm_c_im"][j][g].T
            sh[f"sbt{l}"] = sbt
            sh[f"sct{l}"] = sct
            sh[f"scst{l}"] = np.ascontiguousarray(np.concatenate([_feat_major(inp["ssm_d"][j]), _feat_major(inp["ssm_b_glu"][j])], axis=1))
            sh[f"wglu{l}"] = np.ascontiguousarray(inp["ssm_w_glu"][j].reshape(NCH, 128, 32, 128).transpose(2, 1, 0, 3).reshape(32, 128, NCH * 128))
        if l % 3 == 1:
            j = l // 3
            wqkv = inp["sb_w_qkv"][j]
            sh[f"wqk{l}"] = np.ascontiguousarray(wqkv[:, 0:4096].reshape(NCH, 128, 32, 128).transpose(2, 1, 0, 3).reshape(32, 128, NCH * 128))
            sh[f"wv{l}"] = np.ascontiguousarray(wqkv[:, 4096:6144])
            sh[f"wo{l}"] = np.ascontiguousarray(inp["sb_w_o"][j].reshape(NCH, 128, NCH, 128).transpose(2, 1, 0, 3).reshape(NCH, 128, NCH * 128))
            acst = np.zeros((128, 18), np.float32)
            acst[:, 0:16] = np.arange(128, dtype=np.float32)[:, None] + 128.0 * np.arange(16, dtype=np.float32)[None, :]
            acst[:, 16] = inp["sb_q_gain"][j]
            acst[:, 17] = inp["sb_k_gain"][j]
            sh[f"acst{l}"] = acst
            sh["tri"] = np.ascontiguousarray(np.tril(np.ones((128, 128), np.float32)))
        if l % 3 == 0:
            j = l // 3
            pw = inp["pool_w"][j]
            sh[f"pw{l}"] = np.ascontiguousarray(pw.reshape(4, 4, 128, 512).transpose(0, 2, 1, 3).reshape(4, 128, 2048))
            sh[f"pbs{l}"] = np.ascontiguousarray(np.concatenate([_feat_major(inp["pool_b"][j]), _feat_major(inp["pool_scale"][j])], axis=1))
    return sh


def prep_core(xT_full_b, h):
    m = {}
    xs = np.zeros((D, XW), np.float32)
    lo = h * T
    xs[:, HX:] = xT_full_b[:, lo:lo + T]
    if h == 1:
        xs[:, :HX] = xT_full_b[:, lo - HX:lo]
    m["xT"] = np.ascontiguousarray(xs.reshape(NCH, 128, XW).transpose(1, 0, 2))
    m["gate"] = np.full((128, 1), float(h), np.float32)
    invc = np.zeros((4, 16), np.float32)
    for g in range(4):
        w = 2 << g
        for i in range(16):
            t = lo + i
            invc[g, i] = 1.0 / min(t + 1, w)
    m["invc"] = np.ascontiguousarray(np.broadcast_to(invc.reshape(1, 64), (128, 64)))
    m["tpos"] = np.ascontiguousarray(np.broadcast_to((lo + np.arange(T, dtype=np.float32))[None, :], (128, T)))
    return m


_NC_CACHE = {}


def run_layers(xT_all, inp, layers, nb=BATCH):
    ncores = 2 * nb
    key = (tuple(layers), ncores)
    if key not in _NC_CACHE:
        nc = bass.Bass("TRN2", target_bir_lowering=False)
        nc.dge_precook = False
        bld = Builder(nc, list(layers), layers[0], layers[-1])
        bld.build(ncores)
        _NC_CACHE[key] = (nc, list(bld.in_names))
    nc, in_names = _NC_CACHE[key]
    sh = prep_shared(inp, layers)
    in_maps = []
    for b in range(nb):
        for h in range(2):
            m = dict(sh)
            m.update(prep_core(xT_all[b], h))
            in_maps.append({k: m[k] for k in in_names})
    import time as _t, sys as _s
    _t0 = _t.time()
    res = run_bass_kernel_spmd(nc, in_maps, core_ids=list(range(ncores)))
    print("[kernel] launch layers=%s took %.1fs" % (list(layers), _t.time() - _t0), file=_s.stderr)
    out = np.zeros_like(xT_all)
    for b in range(nb):
        for h in range(2):
            o = res.results[2 * b + h]["outT"]
            out[b][:, h * T:(h + 1) * T] = o.transpose(1, 0, 2).reshape(D, T)
    return out


LAUNCH_GROUPS = [[0, 1, 2, 3]]


def kernel(**inputs):
    inp = {k: np.asarray(v) for k, v in inputs.items()}
    x = inp["x"]
    xT = np.ascontiguousarray(x.transpose(0, 2, 1))
    for grp in LAUNCH_GROUPS:
        xT = run_layers(xT, inp, grp)
    return np.ascontiguousarray(xT.transpose(0, 2, 1)).astype(np.float32)
```

```python
import numpy as np
import concourse.bass as bass
import concourse.mybir as mybir
from concourse.bass_utils import run_bass_kernel_spmd

F32 = mybir.dt.float32
F32R = mybir.dt.float32r
AF = mybir.ActivationFunctionType
ALU = mybir.AluOpType

D = 2048
NCH = 16
SEQ = 2048
BATCH = 4
T = 1024
HX = 16
XW = HX + T
DFF = 5632
NFP = 44
EPS = 1e-6
DEPTH = 4
SAME_ENGINE_SYNC = True


def r_(ap):
    return ap.bitcast(F32R)


class Op:
    __slots__ = ("eng", "fn", "deps", "cls", "inc", "count", "needed")

    def __init__(self, eng, fn, deps, cls, inc):
        self.eng = eng
        self.fn = fn
        self.deps = deps
        self.cls = cls
        self.inc = inc
        self.count = 0
        self.needed = False


class Ring:
    def __init__(self, n):
        self.n = n
        self.i = -1

    def next(self):
        self.i = (self.i + 1) % self.n
        return self.i


class Prog:
    ENGINES = ("pe", "act", "dve", "pool", "sp")

    def __init__(self, nc):
        self.nc = nc
        self.ops = []
        self.last_w = {}
        self.readers = {}

    def _deps(self, reads, writes, me):
        deps = set()
        for k in reads:
            w = self.last_w.get(k)
            if w is not None:
                deps.add(w)
        for k in writes:
            w = self.last_w.get(k)
            if w is not None:
                deps.add(w)
            deps.update(self.readers.get(k, ()))
        for k in reads:
            self.readers.setdefault(k, []).append(me)
        for k in writes:
            self.last_w[k] = me
            self.readers[k] = []
        deps.discard(me)
        return deps

    def op(self, eng, fn, reads=(), writes=(), cls=None, inc=16):
        me = len(self.ops)
        deps = self._deps(reads, writes, me)
        self.ops.append(Op(eng, fn, deps, cls, inc))
        return me

    def barrier(self):
        n = len(self.ops)
        last = {}
        for i in range(n - 1, -1, -1):
            o = self.ops[i]
            key = o.cls if o.cls is not None else o.eng
            if key not in last:
                last[key] = i
        deps = set(last.values())
        for e in self.ENGINES:
            self.ops.append(Op(e, None, set(deps), None, 0))

    def emit(self, block, sems):
        ops = self.ops
        self.barrier()
        for o in ops:
            for d in o.deps:
                ops[d].needed = True
        cnt = {}
        for o in ops:
            if o.cls is not None:
                cnt[o.cls] = cnt.get(o.cls, 0) + o.inc
                o.count = cnt[o.cls]
            elif o.needed and o.fn is not None:
                cnt[o.eng] = cnt.get(o.eng, 0) + 1
                o.count = cnt[o.eng]
            elif o.fn is None:
                o.count = -1
        def semkey(o):
            return o.cls if o.cls is not None else o.eng

        streams = {e: [] for e in self.ENGINES}
        for i, o in enumerate(ops):
            streams[o.eng].append(i)

        def make(engname):
            idxs = streams[engname]

            def body(e):
                seen = {}
                for i in idxs:
                    o = ops[i]
                    waits = {}
                    for d in o.deps:
                        od = ops[d]
                        if od.fn is None:
                            continue
                        k = semkey(od)
                        if od.cls is None and od.eng == engname and not SAME_ENGINE_SYNC:
                            continue
                        if od.cls is None and od.eng == engname and engname == "pe":
                            continue
                        if od.count > waits.get(k, 0):
                            waits[k] = od.count
                    for k, v in waits.items():
                        if v > seen.get(k, 0):
                            e.wait_ge(sems[k], v)
                            seen[k] = v
                    if o.fn is None:
                        continue
                    ins = o.fn(e)
                    if o.cls is not None:
                        ins.then_inc(sems[o.cls], o.inc)
                    elif o.needed:
                        ins.then_inc(sems[o.eng], 1)

            return body

        block.tensor(make("pe"))
        block.scalar(make("act"))
        block.vector(make("dve"))
        block.gpsimd(make("pool"))
        block.sync(make("sp"))

    def classes(self):
        s = set()
        for o in self.ops:
            if o.cls is not None:
                s.add(o.cls)
        return sorted(s)


class Builder:
    def __init__(self, nc, layers, first, last):
        self.nc = nc
        self.layers = layers
        self.first = first
        self.last = last
        self.P = Prog(nc)

    def declare_dram(self):
        nc = self.nc
        self.in_names = []

        def dt(name, shape, kind="ExternalInput"):
            if kind == "ExternalInput":
                self.in_names.append(name)
            return nc.dram_tensor(name, shape, F32, kind=kind).ap()
        self.d_x = dt("xT", [128, NCH, XW])
        self.d_out = dt("outT", [128, NCH, T], "ExternalOutput")
        self.d_gain = dt("gains", [128, 2 * DEPTH * NCH])
        self.d_gate = dt("gate", [128, 1])
        self.d_invc = dt("invc", [128, 4 * 16])
        self.d_wu = {}
        self.d_wd = {}
        self.d_cw = {}
        self.d_pw = {}
        self.d_pbs = {}
        for l in self.layers:
            self.d_wu[l] = dt(f"wu{l}", [2 * NFP, 128, NCH * 128])
            self.d_wd[l] = dt(f"wd{l}", [NFP, 128, D])
            self.d_cw[l] = dt(f"cw{l}", [128, 2 * NFP * 4])
            if l % 3 == 0:
                self.d_pw[l] = dt(f"pw{l}", [4, 128, 4 * 512])
                self.d_pbs[l] = dt(f"pbs{l}", [128, 2 * NCH])
            if l % 3 == 2:
                self.d_sprm = dt(f"sprm{l}", [128, 3 * 64])
                self.d_sbt = dt(f"sbt{l}", [64, 128, 256])
                self.d_sct = dt(f"sct{l}", [64, 128, 256])
                self.d_scst = dt(f"scst{l}", [128, 16 + 32])
                self.d_wglu = dt(f"wglu{l}", [32, 128, NCH * 128])
                if not hasattr(self, "d_tpos"):
                    self.d_tpos = dt("tpos", [128, T])
                self.yd = nc.dram_tensor("yd", [NCH, 128, T], F32).ap()
                self.c_ins = nc.dram_tensor("c_ins", [128, 128], F32).ap()
                self.c_outs = nc.dram_tensor("c_outs", [256, 128], F32).ap()
            if l % 3 == 1:
                self.d_wqk = dt(f"wqk{l}", [32, 128, NCH * 128])
                self.d_wv = dt(f"wv{l}", [D, D])
                self.d_wo = dt(f"wo{l}", [NCH, 128, NCH * 128])
                self.d_tri = dt("tri", [128, 128])
                self.d_acst = dt(f"acst{l}", [128, 18])
                if not hasattr(self, "d_tpos"):
                    self.d_tpos = dt("tpos", [128, T])
                self.qT_d = nc.dram_tensor("qT_d", [NCH, 128, T], F32).ap()
                self.kin = nc.dram_tensor("kin", [D, T], F32).ap()
                self.kall = nc.dram_tensor("kall", [4, 1024, T], F32).ap()
                self.vin = nc.dram_tensor("vin", [T, D], F32).ap()
                self.vall = nc.dram_tensor("vall", [4, 512, D], F32).ap()
        self.c_in2 = nc.dram_tensor("c_in2", [128, NCH * 2], F32).ap()
        self.c_out2 = nc.dram_tensor("c_out2", [256, NCH * 2], F32).ap()
        self.c_in16 = nc.dram_tensor("c_in16", [128, NCH * 16], F32).ap()
        self.c_out16 = nc.dram_tensor("c_out16", [256, NCH * 16], F32).ap()

    def mm_group(self, ps_ap, pairs, reads, writes):
        def fn(e, ps_ap=ps_ap, pairs=pairs):
            n = len(pairs)
            ins = None
            for i, (l, r) in enumerate(pairs):
                ins = e.matmul(ps_ap, lhsT=l, rhs=r, start=(i == 0), stop=(i == n - 1))
            return ins

        self.P.op("pe", fn, reads, writes)

    def dma(self, out, in_, reads, writes, cls, eng="sp"):
        self.P.op(eng, lambda e, out=out, in_=in_: e.dma_start(out=out, in_=in_), reads, writes, cls=cls, inc=16)

    def allgather(self, cin, cout, reads, writes):
        pairs = [[0, 1], [2, 3], [4, 5], [6, 7]][: self.ncores // 2]

        def fn(e, cin=cin, cout=cout):
            return e.collective_compute("AllGather", ALU.bypass, replica_groups=pairs, ins=[cin], outs=[cout])

        self.P.op("pool", fn, reads, writes, cls="cc", inc=1)

    def ew(self, eng, fn, reads, writes):
        self.P.op(eng, fn, reads, writes)

    def rmsnorm(self, gidx, xc0, blocks):
        X, HT, SQ, RSTD, ONES, GAIN = self.X, self.HT, self.SQ, self.RSTD, self.ONES, self.GAIN
        PS = self.PS
        bank = 7
        for (b0, w) in blocks:
            for c in range(NCH):
                s = self.sq_ring.next()
                self.ew("act", lambda e, s=s, c=c, b0=b0, w=w: e.activation(
                    out=r_(SQ[:, s, 0:w]), in_=X[:, c, xc0 + b0:xc0 + b0 + w], func=AF.Square),
                    reads=[("X", c)], writes=[("sq", s)])
                self.P.op("pe", lambda e, s=s, c=c, w=w: e.matmul(
                    PS[bank][:, 0:w], lhsT=r_(ONES[:, :]), rhs=r_(SQ[:, s, 0:w]),
                    start=(c == 0), stop=(c == NCH - 1)),
                    reads=[("sq", s), ("ones",)], writes=[("ps", bank)])
            self.ew("dve", lambda e, b0=b0, w=w: e.tensor_scalar(
                out=RSTD[:, b0:b0 + w], in0=PS[bank][:, 0:w], scalar1=1.0 / D, scalar2=float(EPS),
                op0=ALU.mult, op1=ALU.add),
                reads=[("ps", bank)], writes=[("rstd",)])
            self.ew("act", lambda e, b0=b0, w=w: e.activation(
                out=RSTD[:, b0:b0 + w], in_=RSTD[:, b0:b0 + w], func=AF.Sqrt),
                reads=[("rstd",)], writes=[("rstd",)])
            self.ew("dve", lambda e, b0=b0, w=w: e.reciprocal(
                out=RSTD[:, b0:b0 + w], in_=RSTD[:, b0:b0 + w]),
                reads=[("rstd",)], writes=[("rstd",)])
            for c in range(NCH):
                if c % 2 == 0:
                    self.ew("dve", lambda e, c=c, b0=b0, w=w: e.scalar_tensor_tensor(
                        out=r_(HT[:, c, b0:b0 + w]), in0=X[:, c, xc0 + b0:xc0 + b0 + w],
                        scalar=GAIN[:, gidx * NCH + c:gidx * NCH + c + 1], in1=RSTD[:, b0:b0 + w],
                        op0=ALU.mult, op1=ALU.mult),
                        reads=[("X", c), ("rstd",), ("gain",)], writes=[("HT", c)])
                else:
                    self.ew("pool", lambda e, c=c, b0=b0, w=w: e.tensor_scalar(
                        out=r_(HT[:, c, b0:b0 + w]), in0=X[:, c, xc0 + b0:xc0 + b0 + w],
                        scalar1=GAIN[:, gidx * NCH + c:gidx * NCH + c + 1], scalar2=self.CCOL[:, 0:1], op0=ALU.mult, op1=ALU.add),
                        reads=[("X", c), ("gain",), ("ccol",)], writes=[("HT", c)])
                    self.ew("pool", lambda e, c=c, b0=b0, w=w: e.tensor_tensor(
                        out=r_(HT[:, c, b0:b0 + w]), in0=HT[:, c, b0:b0 + w], in1=RSTD[:, b0:b0 + w], op=ALU.mult),
                        reads=[("HT", c), ("rstd",)], writes=[("HT", c)])

    def exchange_tail(self, ncols):
        X, TB = self.X, self.TB
        cin = self.c_in2 if ncols == 2 else self.c_in16
        cout = self.c_out2 if ncols == 2 else self.c_out16
        tag = f"t{ncols}"
        xkeys = [("X", c) for c in range(NCH)]
        self.dma(out=cin.rearrange("p (c n) -> p c n", n=ncols), in_=X[:, :, XW - ncols:XW],
                 reads=xkeys, writes=[("cin", tag)], cls="tst")
        self.allgather(cin, cout, reads=[("cin", tag)], writes=[("cout", tag)])
        self.dma(out=TB[:, :, 0:ncols], in_=cout[0:128, :].rearrange("p (c n) -> p c n", n=ncols),
                 reads=[("cout", tag)], writes=[("TB",)], cls="tld")
        GATE = self.GATE
        self.ew("dve", lambda e: e.tensor_scalar(
            out=X[:, :, HX - ncols:HX], in0=TB[:, :, 0:ncols], scalar1=GATE[:, 0:1], scalar2=None,
            op0=ALU.mult), reads=[("TB",), ("gate",)], writes=xkeys)

    def pool_mixer(self, l):
        P = self.P
        X, HT, PS = self.X, self.HT, self.PS
        PW, PL, TA, TBF, T16, TMP, PBS, INVC = self.PW, self.PL, self.TA, self.TBF, self.T16, self.TMP, self.PBS, self.INVC
        self.dma(out=PBS[:, 0:2 * NCH], in_=self.d_pbs[l], reads=[], writes=[("pbs",)], cls="c_pbs")
        self.ew("dve", lambda e: e.tensor_tensor(out=PBS[:, 0:NCH], in0=PBS[:, 0:NCH], in1=PBS[:, NCH:2 * NCH], op=ALU.mult),
                reads=[("pbs",)], writes=[("pbs",)])
        self.rmsnorm(l, 0, [(0, 346), (346, 346), (692, 348)])
        for g in range(4):
            w = 2 << g
            ws = self.pw_ring.next()
            self.dma(out=r_(PW[:, ws, :]), in_=r_(self.d_pw[l][g]), reads=[], writes=[("pw", ws)], cls=f"pw{ws}")
            for kk in range(4):
                c = 4 * g + kk
                eng = "dve" if kk % 2 == 0 else "pool"
                src = HT[:, c, :]
                eng = "dve"
                bufs = [TA[:, 0, :], TA[:, 1, :]]
                bkeys = [("ta", 0), ("ta", 1)]
                skey = ("HT", c)
                sh = 1
                lo = 0
                for st in range(g + 1):
                    dst = bufs[st % 2]
                    dkey = bkeys[st % 2]
                    lo2 = lo + sh
                    self.ew(eng, lambda e, dst=dst, src=src, lo2=lo2, sh=sh: e.tensor_tensor(
                        out=dst[:, lo2:XW], in0=src[:, lo2:XW], in1=src[:, lo2 - sh:XW - sh], op=ALU.add),
                        reads=[skey], writes=[dkey])
                    src, skey = dst, dkey
                    lo = lo2
                    sh *= 2
                self.ew(eng, lambda e, src=src, kk=kk, c=c, w=w: e.scalar_tensor_tensor(
                    out=r_(PL[:, kk, :]), in0=src[:, HX:XW], scalar=1.0 / w, in1=HT[:, c, HX:XW],
                    op0=ALU.mult, op1=ALU.subtract),
                    reads=[skey, ("HT", c)], writes=[("pl", kk)])
                self.ew(eng, lambda e, src=src, kk=kk, g=g: e.tensor_tensor(
                    out=T16[:, kk, :], in0=src[:, HX:HX + 16], in1=INVC[:, g * 16:(g + 1) * 16], op=ALU.mult),
                    reads=[skey, ("invc",)], writes=[("t16", kk)])
                self.ew(eng, lambda e, kk=kk, c=c: e.tensor_tensor(
                    out=r_(PL[:, kk, 0:16]), in0=T16[:, kk, :], in1=HT[:, c, HX:HX + 16], op=ALU.subtract),
                    reads=[("t16", kk), ("HT", c)], writes=[("pl", kk)])
            for mm in range(4):
                m = 4 * g + mm
                for th in range(2):
                    bank = 4 + self.dn_ring.next()
                    pairs = [(r_(PW[:, ws, kk * 512 + mm * 128:kk * 512 + (mm + 1) * 128]),
                              r_(PL[:, kk, th * 512:(th + 1) * 512])) for kk in range(4)]
                    self.mm_group(PS[bank][:, :], pairs,
                                  reads=[("pw", ws)] + [("pl", kk) for kk in range(4)], writes=[("ps", bank)])
                    ts = self.tmp_ring.next()
                    self.ew("act", lambda e, bank=bank, ts=ts, m=m: e.activation(
                        out=TMP[:, ts, :], in_=PS[bank][:, :], func=AF.Identity,
                        bias=PBS[:, m:m + 1], scale=PBS[:, NCH + m:NCH + m + 1]),
                        reads=[("ps", bank), ("pbs",)], writes=[("tmp", ts)])
                    self.ew("dve", lambda e, ts=ts, m=m, th=th: e.tensor_tensor(
                        out=X[:, m, HX + th * 512:HX + (th + 1) * 512], in0=X[:, m, HX + th * 512:HX + (th + 1) * 512],
                        in1=TMP[:, ts, :], op=ALU.add),
                        reads=[("tmp", ts), ("X", m)], writes=[("X", m)])

    def attention(self, l):
        P = self.P
        X, HT, PS, WU, SQ, RSTD, ONES, ONESF = self.X, self.HT, self.PS, self.WU, self.SQ, self.RSTD, self.ONES, self.ONESF
        TRI = self.ACST[:, 0:128]
        KPOS = self.PBS[:, 0:16]
        QKG = self.PBS[:, 16:18]
        htkeys = [("HT", k) for k in range(NCH)]
        self.dma(out=r_(TRI), in_=r_(self.d_tri), reads=[], writes=[("tri",)], cls="c_tri")
        self.dma(out=self.PBS[:, 0:18], in_=self.d_acst, reads=[], writes=[("acst",)], cls="c_pbs")
        self.ew("dve", lambda e: e.tensor_scalar(out=QKG[:, 0:1], in0=QKG[:, 0:1], scalar1=float(128 ** -0.5), scalar2=None,
                                                  op0=ALU.mult), reads=[("acst",)], writes=[("acst",)])
        self.rmsnorm(l, HX, [(0, 512), (512, 512)])
        QKS = self.STG
        VST = self.CG[:, :].rearrange("p (a n) -> p a n", a=2)
        for oc in range(32):
            qk = oc // 16
            hh = oc % 16
            wsl = []
            for kh in range(2):
                s_ = self.wu_ring.next()
                self.dma(out=r_(WU[:, s_, :]), in_=r_(self.d_wqk[oc][:, kh * 1024:(kh + 1) * 1024]),
                         reads=[], writes=[("wu", s_)], cls=f"wu{s_}")
                wsl.append(s_)
            st = oc % 2
            for th in range(2):
                bank = self.up_ring.next()
                pairs = [(r_(WU[:, wsl[k // 8], (k % 8) * 128:(k % 8 + 1) * 128]),
                          r_(HT[:, k, th * 512:(th + 1) * 512])) for k in range(NCH)]
                self.mm_group(PS[bank][:, :], pairs, reads=[("wu", wsl[0]), ("wu", wsl[1])] + htkeys, writes=[("ps", bank)])
                sq = self.sq_ring.next()
                self.ew("act", lambda e, bank=bank, sq=sq: e.activation(out=r_(SQ[:, sq, :]), in_=PS[bank][:, :], func=AF.Square),
                        reads=[("ps", bank)], writes=[("sq", sq)])
                self.P.op("pe", lambda e, sq=sq: e.matmul(PS[7][:, :], lhsT=r_(ONES[:, :]), rhs=r_(SQ[:, sq, :]), start=True, stop=True),
                          reads=[("sq", sq), ("ones",)], writes=[("ps", 7)])
                self.ew("dve", lambda e: e.tensor_scalar(out=RSTD[:, 0:512], in0=PS[7][:, :], scalar1=1.0 / 128, scalar2=float(EPS),
                                                          op0=ALU.mult, op1=ALU.add), reads=[("ps", 7)], writes=[("rstd",)])
                self.ew("act", lambda e: e.activation(out=RSTD[:, 0:512], in_=RSTD[:, 0:512], func=AF.Sqrt),
                        reads=[("rstd",)], writes=[("rstd",)])
                self.ew("dve", lambda e: e.reciprocal(out=RSTD[:, 0:512], in_=RSTD[:, 0:512]), reads=[("rstd",)], writes=[("rstd",)])
                self.ew("dve", lambda e, bank=bank, st=st, th=th, qk=qk: e.scalar_tensor_tensor(
                    out=QKS[:, st, th * 512:(th + 1) * 512], in0=PS[bank][:, :], scalar=QKG[:, qk:qk + 1], in1=RSTD[:, 0:512],
                    op0=ALU.mult, op1=ALU.mult), reads=[("ps", bank), ("rstd",), ("acst",)], writes=[("qks", st)])
            dst = self.qT_d[hh] if qk == 0 else self.kin[hh * 128:(hh + 1) * 128, :]
            self.dma(out=dst, in_=QKS[:, st, 0:T], reads=[("qks", st)], writes=[("qkd", oc)], cls=f"qks{st}")
        for tg in range(2):
            for cb in range(4):
                for k in range(NCH):
                    s_ = self.wu_ring.next()
                    self.dma(out=r_(WU[:, s_, 0:512]), in_=r_(self.d_wv[k * 128:(k + 1) * 128, cb * 512:(cb + 1) * 512]),
                             reads=[], writes=[("wu", s_)], cls=f"wu{s_}")
                    for tb in range(4):
                        self.P.op("pe", lambda e, tb=tb, k=k, s_=s_, tg=tg: e.matmul(
                            PS[tb][:, :], lhsT=r_(HT[:, k, tg * 512 + tb * 128:tg * 512 + (tb + 1) * 128]), rhs=r_(WU[:, s_, 0:512]),
                            start=(k == 0), stop=(k == NCH - 1)), reads=[("wu", s_), ("HT", k)], writes=[("ps", tb)])
                for tb in range(4):
                    vs = self.tmp_ring.next()
                    self.ew("act", lambda e, tb=tb, vs=vs: e.activation(out=VST[:, vs, :], in_=PS[tb][:, :], func=AF.Copy),
                            reads=[("ps", tb)], writes=[("vst", vs)])
                    r0 = tg * 512 + tb * 128
                    self.dma(out=self.vin[r0:r0 + 128, cb * 512:(cb + 1) * 512], in_=VST[:, vs, :],
                             reads=[("vst", vs)], writes=[("vind", tg, cb, tb)], cls=f"vst{vs}")
        allk = [("qkd", oc) for oc in range(16, 32)]
        allv = [("vind", tg, cb, tb) for tg in range(2) for cb in range(4) for tb in range(4)]
        for g in range(4):
            self.allgather(self.kin[g * 512:(g + 1) * 512, :], self.kall[g], reads=allk, writes=[("kall", g)])
        for g in range(4):
            self.allgather(self.vin[g * 256:(g + 1) * 256, :], self.vall[g], reads=allv, writes=[("vall", g)])
        P.barrier()
        TPOS = RSTD[:, 0:T]
        self.dma(out=TPOS, in_=self.d_tpos, reads=[], writes=[("rstd",)], cls="c_tpos")
        HTf = HT[:, :, :].rearrange("p c n -> p (c n)")
        KT = HTf[:, 0:4096].rearrange("p (a n) -> p a n", a=2)
        VS = HTf[:, 4096:8192].rearrange("p (a k d) -> p a k d", a=2, k=16)
        OT = HTf[:, 8192:16384].rearrange("p (h n) -> p h n", h=16)
        Gf = self.G[:, :, :].rearrange("p a n -> p (a n)")
        SPM = Gf[:, 0:1024].rearrange("p (a n) -> p a n", a=2)
        ATT = Gf[:, 1024:2048].rearrange("p (a n) -> p a n", a=2)
        LSUM = Gf[:, 2048:2560]
        WDf = self.WD[:, :, :].rearrange("p a n -> p (a n)")
        QT = WDf[:, 0:1024].rearrange("p (a n) -> p a n", a=2)
        NQT = WDf[:, 1024:2048].rearrange("p (a n) -> p a n", a=2)
        STf = self.STG[:, :, :].rearrange("p a n -> p (a n)")
        EB = STf[:, 0:1024].rearrange("p (a n) -> p a n", a=2)
        AR = STf[:, 1024:2048].rearrange("p (a n) -> p a n", a=2)
        MASK = self.CG[:, :].rearrange("p (a n) -> p a n", a=2)
        z_ring, c_ring = Ring(2), Ring(2)
        for c in range(2):
            kbmax = 12 if c == 0 else 16
            tiles = []
            for h in range(16):
                for i, kb in enumerate(range(kbmax - 1, -1, -1)):
                    tiles.append(dict(h=h, kb=kb, i=i, last=(i == kbmax - 1)))
            n = len(tiles)

            def stage1(tl, tau):
                h, kb = tl["h"], tl["kb"]
                hs = h % 2
                if tl["i"] == 0:
                    for half in range(2):
                        r0 = half * 512 + (h % 4) * 128
                        self.dma(out=r_(KT[:, hs, half * T:(half + 1) * T]),
                                 in_=r_(self.kall[h // 4][r0:r0 + 128, :]),
                                 reads=[("kall", h // 4)], writes=[("kt", hs, half)], cls=f"kt{hs}{half}")
                    for half in range(2):
                        for g in range(4):
                            pc = half * 4 + g
                            if pc * 2 >= kbmax:
                                continue
                            self.dma(out=r_(VS[:, hs, pc * 2:pc * 2 + 2, :]),
                                     in_=r_(self.vall[g][half * 256:(half + 1) * 256, h * 128:(h + 1) * 128].rearrange("(k p) d -> p k d", p=128)),
                                     reads=[("vall", g)], writes=[("vs", hs, pc)], cls=f"vs{hs}{pc}")
                    self.dma(out=r_(QT[:, hs, :]), in_=r_(self.qT_d[h][:, c * 512:(c + 1) * 512]),
                             reads=[("qkd", h)], writes=[("qt", hs)], cls=f"qt{hs}")
                    self.ew("dve", lambda e, hs=hs: e.tensor_scalar(out=r_(NQT[:, hs, :]), in0=QT[:, hs, :], scalar1=-1.0, scalar2=None,
                                                                   op0=ALU.mult), reads=[("qt", hs)], writes=[("nqt", hs)])
                zb = z_ring.next()
                tl["zb"] = zb
                sl = tau % 2
                tl["sl"] = sl
                self.P.op("pe", lambda e, zb=zb, hs=hs, kb=kb: e.matmul(PS[zb][:, :], lhsT=r_(KT[:, hs, kb * 128:(kb + 1) * 128]),
                                                                     rhs=r_(QT[:, hs, :]), start=True, stop=True),
                          reads=[("kt", hs, kb // 8), ("qt", hs)], writes=[("ps", zb)])
                self.ew("act", lambda e, zb=zb, sl=sl: e.activation(out=EB[:, sl, :], in_=PS[zb][:, :], func=AF.Exp),
                        reads=[("ps", zb)], writes=[("eb", sl)])
                self.ew("act", lambda e, sl=sl: e.activation(out=EB[:, sl, :], in_=EB[:, sl, :], func=AF.Ln, bias=ONESF[:, 0:1]),
                        reads=[("eb", sl), ("onesf",)], writes=[("eb", sl)])
                self.ew("dve", lambda e, sl=sl, kb=kb, c=c: e.tensor_scalar(out=MASK[:, sl, :], in0=TPOS[:, c * 512:(c + 1) * 512],
                                                                     scalar1=KPOS[:, kb:kb + 1], scalar2=None, op0=ALU.is_gt),
                        reads=[("rstd",), ("acst",)], writes=[("mask", sl)])
                self.ew("pool", lambda e, sl=sl: e.tensor_tensor(out=r_(SPM[:, sl, :]), in0=EB[:, sl, :], in1=MASK[:, sl, :], op=ALU.mult),
                        reads=[("eb", sl), ("mask", sl)], writes=[("spm", sl)])

            def stage2(tl):
                h, kb, sl = tl["h"], tl["kb"], tl["sl"]
                hs = h % 2
                cb_ = 2 + c_ring.next()
                first = tl["i"] == 0

                def fn(e, cb_=cb_, sl=sl, hs=hs, kb=kb, first=first):
                    e.matmul(PS[cb_][:, :], lhsT=r_(TRI), rhs=r_(SPM[:, sl, :]), start=True, stop=False)
                    if not first:
                        e.matmul(PS[cb_][:, :], lhsT=r_(ONES[:, :]), rhs=r_(LSUM), start=False, stop=False)
                    return e.matmul(PS[cb_][:, :], lhsT=r_(KT[:, hs, kb * 128:(kb + 1) * 128]), rhs=r_(NQT[:, hs, :]),
                                    start=False, stop=True)

                self.P.op("pe", fn, reads=[("spm", sl), ("lsum",), ("kt", hs, kb // 8), ("nqt", hs), ("tri",), ("ones",)], writes=[("ps", cb_)])
                if first:
                    self.ew("pool", lambda e, sl=sl: e.tensor_copy(out=r_(LSUM), in_=SPM[:, sl, :]),
                            reads=[("spm", sl)], writes=[("lsum",)])
                else:
                    self.ew("pool", lambda e, sl=sl: e.tensor_tensor(out=r_(LSUM), in0=LSUM, in1=SPM[:, sl, :], op=ALU.add),
                            reads=[("spm", sl), ("lsum",)], writes=[("lsum",)])
                self.ew("act", lambda e, cb_=cb_, sl=sl: e.activation(out=AR[:, sl, :], in_=PS[cb_][:, :], func=AF.Exp, scale=-1.0),
                        reads=[("ps", cb_)], writes=[("ar", sl)])
                self.ew("dve", lambda e, sl=sl: e.tensor_tensor(out=r_(ATT[:, sl, :]), in0=AR[:, sl, :], in1=MASK[:, sl, :], op=ALU.mult),
                        reads=[("ar", sl), ("mask", sl)], writes=[("att", sl)])

            def stage3(tl):
                h, kb, sl = tl["h"], tl["kb"], tl["sl"]
                hs = h % 2
                ob = 4 + hs
                self.P.op("pe", lambda e, ob=ob, hs=hs, kb=kb, sl=sl, tl=tl: e.matmul(
                    PS[ob][:, :], lhsT=r_(VS[:, hs, kb, :]), rhs=r_(ATT[:, sl, :]), start=(tl["i"] == 0), stop=tl["last"]),
                    reads=[("vs", hs, kb // 2), ("att", sl)], writes=[("ps", ob)])
                if tl["last"]:
                    self.ew("act", lambda e, ob=ob, h=h: e.activation(out=r_(OT[:, h, :]), in_=PS[ob][:, :], func=AF.Copy),
                            reads=[("ps", ob)], writes=[("ot", h)])

            for tau in range(n + 2):
                if tau < n:
                    stage1(tiles[tau], tau)
                if 0 <= tau - 1 < n:
                    stage2(tiles[tau - 1])
                if 0 <= tau - 2 < n:
                    stage3(tiles[tau - 2])
            for m in range(NCH):
                wsl = []
                for kh in range(2):
                    s_ = self.wu_ring.next()
                    self.dma(out=r_(WU[:, s_, :]), in_=r_(self.d_wo[m][:, kh * 1024:(kh + 1) * 1024]),
                             reads=[], writes=[("wu", s_)], cls=f"wu{s_}")
                    wsl.append(s_)
                pairs = [(r_(WU[:, wsl[h // 8], (h % 8) * 128:(h % 8 + 1) * 128]), r_(OT[:, h, :])) for h in range(16)]
                self.mm_group(PS[6][:, :], pairs, reads=[("wu", wsl[0]), ("wu", wsl[1])] + [("ot", h) for h in range(16)],
                              writes=[("ps", 6)])
                self.ew("dve", lambda e, m=m, c=c: e.tensor_tensor(
                    out=X[:, m, HX + c * 512:HX + (c + 1) * 512], in0=X[:, m, HX + c * 512:HX + (c + 1) * 512],
                    in1=PS[6][:, :], op=ALU.add), reads=[("ps", 6), ("X", m)], writes=[("X", m)])

    def s5(self, l):
        P = self.P
        X, HT, PS, WU, RSTD, ONESF = self.X, self.HT, self.PS, self.WU, self.RSTD, self.ONESF
        SP = self.SPRM
        LR, LI, LS = SP[:, 0, :], SP[:, 1, :], SP[:, 2, :]
        STEP, MAG, U, FRE, FIM, NFIM = SP[:, 3, :], SP[:, 4, :], SP[:, 5, :], SP[:, 6, :], SP[:, 7, :], SP[:, 8, :]
        W1, W2, W3, W4 = SP[:, 9, :], SP[:, 10, :], SP[:, 11, :], SP[:, 12, :]
        NEGPI = self.NEGPI
        SCST = self.PBS
        STATE, INIT = self.STATE, self.INIT
        TWO_PI = float(2 * np.pi)
        RC = 12582912.0
        kp = [("sprm",)]
        self.dma(out=SP[:, 0:3, :], in_=self.d_sprm.rearrange("p (a n) -> p a n", a=3), reads=[], writes=kp, cls="c_sprm")
        self.dma(out=SCST[:, 0:48], in_=self.d_scst, reads=[], writes=[("acst",)], cls="c_pbs")
        d = lambda fn: self.ew("dve", fn, reads=kp + [("negpi",)], writes=kp)
        a_ = lambda fn: self.ew("act", fn, reads=kp + [("negpi",)], writes=kp)
        a_(lambda e: e.activation(out=STEP, in_=LS, func=AF.Exp))
        d(lambda e: e.tensor_tensor(out=W1, in0=LR, in1=STEP, op=ALU.mult))
        a_(lambda e: e.activation(out=MAG, in_=W1, func=AF.Exp))
        d(lambda e: e.tensor_tensor(out=U, in0=LI, in1=STEP, op=ALU.mult))
        d(lambda e: e.tensor_scalar(out=U, in0=U, scalar1=float(1.0 / (2 * np.pi)), scalar2=None, op0=ALU.mult))
        d(lambda e: e.tensor_scalar(out=W3, in0=U, scalar1=RC, scalar2=None, op0=ALU.add))
        d(lambda e: e.scalar_tensor_tensor(out=W1, in0=W3, scalar=RC, in1=U, op0=ALU.subtract, op1=ALU.subtract))
        d(lambda e: e.tensor_scalar(out=W4, in0=U, scalar1=0.25, scalar2=None, op0=ALU.add))
        d(lambda e: e.tensor_scalar(out=W3, in0=W4, scalar1=RC, scalar2=None, op0=ALU.add))
        d(lambda e: e.scalar_tensor_tensor(out=W2, in0=W3, scalar=RC, in1=W4, op0=ALU.subtract, op1=ALU.subtract))
        a_(lambda e: e.activation(out=W1, in_=W1, func=AF.Sin, scale=TWO_PI))
        a_(lambda e: e.activation(out=W2, in_=W2, func=AF.Sin, scale=TWO_PI))
        d(lambda e: e.scalar_tensor_tensor(out=W3, in0=W2, scalar=-1.0, in1=MAG, op0=ALU.mult, op1=ALU.mult))
        d(lambda e: e.scalar_tensor_tensor(out=W4, in0=W1, scalar=-1.0, in1=MAG, op0=ALU.mult, op1=ALU.mult))
        d(lambda e: e.tensor_scalar(out=W3, in0=W3, scalar1=-1.0, scalar2=None, op0=ALU.add))
        d(lambda e: e.tensor_tensor(out=W1, in0=LR, in1=LR, op=ALU.mult))
        d(lambda e: e.tensor_tensor(out=W2, in0=LI, in1=LI, op=ALU.mult))
        d(lambda e: e.tensor_tensor(out=W1, in0=W1, in1=W2, op=ALU.add))
        d(lambda e: e.reciprocal(out=W1, in_=W1))
        d(lambda e: e.tensor_tensor(out=FRE, in0=W3, in1=LR, op=ALU.mult))
        d(lambda e: e.tensor_tensor(out=W2, in0=W4, in1=LI, op=ALU.mult))
        d(lambda e: e.tensor_tensor(out=FRE, in0=FRE, in1=W2, op=ALU.add))
        d(lambda e: e.tensor_tensor(out=FRE, in0=FRE, in1=W1, op=ALU.mult))
        d(lambda e: e.tensor_tensor(out=FIM, in0=W4, in1=LR, op=ALU.mult))
        d(lambda e: e.tensor_tensor(out=W2, in0=W3, in1=LI, op=ALU.mult))
        d(lambda e: e.tensor_tensor(out=FIM, in0=FIM, in1=W2, op=ALU.subtract))
        d(lambda e: e.tensor_tensor(out=FIM, in0=FIM, in1=W1, op=ALU.mult))
        d(lambda e: e.tensor_scalar(out=NFIM, in0=FIM, scalar1=-1.0, scalar2=None, op0=ALU.mult))
        self.rmsnorm(l, HX, [(0, 512), (512, 512)])
        P.barrier()
        TG = RSTD[:, 0:T]
        self.dma(out=TG, in_=self.d_tpos, reads=[], writes=[("rstd",)], cls="c_tpos")
        STf = self.STG[:, :, :].rearrange("p a n -> p (a n)")
        COSN, SINN, BUR, BUI = STf[:, 0:512], STf[:, 512:1024], STf[:, 1024:1536], STf[:, 1536:2048]
        T1, T2 = self.CG[:, 0:512], self.CG[:, 512:1024]
        Gf = self.G[:, :, :].rearrange("p a n -> p (a n)")
        XR = Gf[:, 0:1024].rearrange("p (a n) -> p a n", a=2)
        XIN = Gf[:, 1024:2048].rearrange("p (a n) -> p a n", a=2)
        YO = Gf[:, 2048:3072].rearrange("p (a n) -> p a n", a=2)
        WDf = self.WD[:, :, :].rearrange("p a n -> p (a n)")
        BW = WDf[:, 0:512].rearrange("p (a n) -> p a n", a=2)
        CWt = WDf[:, 512:1024].rearrange("p (a n) -> p a n", a=2)
        bw_ring, cw_ring, x_ring, b_ring = Ring(2), Ring(2), Ring(2), Ring(2)
        keys_t = [("cosn",), ("sinn",), ("bur",), ("bui",), ("t1",), ("t2",)]

        def tile_pass(ct, second):
            c = ct // 4
            bs = bw_ring.next()
            self.dma(out=r_(BW[:, bs, :]), in_=r_(self.d_sbt[ct]), reads=[], writes=[("bw", bs)], cls=f"bw{bs}")
            if second:
                cs = cw_ring.next()
                self.dma(out=r_(CWt[:, cs, :]), in_=r_(self.d_sct[ct]), reads=[], writes=[("cwt", cs)], cls=f"cwt{cs}")
            for tb in range(2):
                blk = slice(tb * 512, (tb + 1) * 512)
                pb = 2 * b_ring.next()
                self.P.op("pe", lambda e, pb=pb, bs=bs, c=c, blk=blk: e.matmul(PS[pb][:, :], lhsT=r_(BW[:, bs, 0:128]), rhs=r_(HT[:, c, blk]),
                                                                           start=True, stop=True),
                          reads=[("bw", bs), ("HT", c)], writes=[("ps", pb)])
                self.P.op("pe", lambda e, pb=pb, bs=bs, c=c, blk=blk: e.matmul(PS[pb + 1][:, :], lhsT=r_(BW[:, bs, 128:256]), rhs=r_(HT[:, c, blk]),
                                                                           start=True, stop=True),
                          reads=[("bw", bs), ("HT", c)], writes=[("ps", pb + 1)])
                self.ew("dve", lambda e, pb=pb, ct=ct: e.tensor_scalar(out=BUR, in0=PS[pb][:, :], scalar1=FRE[:, ct:ct + 1], scalar2=None, op0=ALU.mult),
                        reads=[("ps", pb)] + kp, writes=[("bur",)])
                self.ew("dve", lambda e, pb=pb, ct=ct: e.scalar_tensor_tensor(out=BUR, in0=PS[pb + 1][:, :], scalar=NFIM[:, ct:ct + 1], in1=BUR,
                                                                          op0=ALU.mult, op1=ALU.add),
                        reads=[("ps", pb + 1), ("bur",)] + kp, writes=[("bur",)])
                self.ew("dve", lambda e, pb=pb, ct=ct: e.tensor_scalar(out=BUI, in0=PS[pb + 1][:, :], scalar1=FRE[:, ct:ct + 1], scalar2=None, op0=ALU.mult),
                        reads=[("ps", pb + 1)] + kp, writes=[("bui",)])
                self.ew("dve", lambda e, pb=pb, ct=ct: e.scalar_tensor_tensor(out=BUI, in0=PS[pb][:, :], scalar=FIM[:, ct:ct + 1], in1=BUI,
                                                                          op0=ALU.mult, op1=ALU.add),
                        reads=[("ps", pb), ("bui",)] + kp, writes=[("bui",)])
                self.ew("dve", lambda e, ct=ct, blk=blk: e.tensor_scalar(out=T1, in0=TG[:, blk], scalar1=U[:, ct:ct + 1], scalar2=None,
                                                                       op0=ALU.mult), reads=[("rstd",)] + kp, writes=[("t1",)])
                CC = self.CCOL
                self.ew("pool", lambda e: e.tensor_scalar(out=T2, in0=T1, scalar1=CC[:, 3:4], scalar2=CC[:, 1:2], op0=ALU.mult, op1=ALU.add),
                        reads=[("t1",), ("ccol",)], writes=[("t2",)])
                self.ew("dve", lambda e: e.scalar_tensor_tensor(out=SINN, in0=T2, scalar=RC, in1=T1, op0=ALU.subtract, op1=ALU.subtract),
                        reads=[("t1",), ("t2",)], writes=[("sinn",)])
                self.ew("pool", lambda e: e.tensor_scalar(out=T1, in0=T1, scalar1=CC[:, 3:4], scalar2=CC[:, 2:3], op0=ALU.mult, op1=ALU.add),
                        reads=[("t1",), ("sinn",), ("ccol",)], writes=[("t1",)])
                self.ew("pool", lambda e: e.tensor_scalar(out=T2, in0=T1, scalar1=CC[:, 3:4], scalar2=CC[:, 1:2], op0=ALU.mult, op1=ALU.add),
                        reads=[("t1",), ("sinn",), ("ccol",)], writes=[("t2",)])
                self.ew("dve", lambda e: e.scalar_tensor_tensor(out=COSN, in0=T2, scalar=RC, in1=T1, op0=ALU.subtract, op1=ALU.subtract),
                        reads=[("t1",), ("t2",)], writes=[("cosn",)])
                self.ew("act", lambda e: e.activation(out=SINN, in_=SINN, func=AF.Sin, scale=TWO_PI),
                        reads=[("sinn",)], writes=[("sinn",)])
                self.ew("act", lambda e: e.activation(out=COSN, in_=COSN, func=AF.Sin, scale=TWO_PI),
                        reads=[("cosn",)], writes=[("cosn",)])
                pl = lambda fn, rd, wr: self.ew("pool", fn, reads=rd, writes=wr)
                pl(lambda e: e.tensor_tensor(out=T1, in0=BUR, in1=COSN, op=ALU.mult), [("bur",), ("cosn",)], [("t1",)])
                pl(lambda e: e.tensor_tensor(out=T2, in0=BUI, in1=SINN, op=ALU.mult), [("bui",), ("sinn",)], [("t2",)])
                pl(lambda e: e.tensor_tensor(out=T1, in0=T1, in1=T2, op=ALU.add), [("t1",), ("t2",)], [("t1",)])
                pl(lambda e: e.tensor_tensor(out=T2, in0=BUI, in1=COSN, op=ALU.mult), [("bui",), ("cosn",), ("t1",)], [("t2",)])
                pl(lambda e: e.tensor_tensor(out=BUR, in0=BUR, in1=SINN, op=ALU.mult), [("bur",), ("sinn",)], [("bur",)])
                pl(lambda e: e.tensor_tensor(out=T2, in0=T2, in1=BUR, op=ALU.subtract), [("t2",), ("bur",)], [("t2",)])
                for comp, TT, tk in ((0, T1, ("t1",)), (1, T2, ("t2",))):
                    if tb == 0:
                        init = INIT[:, ct, comp:comp + 1] if second else 0.0
                        rd = [tk, ("init",)] + kp
                    else:
                        init = STATE[:, ct, comp:comp + 1]
                        rd = [tk, ("state", ct)] + kp
                    self.ew("dve", lambda e, TT=TT, init=init, ct=ct: e.tensor_tensor_scan(
                        out=TT, data0=MAG[:, ct:ct + 1].to_broadcast([128, 512]), data1=TT, initial=init, op0=ALU.mult, op1=ALU.add),
                        reads=rd, writes=[tk])
                for comp, TT, tk in ((0, T1, ("t1",)), (1, T2, ("t2",))):
                    self.ew("act", lambda e, TT=TT, ct=ct, comp=comp: e.activation(out=STATE[:, ct, comp:comp + 1], in_=TT[:, 511:512], func=AF.Copy),
                            reads=[tk], writes=[("state", ct)])
                if not second:
                    continue
                xs = x_ring.next()
                pl(lambda e: e.tensor_tensor(out=BUR, in0=T1, in1=COSN, op=ALU.mult), [("t1",), ("cosn",)], [("bur",)])
                pl(lambda e: e.tensor_tensor(out=BUI, in0=T2, in1=SINN, op=ALU.mult), [("t2",), ("sinn",)], [("bui",)])
                pl(lambda e, xs=xs: e.tensor_tensor(out=r_(XR[:, xs, :]), in0=BUR, in1=BUI, op=ALU.subtract), [("bur",), ("bui",)], [("xr", xs)])
                self.ew("dve", lambda e: e.tensor_tensor(out=BUR, in0=T1, in1=SINN, op=ALU.mult), reads=[("t1",), ("sinn",), ("xr", xs)], writes=[("bur",)])
                self.ew("dve", lambda e: e.tensor_tensor(out=BUI, in0=T2, in1=COSN, op=ALU.mult), reads=[("t2",), ("cosn",), ("xr", xs)], writes=[("bui",)])
                self.ew("dve", lambda e, xs=xs: e.scalar_tensor_tensor(out=r_(XIN[:, xs, :]), in0=BUR, scalar=-1.0, in1=BUI, op0=ALU.mult, op1=ALU.subtract),
                        reads=[("bur",), ("bui",)], writes=[("xin", xs)])
                yb = 4 + tb
                q = ct % 4
                self.P.op("pe", lambda e, yb=yb, cs=cs, xs=xs, q=q: e.matmul(PS[yb][:, :], lhsT=r_(CWt[:, cs, 0:128]), rhs=r_(XR[:, xs, :]),
                                                                         start=(q == 0), stop=False),
                          reads=[("cwt", cs), ("xr", xs)], writes=[("ps", yb)])
                self.P.op("pe", lambda e, yb=yb, cs=cs, xs=xs, q=q: e.matmul(PS[yb][:, :], lhsT=r_(CWt[:, cs, 128:256]), rhs=r_(XIN[:, xs, :]),
                                                                         start=False, stop=(q == 3)),
                          reads=[("cwt", cs), ("xin", xs)], writes=[("ps", yb)])
                if q == 3:
                    self.ew("dve", lambda e, yb=yb, c=c, blk=blk: e.scalar_tensor_tensor(out=PS[yb][:, :], in0=HT[:, c, blk], scalar=SCST[:, c:c + 1],
                                                                                     in1=PS[yb][:, :], op0=ALU.mult, op1=ALU.add),
                            reads=[("HT", c), ("acst",), ("ps", yb)], writes=[("ps", yb)])
                    self.ew("act", lambda e, yb=yb: e.activation(out=PS[6][:, :], in_=PS[yb][:, :], func=AF.Square),
                            reads=[("ps", yb)], writes=[("ps", 6)])
                    self.ew("dve", lambda e: e.tensor_scalar(out=PS[6][:, :], in0=PS[6][:, :], scalar1=0.044715, scalar2=1.0, op0=ALU.mult, op1=ALU.add),
                            reads=[("ps", 6)], writes=[("ps", 6)])
                    self.ew("dve", lambda e, yb=yb, tb=tb: e.tensor_copy(out=r_(YO[:, tb, :]), in_=PS[yb][:, :]),
                            reads=[("ps", yb)], writes=[("yo", tb)])
                    self.ew("dve", lambda e, tb=tb: e.tensor_tensor(out=PS[6][:, :], in0=PS[6][:, :], in1=YO[:, tb, :], op=ALU.mult),
                            reads=[("ps", 6), ("yo", tb)], writes=[("ps", 6)])
                    self.ew("act", lambda e: e.activation(out=PS[6][:, :], in_=PS[6][:, :], func=AF.Sigmoid, scale=1.5957691216057308),
                            reads=[("ps", 6)], writes=[("ps", 6)])
                    self.ew("dve", lambda e, tb=tb: e.tensor_tensor(out=r_(YO[:, tb, :]), in0=YO[:, tb, :], in1=PS[6][:, :], op=ALU.mult),
                            reads=[("ps", 6), ("yo", tb)], writes=[("yo", tb)])
                    self.dma(out=self.yd[c][:, blk], in_=YO[:, tb, :], reads=[("yo", tb)], writes=[("yd", c, tb)], cls=f"yo{tb}")

        for ct in range(64):
            tile_pass(ct, False)
        skeys = [("state", ct) for ct in range(64)]
        self.dma(out=self.c_ins, in_=STATE[:, :, :].rearrange("p a b -> p (a b)"), reads=skeys, writes=[("cin", "s")], cls="tst")
        self.allgather(self.c_ins, self.c_outs, reads=[("cin", "s")], writes=[("cout", "s")])
        self.dma(out=INIT[:, :, :].rearrange("p a b -> p (a b)"), in_=self.c_outs[0:128, :], reads=[("cout", "s")], writes=[("init",)], cls="tld")
        GATE = self.GATE
        self.ew("dve", lambda e: e.tensor_scalar(out=INIT[:, :, :], in0=INIT[:, :, :], scalar1=GATE[:, 0:1], scalar2=None, op0=ALU.mult),
                reads=[("init",), ("gate",)], writes=[("init",)])
        for ct in range(64):
            tile_pass(ct, True)
        P.barrier()
        for c in range(NCH):
            self.dma(out=r_(HT[:, c, 0:T]), in_=r_(self.yd[c]), reads=[("yd", c, 0), ("yd", c, 1)], writes=[("HT", c)], cls=f"xin{c}")
        htkeys = [("HT", k) for k in range(NCH)]
        SG = self.STG
        for m in range(NCH):
            banks = {}
            for half in range(2):
                chunk = m + NCH * half
                wsl = []
                for kh in range(2):
                    s_ = self.wu_ring.next()
                    self.dma(out=r_(WU[:, s_, :]), in_=r_(self.d_wglu[chunk][:, kh * 1024:(kh + 1) * 1024]),
                             reads=[], writes=[("wu", s_)], cls=f"wu{s_}")
                    wsl.append(s_)
                for th in range(2):
                    bank = half * 2 + th
                    banks[(half, th)] = bank
                    pairs = [(r_(WU[:, wsl[k // 8], (k % 8) * 128:(k % 8 + 1) * 128]), r_(HT[:, k, th * 512:(th + 1) * 512])) for k in range(NCH)]
                    self.mm_group(PS[bank][:, :], pairs, reads=[("wu", wsl[0]), ("wu", wsl[1])] + htkeys, writes=[("ps", bank)])
            for th in range(2):
                bv, bg = banks[(0, th)], banks[(1, th)]
                self.ew("act", lambda e, bg=bg, th=th, m=m: e.activation(out=SG[:, th, 0:512], in_=PS[bg][:, :], func=AF.Sigmoid,
                                                                       bias=SCST[:, 32 + m:33 + m]),
                        reads=[("ps", bg), ("acst",)], writes=[("sg", th)])
                self.ew("dve", lambda e, bv=bv, th=th, m=m: e.scalar_tensor_tensor(out=SG[:, th, 0:512], in0=PS[bv][:, :], scalar=SCST[:, 16 + m:17 + m],
                                                                               in1=SG[:, th, 0:512], op0=ALU.add, op1=ALU.mult),
                        reads=[("ps", bv), ("sg", th), ("acst",)], writes=[("sg", th)])
                self.ew("dve", lambda e, th=th, m=m: e.tensor_tensor(out=X[:, m, HX + th * 512:HX + (th + 1) * 512],
                                                                   in0=X[:, m, HX + th * 512:HX + (th + 1) * 512], in1=SG[:, th, 0:512], op=ALU.add),
                        reads=[("sg", th), ("X", m)], writes=[("X", m)])

    def ffn(self, l):
        X, HT, PS = self.X, self.HT, self.PS
        WU, WD, STG, CG, G, CW = self.WU, self.WD, self.STG, self.CG, self.G, self.CW
        self.dma(out=CW[:, :], in_=self.d_cw[l], reads=[], writes=[("cw",)], cls="c_cw")
        self.rmsnorm(DEPTH + l, HX - 2, [(0, 342), (342, 342), (684, 342)])
        GF = 2
        ngroups = NFP // GF
        htkeys = [("HT", k) for k in range(NCH)]

        def down(g, gs):
            for ch in range(2):
                slots = []
                for jj in range(GF):
                    j = g * GF + jj
                    s = self.wd_ring.next()
                    self.dma(out=r_(WD[:, s, :]), in_=r_(self.d_wd[l][j][:, ch * 1024:(ch + 1) * 1024]),
                             reads=[], writes=[("wd", s)], cls=f"wd{s}")
                    slots.append(s)
                for mm in range(8):
                    m = ch * 8 + mm
                    for th in range(2):
                        bank = 4 + self.dn_ring.next()
                        pairs = [(r_(WD[:, slots[jj], mm * 128:(mm + 1) * 128]),
                                  r_(G[:, gs * GF + jj, th * 512:(th + 1) * 512])) for jj in range(GF)]
                        self.mm_group(PS[bank][:, :], pairs,
                                      reads=[("wd", s) for s in slots] + [("g", gs * GF + jj) for jj in range(GF)],
                                      writes=[("ps", bank)])
                        self.ew("dve", lambda e, bank=bank, m=m, th=th: e.tensor_tensor(
                            out=X[:, m, HX + th * 512:HX + (th + 1) * 512],
                            in0=X[:, m, HX + th * 512:HX + (th + 1) * 512], in1=PS[bank][:, :], op=ALU.add),
                            reads=[("ps", bank), ("X", m)], writes=[("X", m)])

        for g in range(ngroups):
            gs = g % 2
            for jj in range(GF):
                j = g * GF + jj
                gi = gs * GF + jj
                for half in range(2):
                    chunk = j + NFP * half
                    wsl = []
                    for kh in range(2):
                        s = self.wu_ring.next()
                        self.dma(out=r_(WU[:, s, :]), in_=r_(self.d_wu[l][chunk][:, kh * 1024:(kh + 1) * 1024]),
                                 reads=[], writes=[("wu", s)], cls=f"wu{s}")
                        wsl.append(s)
                    for blk in range(3):
                        bank = self.up_ring.next()
                        pairs = [(r_(WU[:, wsl[k // 8], (k % 8) * 128:(k % 8 + 1) * 128]),
                                  r_(HT[:, k, blk * 342:(blk + 1) * 342])) for k in range(NCH)]
                        self.mm_group(PS[bank][:, 0:342], pairs,
                                      reads=[("wu", wsl[0]), ("wu", wsl[1])] + htkeys, writes=[("ps", bank)])
                        self.ew("act", lambda e, bank=bank, half=half, blk=blk: e.activation(
                            out=STG[:, half, blk * 342:(blk + 1) * 342], in_=PS[bank][:, 0:342], func=AF.Copy),
                            reads=[("ps", bank)], writes=[("stg", half, blk)])
                    eng = "dve" if half == 0 else "pool"
                    dst = r_(G[:, gi, :]) if half == 0 else CG[:, :]
                    dsrc = G[:, gi, :] if half == 0 else CG[:, :]
                    dkey = ("g", gi) if half == 0 else ("cg",)
                    cwb = chunk * 4
                    self.ew(eng, lambda e, dst=dst, half=half, cwb=cwb: e.tensor_scalar(
                        out=dst, in0=STG[:, half, 2:2 + T], scalar1=CW[:, cwb + 2:cwb + 3], scalar2=CW[:, cwb + 3:cwb + 4],
                        op0=ALU.mult, op1=ALU.add), reads=[("stg", half, 0), ("stg", half, 1), ("stg", half, 2), ("cw",)], writes=[dkey])
                    for tap in (1, 0):
                        if eng == "dve":
                            self.ew(eng, lambda e, dst=dst, dsrc=dsrc, half=half, cwb=cwb, tap=tap: e.scalar_tensor_tensor(
                                out=dst, in0=STG[:, half, tap:tap + T], scalar=CW[:, cwb + tap:cwb + tap + 1], in1=dsrc,
                                op0=ALU.mult, op1=ALU.add), reads=[("stg", half, 0), ("stg", half, 1), ("stg", half, 2), ("cw",), dkey], writes=[dkey])
                        else:
                            PT = self.RSTD[:, 0:T]
                            self.ew(eng, lambda e, half=half, cwb=cwb, tap=tap, PT=PT: e.tensor_scalar(
                                out=PT, in0=STG[:, half, tap:tap + T], scalar1=CW[:, cwb + tap:cwb + tap + 1], scalar2=self.CCOL[:, 0:1],
                                op0=ALU.mult, op1=ALU.add), reads=[("stg", half, 0), ("stg", half, 1), ("stg", half, 2), ("cw",), ("ccol",)], writes=[("rstd",)])
                            self.ew(eng, lambda e, dst=dst, dsrc=dsrc, PT=PT: e.tensor_tensor(
                                out=dst, in0=dsrc, in1=PT, op=ALU.add), reads=[("rstd",), dkey], writes=[dkey])
                self.ew("act", lambda e: e.activation(out=CG[:, :], in_=CG[:, :], func=AF.Silu),
                        reads=[("cg",)], writes=[("cg",)])
                self.ew("dve", lambda e, gi=gi: e.tensor_tensor(out=r_(G[:, gi, :]), in0=G[:, gi, :], in1=CG[:, :], op=ALU.mult),
                        reads=[("cg",), ("g", gi)], writes=[("g", gi)])
            if g >= 1:
                down(g - 1, (g - 1) % 2)
        down(ngroups - 1, (ngroups - 1) % 2)

    def build(self, ncores):
        nc = self.nc
        self.ncores = ncores
        self.declare_dram()
        P = self.P
        from contextlib import ExitStack
        with ExitStack() as es:
            sb = lambda name, shape: es.enter_context(nc.sbuf_tensor(name, shape, F32))
            self.X = sb("X", [128, NCH, XW])
            self.HT = sb("HT", [128, NCH, XW])
            self.RSTD = sb("RSTD", [128, XW])
            self.SQ = sb("SQ", [128, 2, 512])
            self.ACST = sb("ACST", [128, 128])
            self.SPRM = sb("SPRM", [128, 13, 64])
            self.NEGPI = sb("NEGPI", [128, 1])
            self.CCOL = sb("CCOL", [128, 4])
            self.STATE = sb("STATE", [128, 64, 2])
            self.INIT = sb("INIT", [128, 64, 2])
            self.ONES = sb("ONES", [128, 128])
            self.ONESF = sb("ONESF", [128, 128])
            self.GAIN = sb("GAIN", [128, 2 * DEPTH * NCH])
            self.GATE = sb("GATE", [128, 1])
            self.INVC = sb("INVC", [128, 64])
            self.TB = sb("TB", [128, NCH, 16])
            self.CW = sb("CW", [128, 2 * NFP * 4])
            self.PBS = sb("PBS", [128, 48])
            self.T16 = sb("T16", [128, 4, 16])
            self.WU = sb("WU", [128, 4, 1024])
            self.WD = sb("WD", [128, 4, 1024])
            self.STG = sb("STG", [128, 2, XW])
            self.CG = sb("CG", [128, T])
            self.G = sb("G", [128, 4, T])
            self.PW = self.WU[:, 0:4, :].rearrange("p (a b) n -> p a (b n)", a=2)
            self.PL = self.G
            self.TA = self.STG
            self.PS = [es.enter_context(nc.psum_tensor(f"ps{i}", [128, 512], F32)) for i in range(8)]
            self.TBF = None
            self.TMP = self.CG[:, :].rearrange("p (a n) -> p a n", a=2)
            self.sq_ring = Ring(2)
            self.wu_ring = Ring(4)
            self.wd_ring = Ring(4)
            self.up_ring = Ring(4)
            self.dn_ring = Ring(3)
            self.pw_ring = Ring(2)
            self.tmp_ring = Ring(2)

            xkeys = [("X", c) for c in range(NCH)]
            for c in range(NCH):
                self.dma(out=self.X[:, c, :], in_=self.d_x[:, c, :], reads=[], writes=[("X", c)], cls=f"xin{c}")
            self.dma(out=self.GAIN[:, :], in_=self.d_gain, reads=[], writes=[("gain",)], cls="c_gain")
            self.dma(out=self.GATE[:, :], in_=self.d_gate, reads=[], writes=[("gate",)], cls="c_gate")
            self.dma(out=self.INVC[:, :], in_=self.d_invc, reads=[], writes=[("invc",)], cls="c_invc")
            self.ew("pool", lambda e: e.memset(self.ONESF[:, :], 1.0), reads=[], writes=[("onesf",)])
            self.ew("pool", lambda e: e.memset(self.CCOL[:, 0:1], 0.0), reads=[], writes=[("ccol",)])
            self.ew("pool", lambda e: e.memset(self.CCOL[:, 1:2], 12582912.0), reads=[("ccol",)], writes=[("ccol",)])
            self.ew("pool", lambda e: e.memset(self.CCOL[:, 2:3], 0.25), reads=[("ccol",)], writes=[("ccol",)])
            self.ew("pool", lambda e: e.memset(self.CCOL[:, 3:4], 1.0), reads=[("ccol",)], writes=[("ccol",)])
            self.ew("pool", lambda e: e.memset(self.NEGPI[:, :], 0.0), reads=[], writes=[("negpi",)])
            self.ew("dve", lambda e: e.tensor_copy(out=r_(self.ONES[:, :]), in_=self.ONESF[:, :]),
                    reads=[("onesf",)], writes=[("ones",)])
            for l in self.layers:
                kind = l % 3
                if kind == 0:
                    if l != 0:
                        if l != self.first:
                            self.exchange_tail(16)
                    P.barrier()
                    self.pool_mixer(l)
                elif kind == 1:
                    P.barrier()
                    self.attention(l)
                else:
                    P.barrier()
                    self.s5(l)
                P.barrier()
                self.exchange_tail(2)
                self.ffn(l)
                P.barrier()
            for c in range(NCH):
                self.dma(out=self.d_out[:, c, :], in_=self.X[:, c, HX:XW], reads=[("X", c)], writes=[("out", c)], cls=f"xin{c}")

            clss = P.classes()
            names = list(Prog.ENGINES[:4]) + clss
            sems = {}
            for n in names:
                sems[n] = es.enter_context(nc.semaphore("s_" + n))
            block = es.enter_context(nc.Block())
            P.emit(block, sems)
        return nc


def _feat_major(v):
    n = v.shape[-1] // 128
    return np.ascontiguousarray(v.reshape(n, 128).T)


def prep_shared(inp, layers):
    sh = {}
    gains = np.concatenate([inp["norm_mix_g"], inp["norm_ffn_g"]], axis=0)
    sh["gains"] = np.ascontiguousarray(gains.reshape(2 * DEPTH, NCH, 128).transpose(2, 0, 1).reshape(128, -1))
    for l in layers:
        wu = inp["ffn_w_up"][l]
        sh[f"wu{l}"] = np.ascontiguousarray(wu.reshape(NCH, 128, 2 * NFP, 128).transpose(2, 1, 0, 3).reshape(2 * NFP, 128, NCH * 128))
        sh[f"wd{l}"] = np.ascontiguousarray(inp["ffn_w_down"][l].reshape(NFP, 128, D))
        cw = np.concatenate([inp["ffn_conv_w"][l], inp["ffn_conv_b"][l][None]], axis=0)
        sh[f"cw{l}"] = np.ascontiguousarray(cw.reshape(4, 2 * NFP, 128).transpose(2, 1, 0).reshape(128, -1))
        if l % 3 == 2:
            j = l // 3
            ch = lambda v: np.ascontiguousarray(v.reshape(64, 128).T)
            ls = np.repeat(inp["ssm_log_step"][j][:, None], 64, axis=1)
            sh[f"sprm{l}"] = np.ascontiguousarray(np.concatenate([ch(inp["ssm_lam_re"][j]), ch(inp["ssm_lam_im"][j]), ch(ls)], axis=1))
            sbt = np.zeros((64, 128, 256), np.float32)
            sct = np.zeros((64, 128, 256), np.float32)
            for ct in range(64):
                q = ct % 4
                for gg in range(2):
                    g = 2 * ct + gg
                    rows = slice(q * 32 + gg * 16, q * 32 + gg * 16 + 16)
                    cols = slice(gg * 64, gg * 64 + 64)
                    sbt[ct, rows, 0:128][:, cols] = inp["ssm_b_re"][j][g].T
                    sbt[ct, rows, 128:256][:, cols] = inp["ssm_b_im"][j][g].T
                    sct[ct, cols, 0:128][:, rows] = inp["ssm_c_re"][j][g].T
                    sct[ct, cols, 128:256][:, rows] = inp["ssm_c_im"][j][g].T
            sh[f"sbt{l}"] = sbt
            sh[f"sct{l}"] = sct
            sh[f"scst{l}"] = np.ascontiguousarray(np.concatenate([_feat_major(inp["ssm_d"][j]), _feat_major(inp["ssm_b_glu"][j])], axis=1))
            sh[f"wglu{l}"] = np.ascontiguousarray(inp["ssm_w_glu"][j].reshape(NCH, 128, 32, 128).transpose(2, 1, 0, 3).reshape(32, 128, NCH * 128))
        if l % 3 == 1:
            j = l // 3
            wqkv = inp["sb_w_qkv"][j]
            sh[f"wqk{l}"] = np.ascontiguousarray(wqkv[:, 0:4096].reshape(NCH, 128, 32, 128).transpose(2, 1, 0, 3).reshape(32, 128, NCH * 128))
            sh[f"wv{l}"] = np.ascontiguousarray(wqkv[:, 4096:6144])
            sh[f"wo{l}"] = np.ascontiguousarray(inp["sb_w_o"][j].reshape(NCH, 128, NCH, 128).transpose(2, 1, 0, 3).reshape(NCH, 128, NCH * 128))
            acst = np.zeros((128, 18), np.float32)
            acst[:, 0:16] = np.arange(128, dtype=np.float32)[:, None] + 128.0 * np.arange(16, dtype=np.float32)[None, :]
            acst[:, 16] = inp["sb_q_gain"][j]
            acst[:, 17] = inp["sb_k_gain"][j]
            sh[f"acst{l}"] = acst
            sh["tri"] = np.ascontiguousarray(np.tril(np.ones((128, 128), np.float32)))
        if l % 3 == 0:
            j = l // 3
            pw = inp["pool_w"][j]
            sh[f"pw{l}"] = np.ascontiguousarray(pw.reshape(4, 4, 128, 512).transpose(0, 2, 1, 3).reshape(4, 128, 2048))
            sh[f"pbs{l}"] = np.ascontiguousarray(np.concatenate([_feat_major(inp["pool_b"][j]), _feat_major(inp["pool_scale"][j])], axis=1))
    return sh


def prep_core(xT_full_b, h):
    m = {}
    xs = np.zeros((D, XW), np.float32)
    lo = h * T
    xs[:, HX:] = xT_full_b[:, lo:lo + T]
    if h == 1:
        xs[:, :HX] = xT_full_b[:, lo - HX:lo]
    m["xT"] = np.ascontiguousarray(xs.reshape(NCH, 128, XW).transpose(1, 0, 2))
    m["gate"] = np.full((128, 1), float(h), np.float32)
    invc = np.zeros((4, 16), np.float32)
    for g in range(4):
        w = 2 << g
        for i in range(16):
            t = lo + i
            invc[g, i] = 1.0 / min(t + 1, w)
    m["invc"] = np.ascontiguousarray(np.broadcast_to(invc.reshape(1, 64), (128, 64)))
    m["tpos"] = np.ascontiguousarray(np.broadcast_to((lo + np.arange(T, dtype=np.float32))[None, :], (128, T)))
    return m


_NC_CACHE = {}


def run_layers(xT_all, inp, layers, nb=BATCH):
    ncores = 2 * nb
    key = (tuple(layers), ncores)
    if key not in _NC_CACHE:
        nc = bass.Bass("TRN2", target_bir_lowering=False)
        nc.dge_precook = False
        bld = Builder(nc, list(layers), layers[0], layers[-1])
        bld.build(ncores)
        _NC_CACHE[key] = (nc, list(bld.in_names))
    nc, in_names = _NC_CACHE[key]
    sh = prep_shared(inp, layers)
    in_maps = []
    for b in range(nb):
        for h in range(2):
            m = dict(sh)
            m.update(prep_core(xT_all[b], h))
            in_maps.append({k: m[k] for k in in_names})
    import time as _t, sys as _s
    _t0 = _t.time()
    res = run_bass_kernel_spmd(nc, in_maps, core_ids=list(range(ncores)))
    print("[kernel] launch layers=%s took %.1fs" % (list(layers), _t.time() - _t0), file=_s.stderr)
    out = np.zeros_like(xT_all)
    for b in range(nb):
        for h in range(2):
            o = res.results[2 * b + h]["outT"]
            out[b][:, h * T:(h + 1) * T] = o.transpose(1, 0, 2).reshape(D, T)
    return out


LAUNCH_GROUPS = [[0, 1, 2, 3]]


def kernel(**inputs):
    inp = {k: np.asarray(v) for k, v in inputs.items()}
    x = inp["x"]
    xT = np.ascontiguousarray(x.transpose(0, 2, 1))
    for grp in LAUNCH_GROUPS:
        xT = run_layers(xT, inp, grp)
    return np.ascontiguousarray(xT.transpose(0, 2, 1)).astype(np.float32)
```

```python
import numpy as np
import concourse.bass as bass
import concourse.mybir as mybir
from concourse.bass_utils import run_bass_kernel_spmd

F32 = mybir.dt.float32
F32R = mybir.dt.float32r
AF = mybir.ActivationFunctionType
ALU = mybir.AluOpType

D = 2048
NCH = 16
SEQ = 2048
BATCH = 4
T = 1024
HX = 16
XW = HX + T
DFF = 5632
NFP = 44
EPS = 1e-6
DEPTH = 4
SAME_ENGINE_SYNC = True


def r_(ap):
    return ap.bitcast(F32R)


class Op:
    __slots__ = ("eng", "fn", "deps", "cls", "inc", "count", "needed")

    def __init__(self, eng, fn, deps, cls, inc):
        self.eng = eng
        self.fn = fn
        self.deps = deps
        self.cls = cls
        self.inc = inc
        self.count = 0
        self.needed = False


class Ring:
    def __init__(self, n):
        self.n = n
        self.i = -1

    def next(self):
        self.i = (self.i + 1) % self.n
        return self.i


class Prog:
    ENGINES = ("pe", "act", "dve", "pool", "sp")

    def __init__(self, nc):
        self.nc = nc
        self.ops = []
        self.last_w = {}
        self.readers = {}

    def _deps(self, reads, writes, me):
        deps = set()
        for k in reads:
            w = self.last_w.get(k)
            if w is not None:
                deps.add(w)
        for k in writes:
            w = self.last_w.get(k)
            if w is not None:
                deps.add(w)
            deps.update(self.readers.get(k, ()))
        for k in reads:
            self.readers.setdefault(k, []).append(me)
        for k in writes:
            self.last_w[k] = me
            self.readers[k] = []
        deps.discard(me)
        return deps

    def op(self, eng, fn, reads=(), writes=(), cls=None, inc=16):
        me = len(self.ops)
        deps = self._deps(reads, writes, me)
        self.ops.append(Op(eng, fn, deps, cls, inc))
        return me

    def barrier(self):
        n = len(self.ops)
        last = {}
        for i in range(n - 1, -1, -1):
            o = self.ops[i]
            key = o.cls if o.cls is not None else o.eng
            if key not in last:
                last[key] = i
        deps = set(last.values())
        for e in self.ENGINES:
            self.ops.append(Op(e, None, set(deps), None, 0))

    def emit(self, block, sems):
        ops = self.ops
        self.barrier()
        for o in ops:
            for d in o.deps:
                ops[d].needed = True
        cnt = {}
        for o in ops:
            if o.cls is not None:
                cnt[o.cls] = cnt.get(o.cls, 0) + o.inc
                o.count = cnt[o.cls]
            elif o.needed and o.fn is not None:
                cnt[o.eng] = cnt.get(o.eng, 0) + 1
                o.count = cnt[o.eng]
            elif o.fn is None:
                o.count = -1
        def semkey(o):
            return o.cls if o.cls is not None else o.eng

        streams = {e: [] for e in self.ENGINES}
        for i, o in enumerate(ops):
            streams[o.eng].append(i)

        def make(engname):
            idxs = streams[engname]

            def body(e):
                seen = {}
                for i in idxs:
                    o = ops[i]
                    waits = {}
                    for d in o.deps:
                        od = ops[d]
                        if od.fn is None:
                            continue
                        k = semkey(od)
                        if od.cls is None and od.eng == engname and not SAME_ENGINE_SYNC:
                            continue
                        if od.cls is None and od.eng == engname and engname == "pe":
                            continue
                        if od.count > waits.get(k, 0):
                            waits[k] = od.count
                    for k, v in waits.items():
                        if v > seen.get(k, 0):
                            e.wait_ge(sems[k], v)
                            seen[k] = v
                    if o.fn is None:
                        continue
                    ins = o.fn(e)
                    if o.cls is not None:
                        ins.then_inc(sems[o.cls], o.inc)
                    elif o.needed:
                        ins.then_inc(sems[o.eng], 1)

            return body

        block.tensor(make("pe"))
        block.scalar(make("act"))
        block.vector(make("dve"))
        block.gpsimd(make("pool"))
        block.sync(make("sp"))

    def classes(self):
        s = set()
        for o in self.ops:
            if o.cls is not None:
                s.add(o.cls)
        return sorted(s)


class Builder:
    def __init__(self, nc, layers, first, last):
        self.nc = nc
        self.layers = layers
        self.first = first
        self.last = last
        self.P = Prog(nc)

    def declare_dram(self):
        nc = self.nc
        self.in_names = []

        def dt(name, shape, kind="ExternalInput"):
            if kind == "ExternalInput":
                self.in_names.append(name)
            return nc.dram_tensor(name, shape, F32, kind=kind).ap()
        self.d_x = dt("xT", [128, NCH, XW])
        self.d_out = dt("outT", [128, NCH, T], "ExternalOutput")
        self.d_gain = dt("gains", [128, 2 * DEPTH * NCH])
        self.d_gate = dt("gate", [128, 1])
        self.d_invc = dt("invc", [128, 4 * 16])
        self.d_wu = {}
        self.d_wd = {}
        self.d_cw = {}
        self.d_pw = {}
        self.d_pbs = {}
        for l in self.layers:
            self.d_wu[l] = dt(f"wu{l}", [2 * NFP, 128, NCH * 128])
            self.d_wd[l] = dt(f"wd{l}", [NFP, 128, D])
            self.d_cw[l] = dt(f"cw{l}", [128, 2 * NFP * 4])
            if l % 3 == 0:
                self.d_pw[l] = dt(f"pw{l}", [4, 128, 4 * 512])
                self.d_pbs[l] = dt(f"pbs{l}", [128, 2 * NCH])
            if l % 3 == 2:
                self.d_sprm = dt(f"sprm{l}", [128, 3 * 64])
                self.d_sbt = dt(f"sbt{l}", [64, 128, 256])
                self.d_sct = dt(f"sct{l}", [64, 128, 256])
                self.d_scst = dt(f"scst{l}", [128, 16 + 32])
                self.d_wglu = dt(f"wglu{l}", [32, 128, NCH * 128])
                if not hasattr(self, "d_tpos"):
                    self.d_tpos = dt("tpos", [128, T])
                self.yd = nc.dram_tensor("yd", [NCH, 128, T], F32).ap()
                self.c_ins = nc.dram_tensor("c_ins", [128, 128], F32).ap()
                self.c_outs = nc.dram_tensor("c_outs", [256, 128], F32).ap()
            if l % 3 == 1:
                self.d_wqk = dt(f"wqk{l}", [32, 128, NCH * 128])
                self.d_wv = dt(f"wv{l}", [D, D])
                self.d_wo = dt(f"wo{l}", [NCH, 128, NCH * 128])
                self.d_tri = dt("tri", [128, 128])
                self.d_acst = dt(f"acst{l}", [128, 18])
                if not hasattr(self, "d_tpos"):
                    self.d_tpos = dt("tpos", [128, T])
                self.qT_d = nc.dram_tensor("qT_d", [NCH, 128, T], F32).ap()
                self.kin = nc.dram_tensor("kin", [D, T], F32).ap()
                self.kall = nc.dram_tensor("kall", [4, 1024, T], F32).ap()
                self.vin = nc.dram_tensor("vin", [T, D], F32).ap()
                self.vall = nc.dram_tensor("vall", [4, 512, D], F32).ap()
        self.c_in2 = nc.dram_tensor("c_in2", [128, NCH * 2], F32).ap()
        self.c_out2 = nc.dram_tensor("c_out2", [256, NCH * 2], F32).ap()
        self.c_in16 = nc.dram_tensor("c_in16", [128, NCH * 16], F32).ap()
        self.c_out16 = nc.dram_tensor("c_out16", [256, NCH * 16], F32).ap()

    def mm_group(self, ps_ap, pairs, reads, writes):
        def fn(e, ps_ap=ps_ap, pairs=pairs):
            n = len(pairs)
            ins = None
            for i, (l, r) in enumerate(pairs):
                ins = e.matmul(ps_ap, lhsT=l, rhs=r, start=(i == 0), stop=(i == n - 1))
            return ins

        self.P.op("pe", fn, reads, writes)

    def dma(self, out, in_, reads, writes, cls, eng="sp"):
        self.P.op(eng, lambda e, out=out, in_=in_: e.dma_start(out=out, in_=in_), reads, writes, cls=cls, inc=16)

    def allgather(self, cin, cout, reads, writes):
        pairs = [[0, 1], [2, 3], [4, 5], [6, 7]][: self.ncores // 2]

        def fn(e, cin=cin, cout=cout):
            return e.collective_compute("AllGather", ALU.bypass, replica_groups=pairs, ins=[cin], outs=[cout])

        self.P.op("pool", fn, reads, writes, cls="cc", inc=1)

    def ew(self, eng, fn, reads, writes):
        self.P.op(eng, fn, reads, writes)

    def rmsnorm(self, gidx, xc0, blocks):
        X, HT, SQ, RSTD, ONES, GAIN = self.X, self.HT, self.SQ, self.RSTD, self.ONES, self.GAIN
        PS = self.PS
        bank = 7
        for (b0, w) in blocks:
            for c in range(NCH):
                s = self.sq_ring.next()
                self.ew("act", lambda e, s=s, c=c, b0=b0, w=w: e.activation(
                    out=r_(SQ[:, s, 0:w]), in_=X[:, c, xc0 + b0:xc0 + b0 + w], func=AF.Square),
                    reads=[("X", c)], writes=[("sq", s)])
                self.P.op("pe", lambda e, s=s, c=c, w=w: e.matmul(
                    PS[bank][:, 0:w], lhsT=r_(ONES[:, :]), rhs=r_(SQ[:, s, 0:w]),
                    start=(c == 0), stop=(c == NCH - 1)),
                    reads=[("sq", s), ("ones",)], writes=[("ps", bank)])
            self.ew("dve", lambda e, b0=b0, w=w: e.tensor_scalar(
                out=RSTD[:, b0:b0 + w], in0=PS[bank][:, 0:w], scalar1=1.0 / D, scalar2=float(EPS),
                op0=ALU.mult, op1=ALU.add),
                reads=[("ps", bank)], writes=[("rstd",)])
            self.ew("act", lambda e, b0=b0, w=w: e.activation(
                out=RSTD[:, b0:b0 + w], in_=RSTD[:, b0:b0 + w], func=AF.Sqrt),
                reads=[("rstd",)], writes=[("rstd",)])
            self.ew("dve", lambda e, b0=b0, w=w: e.reciprocal(
                out=RSTD[:, b0:b0 + w], in_=RSTD[:, b0:b0 + w]),
                reads=[("rstd",)], writes=[("rstd",)])
            for c in range(NCH):
                if c % 2 == 0:
                    self.ew("dve", lambda e, c=c, b0=b0, w=w: e.scalar_tensor_tensor(
                        out=r_(HT[:, c, b0:b0 + w]), in0=X[:, c, xc0 + b0:xc0 + b0 + w],
                        scalar=GAIN[:, gidx * NCH + c:gidx * NCH + c + 1], in1=RSTD[:, b0:b0 + w],
                        op0=ALU.mult, op1=ALU.mult),
                        reads=[("X", c), ("rstd",), ("gain",)], writes=[("HT", c)])
                else:
                    self.ew("pool", lambda e, c=c, b0=b0, w=w: e.tensor_scalar(
                        out=r_(HT[:, c, b0:b0 + w]), in0=X[:, c, xc0 + b0:xc0 + b0 + w],
                        scalar1=GAIN[:, gidx * NCH + c:gidx * NCH + c + 1], scalar2=self.CCOL[:, 0:1], op0=ALU.mult, op1=ALU.add),
                        reads=[("X", c), ("gain",), ("ccol",)], writes=[("HT", c)])
                    self.ew("pool", lambda e, c=c, b0=b0, w=w: e.tensor_tensor(
                        out=r_(HT[:, c, b0:b0 + w]), in0=HT[:, c, b0:b0 + w], in1=RSTD[:, b0:b0 + w], op=ALU.mult),
                        reads=[("HT", c), ("rstd",)], writes=[("HT", c)])

    def exchange_tail(self, ncols):
        X, TB = self.X, self.TB
        cin = self.c_in2 if ncols == 2 else self.c_in16
        cout = self.c_out2 if ncols == 2 else self.c_out16
        tag = f"t{ncols}"
        xkeys = [("X", c) for c in range(NCH)]
        self.dma(out=cin.rearrange("p (c n) -> p c n", n=ncols), in_=X[:, :, XW - ncols:XW],
                 reads=xkeys, writes=[("cin", tag)], cls="tst")
        self.allgather(cin, cout, reads=[("cin", tag)], writes=[("cout", tag)])
        self.dma(out=TB[:, :, 0:ncols], in_=cout[0:128, :].rearrange("p (c n) -> p c n", n=ncols),
                 reads=[("cout", tag)], writes=[("TB",)], cls="tld")
        GATE = self.GATE
        self.ew("dve", lambda e: e.tensor_scalar(
            out=X[:, :, HX - ncols:HX], in0=TB[:, :, 0:ncols], scalar1=GATE[:, 0:1], scalar2=None,
            op0=ALU.mult), reads=[("TB",), ("gate",)], writes=xkeys)

    def pool_mixer(self, l):
        P = self.P
        X, HT, PS = self.X, self.HT, self.PS
        PW, PL, TA, TBF, T16, TMP, PBS, INVC = self.PW, self.PL, self.TA, self.TBF, self.T16, self.TMP, self.PBS, self.INVC
        self.dma(out=PBS[:, 0:2 * NCH], in_=self.d_pbs[l], reads=[], writes=[("pbs",)], cls="c_pbs")
        self.ew("dve", lambda e: e.tensor_tensor(out=PBS[:, 0:NCH], in0=PBS[:, 0:NCH], in1=PBS[:, NCH:2 * NCH], op=ALU.mult),
                reads=[("pbs",)], writes=[("pbs",)])
        self.rmsnorm(l, 0, [(0, 346), (346, 346), (692, 348)])
        for g in range(4):
            w = 2 << g
            ws = self.pw_ring.next()
            self.dma(out=r_(PW[:, ws, :]), in_=r_(self.d_pw[l][g]), reads=[], writes=[("pw", ws)], cls=f"pw{ws}")
            for kk in range(4):
                c = 4 * g + kk
                eng = "dve" if kk % 2 == 0 else "pool"
                src = HT[:, c, :]
                eng = "dve"
                bufs = [TA[:, 0, :], TA[:, 1, :]]
                bkeys = [("ta", 0), ("ta", 1)]
                skey = ("HT", c)
                sh = 1
                lo = 0
                for st in range(g + 1):
                    dst = bufs[st % 2]
                    dkey = bkeys[st % 2]
                    lo2 = lo + sh
                    self.ew(eng, lambda e, dst=dst, src=src, lo2=lo2, sh=sh: e.tensor_tensor(
                        out=dst[:, lo2:XW], in0=src[:, lo2:XW], in1=src[:, lo2 - sh:XW - sh], op=ALU.add),
                        reads=[skey], writes=[dkey])
                    src, skey = dst, dkey
                    lo = lo2
                    sh *= 2
                self.ew(eng, lambda e, src=src, kk=kk, c=c, w=w: e.scalar_tensor_tensor(
                    out=r_(PL[:, kk, :]), in0=src[:, HX:XW], scalar=1.0 / w, in1=HT[:, c, HX:XW],
                    op0=ALU.mult, op1=ALU.subtract),
                    reads=[skey, ("HT", c)], writes=[("pl", kk)])
                self.ew(eng, lambda e, src=src, kk=kk, g=g: e.tensor_tensor(
                    out=T16[:, kk, :], in0=src[:, HX:HX + 16], in1=INVC[:, g * 16:(g + 1) * 16], op=ALU.mult),
                    reads=[skey, ("invc",)], writes=[("t16", kk)])
                self.ew(eng, lambda e, kk=kk, c=c: e.tensor_tensor(
                    out=r_(PL[:, kk, 0:16]), in0=T16[:, kk, :], in1=HT[:, c, HX:HX + 16], op=ALU.subtract),
                    reads=[("t16", kk), ("HT", c)], writes=[("pl", kk)])
            for mm in range(4):
                m = 4 * g + mm
                for th in range(2):
                    bank = 4 + self.dn_ring.next()
                    pairs = [(r_(PW[:, ws, kk * 512 + mm * 128:kk * 512 + (mm + 1) * 128]),
                              r_(PL[:, kk, th * 512:(th + 1) * 512])) for kk in range(4)]
                    self.mm_group(PS[bank][:, :], pairs,
                                  reads=[("pw", ws)] + [("pl", kk) for kk in range(4)], writes=[("ps", bank)])
                    ts = self.tmp_ring.next()
                    self.ew("act", lambda e, bank=bank, ts=ts, m=m: e.activation(
                        out=TMP[:, ts, :], in_=PS[bank][:, :], func=AF.Identity,
                        bias=PBS[:, m:m + 1], scale=PBS[:, NCH + m:NCH + m + 1]),
                        reads=[("ps", bank), ("pbs",)], writes=[("tmp", ts)])
                    self.ew("dve", lambda e, ts=ts, m=m, th=th: e.tensor_tensor(
                        out=X[:, m, HX + th * 512:HX + (th + 1) * 512], in0=X[:, m, HX + th * 512:HX + (th + 1) * 512],
                        in1=TMP[:, ts, :], op=ALU.add),
                        reads=[("tmp", ts), ("X", m)], writes=[("X", m)])

    def attention(self, l):
        P = self.P
        X, HT, PS, WU, SQ, RSTD, ONES, ONESF = self.X, self.HT, self.PS, self.WU, self.SQ, self.RSTD, self.ONES, self.ONESF
        TRI = self.ACST[:, 0:128]
        KPOS = self.PBS[:, 0:16]
        QKG = self.PBS[:, 16:18]
        htkeys = [("HT", k) for k in range(NCH)]
        self.dma(out=r_(TRI), in_=r_(self.d_tri), reads=[], writes=[("tri",)], cls="c_tri")
        self.dma(out=self.PBS[:, 0:18], in_=self.d_acst, reads=[], writes=[("acst",)], cls="c_pbs")
        self.ew("dve", lambda e: e.tensor_scalar(out=QKG[:, 0:1], in0=QKG[:, 0:1], scalar1=float(128 ** -0.5), scalar2=None,
                                                  op0=ALU.mult), reads=[("acst",)], writes=[("acst",)])
        self.rmsnorm(l, HX, [(0, 512), (512, 512)])
        QKS = self.STG
        VST = self.CG[:, :].rearrange("p (a n) -> p a n", a=2)
        for oc in range(32):
            qk = oc // 16
            hh = oc % 16
            wsl = []
            for kh in range(2):
                s_ = self.wu_ring.next()
                self.dma(out=r_(WU[:, s_, :]), in_=r_(self.d_wqk[oc][:, kh * 1024:(kh + 1) * 1024]),
                         reads=[], writes=[("wu", s_)], cls=f"wu{s_}")
                wsl.append(s_)
            st = oc % 2
            for th in range(2):
                bank = self.up_ring.next()
                pairs = [(r_(WU[:, wsl[k // 8], (k % 8) * 128:(k % 8 + 1) * 128]),
                          r_(HT[:, k, th * 512:(th + 1) * 512])) for k in range(NCH)]
                self.mm_group(PS[bank][:, :], pairs, reads=[("wu", wsl[0]), ("wu", wsl[1])] + htkeys, writes=[("ps", bank)])
                sq = self.sq_ring.next()
                self.ew("act", lambda e, bank=bank, sq=sq: e.activation(out=r_(SQ[:, sq, :]), in_=PS[bank][:, :], func=AF.Square),
                        reads=[("ps", bank)], writes=[("sq", sq)])
                self.P.op("pe", lambda e, sq=sq: e.matmul(PS[7][:, :], lhsT=r_(ONES[:, :]), rhs=r_(SQ[:, sq, :]), start=True, stop=True),
                          reads=[("sq", sq), ("ones",)], writes=[("ps", 7)])
                self.ew("dve", lambda e: e.tensor_scalar(out=RSTD[:, 0:512], in0=PS[7][:, :], scalar1=1.0 / 128, scalar2=float(EPS),
                                                          op0=ALU.mult, op1=ALU.add), reads=[("ps", 7)], writes=[("rstd",)])
                self.ew("act", lambda e: e.activation(out=RSTD[:, 0:512], in_=RSTD[:, 0:512], func=AF.Sqrt),
                        reads=[("rstd",)], writes=[("rstd",)])
                self.ew("dve", lambda e: e.reciprocal(out=RSTD[:, 0:512], in_=RSTD[:, 0:512]), reads=[("rstd",)], writes=[("rstd",)])
                self.ew("dve", lambda e, bank=bank, st=st, th=th, qk=qk: e.scalar_tensor_tensor(
                    out=QKS[:, st, th * 512:(th + 1) * 512], in0=PS[bank][:, :], scalar=QKG[:, qk:qk + 1], in1=RSTD[:, 0:512],
                    op0=ALU.mult, op1=ALU.mult), reads=[("ps", bank), ("rstd",), ("acst",)], writes=[("qks", st)])
            dst = self.qT_d[hh] if qk == 0 else self.kin[hh * 128:(hh + 1) * 128, :]
            self.dma(out=dst, in_=QKS[:, st, 0:T], reads=[("qks", st)], writes=[("qkd", oc)], cls=f"qks{st}")
        for tg in range(2):
            for cb in range(4):
                for k in range(NCH):
                    s_ = self.wu_ring.next()
                    self.dma(out=r_(WU[:, s_, 0:512]), in_=r_(self.d_wv[k * 128:(k + 1) * 128, cb * 512:(cb + 1) * 512]),
                             reads=[], writes=[("wu", s_)], cls=f"wu{s_}")
                    for tb in range(4):
                        self.P.op("pe", lambda e, tb=tb, k=k, s_=s_, tg=tg: e.matmul(
                            PS[tb][:, :], lhsT=r_(HT[:, k, tg * 512 + tb * 128:tg * 512 + (tb + 1) * 128]), rhs=r_(WU[:, s_, 0:512]),
                            start=(k == 0), stop=(k == NCH - 1)), reads=[("wu", s_), ("HT", k)], writes=[("ps", tb)])
                for tb in range(4):
                    vs = self.tmp_ring.next()
                    self.ew("act", lambda e, tb=tb, vs=vs: e.activation(out=VST[:, vs, :], in_=PS[tb][:, :], func=AF.Copy),
                            reads=[("ps", tb)], writes=[("vst", vs)])
                    r0 = tg * 512 + tb * 128
                    self.dma(out=self.vin[r0:r0 + 128, cb * 512:(cb + 1) * 512], in_=VST[:, vs, :],
                             reads=[("vst", vs)], writes=[("vind", tg, cb, tb)], cls=f"vst{vs}")
        allk = [("qkd", oc) for oc in range(16, 32)]
        allv = [("vind", tg, cb, tb) for tg in range(2) for cb in range(4) for tb in range(4)]
        for g in range(4):
            self.allgather(self.kin[g * 512:(g + 1) * 512, :], self.kall[g], reads=allk, writes=[("kall", g)])
        for g in range(4):
            self.allgather(self.vin[g * 256:(g + 1) * 256, :], self.vall[g], reads=allv, writes=[("vall", g)])
        P.barrier()
        TPOS = RSTD[:, 0:T]
        self.dma(out=TPOS, in_=self.d_tpos, reads=[], writes=[("rstd",)], cls="c_tpos")
        HTf = HT[:, :, :].rearrange("p c n -> p (c n)")
        KT = HTf[:, 0:4096].rearrange("p (a n) -> p a n", a=2)
        VS = HTf[:, 4096:8192].rearrange("p (a k d) -> p a k d", a=2, k=16)
        OT = HTf[:, 8192:16384].rearrange("p (h n) -> p h n", h=16)
        Gf = self.G[:, :, :].rearrange("p a n -> p (a n)")
        SPM = Gf[:, 0:1024].rearrange("p (a n) -> p a n", a=2)
        ATT = Gf[:, 1024:2048].rearrange("p (a n) -> p a n", a=2)
        LSUM = Gf[:, 2048:2560]
        WDf = self.WD[:, :, :].rearrange("p a n -> p (a n)")
        QT = WDf[:, 0:1024].rearrange("p (a n) -> p a n", a=2)
        NQT = WDf[:, 1024:2048].rearrange("p (a n) -> p a n", a=2)
        STf = self.STG[:, :, :].rearrange("p a n -> p (a n)")
        EB = STf[:, 0:1024].rearrange("p (a n) -> p a n", a=2)
        AR = STf[:, 1024:2048].rearrange("p (a n) -> p a n", a=2)
        MASK = self.CG[:, :].rearrange("p (a n) -> p a n", a=2)
        z_ring, c_ring = Ring(2), Ring(2)
        for c in range(2):
            kbmax = 12 if c == 0 else 16
            tiles = []
            for h in range(16):
                for i, kb in enumerate(range(kbmax - 1, -1, -1)):
                    tiles.append(dict(h=h, kb=kb, i=i, last=(i == kbmax - 1)))
            n = len(tiles)

            def stage1(tl, tau):
                h, kb = tl["h"], tl["kb"]
                hs = h % 2
                if tl["i"] == 0:
                    for half in range(2):
                        r0 = half * 512 + (h % 4) * 128
                        self.dma(out=r_(KT[:, hs, half * T:(half + 1) * T]),
                                 in_=r_(self.kall[h // 4][r0:r0 + 128, :]),
                                 reads=[("kall", h // 4)], writes=[("kt", hs, half)], cls=f"kt{hs}{half}")
                    for half in range(2):
                        for g in range(4):
                            pc = half * 4 + g
                            if pc * 2 >= kbmax:
                                continue
                            self.dma(out=r_(VS[:, hs, pc * 2:pc * 2 + 2, :]),
                                     in_=r_(self.vall[g][half * 256:(half + 1) * 256, h * 128:(h + 1) * 128].rearrange("(k p) d -> p k d", p=128)),
                                     reads=[("vall", g)], writes=[("vs", hs, pc)], cls=f"vs{hs}{pc}")
                    self.dma(out=r_(QT[:, hs, :]), in_=r_(self.qT_d[h][:, c * 512:(c + 1) * 512]),
                             reads=[("qkd", h)], writes=[("qt", hs)], cls=f"qt{hs}")
                    self.ew("dve", lambda e, hs=hs: e.tensor_scalar(out=r_(NQT[:, hs, :]), in0=QT[:, hs, :], scalar1=-1.0, scalar2=None,
                                                                   op0=ALU.mult), reads=[("qt", hs)], writes=[("nqt", hs)])
                zb = z_ring.next()
                tl["zb"] = zb
                sl = tau % 2
                tl["sl"] = sl
                self.P.op("pe", lambda e, zb=zb, hs=hs, kb=kb: e.matmul(PS[zb][:, :], lhsT=r_(KT[:, hs, kb * 128:(kb + 1) * 128]),
                                                                     rhs=r_(QT[:, hs, :]), start=True, stop=True),
                          reads=[("kt", hs, kb // 8), ("qt", hs)], writes=[("ps", zb)])
                self.ew("act", lambda e, zb=zb, sl=sl: e.activation(out=EB[:, sl, :], in_=PS[zb][:, :], func=AF.Exp),
                        reads=[("ps", zb)], writes=[("eb", sl)])
                self.ew("act", lambda e, sl=sl: e.activation(out=EB[:, sl, :], in_=EB[:, sl, :], func=AF.Ln, bias=ONESF[:, 0:1]),
                        reads=[("eb", sl), ("onesf",)], writes=[("eb", sl)])
                self.ew("dve", lambda e, sl=sl, kb=kb, c=c: e.tensor_scalar(out=MASK[:, sl, :], in0=TPOS[:, c * 512:(c + 1) * 512],
                                                                     scalar1=KPOS[:, kb:kb + 1], scalar2=None, op0=ALU.is_gt),
                        reads=[("rstd",), ("acst",)], writes=[("mask", sl)])
                self.ew("pool", lambda e, sl=sl: e.tensor_tensor(out=r_(SPM[:, sl, :]), in0=EB[:, sl, :], in1=MASK[:, sl, :], op=ALU.mult),
                        reads=[("eb", sl), ("mask", sl)], writes=[("spm", sl)])

            def stage2(tl):
                h, kb, sl = tl["h"], tl["kb"], tl["sl"]
                hs = h % 2
                cb_ = 2 + c_ring.next()
                first = tl["i"] == 0

                def fn(e, cb_=cb_, sl=sl, hs=hs, kb=kb, first=first):
                    e.matmul(PS[cb_][:, :], lhsT=r_(TRI), rhs=r_(SPM[:, sl, :]), start=True, stop=False)
                    if not first:
                        e.matmul(PS[cb_][:, :], lhsT=r_(ONES[:, :]), rhs=r_(LSUM), start=False, stop=False)
                    return e.matmul(PS[cb_][:, :], lhsT=r_(KT[:, hs, kb * 128:(kb + 1) * 128]), rhs=r_(NQT[:, hs, :]),
                                    start=False, stop=True)

                self.P.op("pe", fn, reads=[("spm", sl), ("lsum",), ("kt", hs, kb // 8), ("nqt", hs), ("tri",), ("ones",)], writes=[("ps", cb_)])
                if first:
                    self.ew("pool", lambda e, sl=sl: e.tensor_copy(out=r_(LSUM), in_=SPM[:, sl, :]),
                            reads=[("spm", sl)], writes=[("lsum",)])
                else:
                    self.ew("pool", lambda e, sl=sl: e.tensor_tensor(out=r_(LSUM), in0=LSUM, in1=SPM[:, sl, :], op=ALU.add),
                            reads=[("spm", sl), ("lsum",)], writes=[("lsum",)])
                self.ew("act", lambda e, cb_=cb_, sl=sl: e.activation(out=AR[:, sl, :], in_=PS[cb_][:, :], func=AF.Exp, scale=-1.0),
                        reads=[("ps", cb_)], writes=[("ar", sl)])
                self.ew("dve", lambda e, sl=sl: e.tensor_tensor(out=r_(ATT[:, sl, :]), in0=AR[:, sl, :], in1=MASK[:, sl, :], op=ALU.mult),
                        reads=[("ar", sl), ("mask", sl)], writes=[("att", sl)])

            def stage3(tl):
                h, kb, sl = tl["h"], tl["kb"], tl["sl"]
                hs = h % 2
                ob = 4 + hs
                self.P.op("pe", lambda e, ob=ob, hs=hs, kb=kb, sl=sl, tl=tl: e.matmul(
                    PS[ob][:, :], lhsT=r_(VS[:, hs, kb, :]), rhs=r_(ATT[:, sl, :]), start=(tl["i"] == 0), stop=tl["last"]),
                    reads=[("vs", hs, kb // 2), ("att", sl)], writes=[("ps", ob)])
                if tl["last"]:
                    self.ew("act", lambda e, ob=ob, h=h: e.activation(out=r_(OT[:, h, :]), in_=PS[ob][:, :], func=AF.Copy),
                            reads=[("ps", ob)], writes=[("ot", h)])

            for tau in range(n + 2):
                if tau < n:
                    stage1(tiles[tau], tau)
                if 0 <= tau - 1 < n:
                    stage2(tiles[tau - 1])
                if 0 <= tau - 2 < n:
                    stage3(tiles[tau - 2])
            for m in range(NCH):
                wsl = []
                for kh in range(2):
                    s_ = self.wu_ring.next()
                    self.dma(out=r_(WU[:, s_, :]), in_=r_(self.d_wo[m][:, kh * 1024:(kh + 1) * 1024]),
                             reads=[], writes=[("wu", s_)], cls=f"wu{s_}")
                    wsl.append(s_)
                pairs = [(r_(WU[:, wsl[h // 8], (h % 8) * 128:(h % 8 + 1) * 128]), r_(OT[:, h, :])) for h in range(16)]
                self.mm_group(PS[6][:, :], pairs, reads=[("wu", wsl[0]), ("wu", wsl[1])] + [("ot", h) for h in range(16)],
                              writes=[("ps", 6)])
                self.ew("dve", lambda e, m=m, c=c: e.tensor_tensor(
                    out=X[:, m, HX + c * 512:HX + (c + 1) * 512], in0=X[:, m, HX + c * 512:HX + (c + 1) * 512],
                    in1=PS[6][:, :], op=ALU.add), reads=[("ps", 6), ("X", m)], writes=[("X", m)])

    def s5(self, l):
        P = self.P
        X, HT, PS, WU, RSTD, ONESF = self.X, self.HT, self.PS, self.WU, self.RSTD, self.ONESF
        SP = self.SPRM
        LR, LI, LS = SP[:, 0, :], SP[:, 1, :], SP[:, 2, :]
        STEP, MAG, U, FRE, FIM, NFIM = SP[:, 3, :], SP[:, 4, :], SP[:, 5, :], SP[:, 6, :], SP[:, 7, :], SP[:, 8, :]
        W1, W2, W3, W4 = SP[:, 9, :], SP[:, 10, :], SP[:, 11, :], SP[:, 12, :]
        NEGPI = self.NEGPI
        SCST = self.PBS
        STATE, INIT = self.STATE, self.INIT
        TWO_PI = float(2 * np.pi)
        RC = 12582912.0
        kp = [("sprm",)]
        self.dma(out=SP[:, 0:3, :], in_=self.d_sprm.rearrange("p (a n) -> p a n", a=3), reads=[], writes=kp, cls="c_sprm")
        self.dma(out=SCST[:, 0:48], in_=self.d_scst, reads=[], writes=[("acst",)], cls="c_pbs")
        d = lambda fn: self.ew("dve", fn, reads=kp + [("negpi",)], writes=kp)
        a_ = lambda fn: self.ew("act", fn, reads=kp + [("negpi",)], writes=kp)
        a_(lambda e: e.activation(out=STEP, in_=LS, func=AF.Exp))
        d(lambda e: e.tensor_tensor(out=W1, in0=LR, in1=STEP, op=ALU.mult))
        a_(lambda e: e.activation(out=MAG, in_=W1, func=AF.Exp))
        d(lambda e: e.tensor_tensor(out=U, in0=LI, in1=STEP, op=ALU.mult))
        d(lambda e: e.tensor_scalar(out=U, in0=U, scalar1=float(1.0 / (2 * np.pi)), scalar2=None, op0=ALU.mult))
        d(lambda e: e.tensor_scalar(out=W3, in0=U, scalar1=RC, scalar2=None, op0=ALU.add))
        d(lambda e: e.scalar_tensor_tensor(out=W1, in0=W3, scalar=RC, in1=U, op0=ALU.subtract, op1=ALU.subtract))
        d(lambda e: e.tensor_scalar(out=W4, in0=U, scalar1=0.25, scalar2=None, op0=ALU.add))
        d(lambda e: e.tensor_scalar(out=W3, in0=W4, scalar1=RC, scalar2=None, op0=ALU.add))
        d(lambda e: e.scalar_tensor_tensor(out=W2, in0=W3, scalar=RC, in1=W4, op0=ALU.subtract, op1=ALU.subtract))
        a_(lambda e: e.activation(out=W1, in_=W1, func=AF.Sin, scale=TWO_PI))
        a_(lambda e: e.activation(out=W2, in_=W2, func=AF.Sin, scale=TWO_PI))
        d(lambda e: e.scalar_tensor_tensor(out=W3, in0=W2, scalar=-1.0, in1=MAG, op0=ALU.mult, op1=ALU.mult))
        d(lambda e: e.scalar_tensor_tensor(out=W4, in0=W1, scalar=-1.0, in1=MAG, op0=ALU.mult, op1=ALU.mult))
        d(lambda e: e.tensor_scalar(out=W3, in0=W3, scalar1=-1.0, scalar2=None, op0=ALU.add))
        d(lambda e: e.tensor_tensor(out=W1, in0=LR, in1=LR, op=ALU.mult))
        d(lambda e: e.tensor_tensor(out=W2, in0=LI, in1=LI, op=ALU.mult))
        d(lambda e: e.tensor_tensor(out=W1, in0=W1, in1=W2, op=ALU.add))
        d(lambda e: e.reciprocal(out=W1, in_=W1))
        d(lambda e: e.tensor_tensor(out=FRE, in0=W3, in1=LR, op=ALU.mult))
        d(lambda e: e.tensor_tensor(out=W2, in0=W4, in1=LI, op=ALU.mult))
        d(lambda e: e.tensor_tensor(out=FRE, in0=FRE, in1=W2, op=ALU.add))
        d(lambda e: e.tensor_tensor(out=FRE, in0=FRE, in1=W1, op=ALU.mult))
        d(lambda e: e.tensor_tensor(out=FIM, in0=W4, in1=LR, op=ALU.mult))
        d(lambda e: e.tensor_tensor(out=W2, in0=W3, in1=LI, op=ALU.mult))
        d(lambda e: e.tensor_tensor(out=FIM, in0=FIM, in1=W2, op=ALU.subtract))
        d(lambda e: e.tensor_tensor(out=FIM, in0=FIM, in1=W1, op=ALU.mult))
        d(lambda e: e.tensor_scalar(out=NFIM, in0=FIM, scalar1=-1.0, scalar2=None, op0=ALU.mult))
        self.rmsnorm(l, HX, [(0, 512), (512, 512)])
        P.barrier()
        TG = RSTD[:, 0:T]
        self.dma(out=TG, in_=self.d_tpos, reads=[], writes=[("rstd",)], cls="c_tpos")
        STf = self.STG[:, :, :].rearrange("p a n -> p (a n)")
        COSN, SINN, BUR, BUI = STf[:, 0:512], STf[:, 512:1024], STf[:, 1024:1536], STf[:, 1536:2048]
        T1, T2 = self.CG[:, 0:512], self.CG[:, 512:1024]
        Gf = self.G[:, :, :].rearrange("p a n -> p (a n)")
        XR = Gf[:, 0:1024].rearrange("p (a n) -> p a n", a=2)
        XIN = Gf[:, 1024:2048].rearrange("p (a n) -> p a n", a=2)
        YO = Gf[:, 2048:3072].rearrange("p (a n) -> p a n", a=2)
        WDf = self.WD[:, :, :].rearrange("p a n -> p (a n)")
        BW = WDf[:, 0:512].rearrange("p (a n) -> p a n", a=2)
        CWt = WDf[:, 512:1024].rearrange("p (a n) -> p a n", a=2)
        bw_ring, cw_ring, x_ring, b_ring = Ring(2), Ring(2), Ring(2), Ring(2)
        keys_t = [("cosn",), ("sinn",), ("bur",), ("bui",), ("t1",), ("t2",)]

        def tile_pass(ct, second):
            c = ct // 4
            bs = bw_ring.next()
            self.dma(out=r_(BW[:, bs, :]), in_=r_(self.d_sbt[ct]), reads=[], writes=[("bw", bs)], cls=f"bw{bs}")
            if second:
                cs = cw_ring.next()
                self.dma(out=r_(CWt[:, cs, :]), in_=r_(self.d_sct[ct]), reads=[], writes=[("cwt", cs)], cls=f"cwt{cs}")
            for tb in range(2):
                blk = slice(tb * 512, (tb + 1) * 512)
                pb = 2 * b_ring.next()
                self.P.op("pe", lambda e, pb=pb, bs=bs, c=c, blk=blk: e.matmul(PS[pb][:, :], lhsT=r_(BW[:, bs, 0:128]), rhs=r_(HT[:, c, blk]),
                                                                           start=True, stop=True),
                          reads=[("bw", bs), ("HT", c)], writes=[("ps", pb)])
                self.P.op("pe", lambda e, pb=pb, bs=bs, c=c, blk=blk: e.matmul(PS[pb + 1][:, :], lhsT=r_(BW[:, bs, 128:256]), rhs=r_(HT[:, c, blk]),
                                                                           start=True, stop=True),
                          reads=[("bw", bs), ("HT", c)], writes=[("ps", pb + 1)])
                self.ew("dve", lambda e, pb=pb, ct=ct: e.tensor_scalar(out=BUR, in0=PS[pb][:, :], scalar1=FRE[:, ct:ct + 1], scalar2=None, op0=ALU.mult),
                        reads=[("ps", pb)] + kp, writes=[("bur",)])
                self.ew("dve", lambda e, pb=pb, ct=ct: e.scalar_tensor_tensor(out=BUR, in0=PS[pb + 1][:, :], scalar=NFIM[:, ct:ct + 1], in1=BUR,
                                                                          op0=ALU.mult, op1=ALU.add),
                        reads=[("ps", pb + 1), ("bur",)] + kp, writes=[("bur",)])
                self.ew("dve", lambda e, pb=pb, ct=ct: e.tensor_scalar(out=BUI, in0=PS[pb + 1][:, :], scalar1=FRE[:, ct:ct + 1], scalar2=None, op0=ALU.mult),
                        reads=[("ps", pb + 1)] + kp, writes=[("bui",)])
                self.ew("dve", lambda e, pb=pb, ct=ct: e.scalar_tensor_tensor(out=BUI, in0=PS[pb][:, :], scalar=FIM[:, ct:ct + 1], in1=BUI,
                                                                          op0=ALU.mult, op1=ALU.add),
                        reads=[("ps", pb), ("bui",)] + kp, writes=[("bui",)])
                self.ew("dve", lambda e, ct=ct, blk=blk: e.tensor_scalar(out=T1, in0=TG[:, blk], scalar1=U[:, ct:ct + 1], scalar2=None,
                                                                       op0=ALU.mult), reads=[("rstd",)] + kp, writes=[("t1",)])
                self.ew("dve", lambda e: e.tensor_scalar(out=T2, in0=T1, scalar1=RC, scalar2=None, op0=ALU.add),
                        reads=[("t1",)], writes=[("t2",)])
                self.ew("dve", lambda e: e.scalar_tensor_tensor(out=SINN, in0=T2, scalar=RC, in1=T1, op0=ALU.subtract, op1=ALU.subtract),
                        reads=[("t1",), ("t2",)], writes=[("sinn",)])
                self.ew("dve", lambda e: e.tensor_scalar(out=T1, in0=T1, scalar1=0.25, scalar2=None, op0=ALU.add),
                        reads=[("t1",), ("sinn",)], writes=[("t1",)])
                self.ew("dve", lambda e: e.tensor_scalar(out=T2, in0=T1, scalar1=RC, scalar2=None, op0=ALU.add),
                        reads=[("t1",), ("sinn",)], writes=[("t2",)])
                self.ew("dve", lambda e: e.scalar_tensor_tensor(out=COSN, in0=T2, scalar=RC, in1=T1, op0=ALU.subtract, op1=ALU.subtract),
                        reads=[("t1",), ("t2",)], writes=[("cosn",)])
                self.ew("act", lambda e: e.activation(out=SINN, in_=SINN, func=AF.Sin, scale=TWO_PI),
                        reads=[("sinn",)], writes=[("sinn",)])
                self.ew("act", lambda e: e.activation(out=COSN, in_=COSN, func=AF.Sin, scale=TWO_PI),
                        reads=[("cosn",)], writes=[("cosn",)])
                pl = lambda fn, rd, wr: self.ew("pool", fn, reads=rd, writes=wr)
                pl(lambda e: e.tensor_tensor(out=T1, in0=BUR, in1=COSN, op=ALU.mult), [("bur",), ("cosn",)], [("t1",)])
                pl(lambda e: e.tensor_tensor(out=T2, in0=BUI, in1=SINN, op=ALU.mult), [("bui",), ("sinn",)], [("t2",)])
                pl(lambda e: e.tensor_tensor(out=T1, in0=T1, in1=T2, op=ALU.add), [("t1",), ("t2",)], [("t1",)])
                pl(lambda e: e.tensor_tensor(out=T2, in0=BUI, in1=COSN, op=ALU.mult), [("bui",), ("cosn",), ("t1",)], [("t2",)])
                pl(lambda e: e.tensor_tensor(out=BUR, in0=BUR, in1=SINN, op=ALU.mult), [("bur",), ("sinn",)], [("bur",)])
                pl(lambda e: e.tensor_tensor(out=T2, in0=T2, in1=BUR, op=ALU.subtract), [("t2",), ("bur",)], [("t2",)])
                for comp, TT, tk in ((0, T1, ("t1",)), (1, T2, ("t2",))):
                    if tb == 0:
                        init = INIT[:, ct, comp:comp + 1] if second else 0.0
                        rd = [tk, ("init",)] + kp
                    else:
                        init = STATE[:, ct, comp:comp + 1]
                        rd = [tk, ("state", ct)] + kp
                    self.ew("dve", lambda e, TT=TT, init=init, ct=ct: e.tensor_tensor_scan(
                        out=TT, data0=MAG[:, ct:ct + 1].to_broadcast([128, 512]), data1=TT, initial=init, op0=ALU.mult, op1=ALU.add),
                        reads=rd, writes=[tk])
                for comp, TT, tk in ((0, T1, ("t1",)), (1, T2, ("t2",))):
                    self.ew("act", lambda e, TT=TT, ct=ct, comp=comp: e.activation(out=STATE[:, ct, comp:comp + 1], in_=TT[:, 511:512], func=AF.Copy),
                            reads=[tk], writes=[("state", ct)])
                if not second:
                    continue
                xs = x_ring.next()
                pl(lambda e: e.tensor_tensor(out=BUR, in0=T1, in1=COSN, op=ALU.mult), [("t1",), ("cosn",)], [("bur",)])
                pl(lambda e: e.tensor_tensor(out=BUI, in0=T2, in1=SINN, op=ALU.mult), [("t2",), ("sinn",)], [("bui",)])
                pl(lambda e, xs=xs: e.tensor_tensor(out=r_(XR[:, xs, :]), in0=BUR, in1=BUI, op=ALU.subtract), [("bur",), ("bui",)], [("xr", xs)])
                self.ew("dve", lambda e: e.tensor_tensor(out=BUR, in0=T1, in1=SINN, op=ALU.mult), reads=[("t1",), ("sinn",), ("xr", xs)], writes=[("bur",)])
                self.ew("dve", lambda e: e.tensor_tensor(out=BUI, in0=T2, in1=COSN, op=ALU.mult), reads=[("t2",), ("cosn",), ("xr", xs)], writes=[("bui",)])
                self.ew("dve", lambda e, xs=xs: e.scalar_tensor_tensor(out=r_(XIN[:, xs, :]), in0=BUR, scalar=-1.0, in1=BUI, op0=ALU.mult, op1=ALU.subtract),
                        reads=[("bur",), ("bui",)], writes=[("xin", xs)])
                yb = 4 + tb
                q = ct % 4
                self.P.op("pe", lambda e, yb=yb, cs=cs, xs=xs, q=q: e.matmul(PS[yb][:, :], lhsT=r_(CWt[:, cs, 0:128]), rhs=r_(XR[:, xs, :]),
                                                                         start=(q == 0), stop=False),
                          reads=[("cwt", cs), ("xr", xs)], writes=[("ps", yb)])
                self.P.op("pe", lambda e, yb=yb, cs=cs, xs=xs, q=q: e.matmul(PS[yb][:, :], lhsT=r_(CWt[:, cs, 128:256]), rhs=r_(XIN[:, xs, :]),
                                                                         start=False, stop=(q == 3)),
                          reads=[("cwt", cs), ("xin", xs)], writes=[("ps", yb)])
                if q == 3:
                    self.ew("dve", lambda e, yb=yb, c=c, blk=blk: e.scalar_tensor_tensor(out=PS[yb][:, :], in0=HT[:, c, blk], scalar=SCST[:, c:c + 1],
                                                                                     in1=PS[yb][:, :], op0=ALU.mult, op1=ALU.add),
                            reads=[("HT", c), ("acst",), ("ps", yb)], writes=[("ps", yb)])
                    self.ew("act", lambda e, yb=yb: e.activation(out=PS[6][:, :], in_=PS[yb][:, :], func=AF.Square),
                            reads=[("ps", yb)], writes=[("ps", 6)])
                    self.ew("dve", lambda e: e.tensor_scalar(out=PS[6][:, :], in0=PS[6][:, :], scalar1=0.044715, scalar2=1.0, op0=ALU.mult, op1=ALU.add),
                            reads=[("ps", 6)], writes=[("ps", 6)])
                    self.ew("dve", lambda e, yb=yb, tb=tb: e.tensor_copy(out=r_(YO[:, tb, :]), in_=PS[yb][:, :]),
                            reads=[("ps", yb)], writes=[("yo", tb)])
                    self.ew("dve", lambda e, tb=tb: e.tensor_tensor(out=PS[6][:, :], in0=PS[6][:, :], in1=YO[:, tb, :], op=ALU.mult),
                            reads=[("ps", 6), ("yo", tb)], writes=[("ps", 6)])
                    self.ew("act", lambda e: e.activation(out=PS[6][:, :], in_=PS[6][:, :], func=AF.Sigmoid, scale=1.5957691216057308),
                            reads=[("ps", 6)], writes=[("ps", 6)])
                    self.ew("dve", lambda e, tb=tb: e.tensor_tensor(out=r_(YO[:, tb, :]), in0=YO[:, tb, :], in1=PS[6][:, :], op=ALU.mult),
                            reads=[("ps", 6), ("yo", tb)], writes=[("yo", tb)])
                    self.dma(out=self.yd[c][:, blk], in_=YO[:, tb, :], reads=[("yo", tb)], writes=[("yd", c, tb)], cls=f"yo{tb}")

        for ct in range(64):
            tile_pass(ct, False)
        skeys = [("state", ct) for ct in range(64)]
        self.dma(out=self.c_ins, in_=STATE[:, :, :].rearrange("p a b -> p (a b)"), reads=skeys, writes=[("cin", "s")], cls="tst")
        self.allgather(self.c_ins, self.c_outs, reads=[("cin", "s")], writes=[("cout", "s")])
        self.dma(out=INIT[:, :, :].rearrange("p a b -> p (a b)"), in_=self.c_outs[0:128, :], reads=[("cout", "s")], writes=[("init",)], cls="tld")
        GATE = self.GATE
        self.ew("dve", lambda e: e.tensor_scalar(out=INIT[:, :, :], in0=INIT[:, :, :], scalar1=GATE[:, 0:1], scalar2=None, op0=ALU.mult),
                reads=[("init",), ("gate",)], writes=[("init",)])
        for ct in range(64):
            tile_pass(ct, True)
        P.barrier()
        for c in range(NCH):
            self.dma(out=r_(HT[:, c, 0:T]), in_=r_(self.yd[c]), reads=[("yd", c, 0), ("yd", c, 1)], writes=[("HT", c)], cls=f"xin{c}")
        htkeys = [("HT", k) for k in range(NCH)]
        SG = self.STG
        for m in range(NCH):
            banks = {}
            for half in range(2):
                chunk = m + NCH * half
                wsl = []
                for kh in range(2):
                    s_ = self.wu_ring.next()
                    self.dma(out=r_(WU[:, s_, :]), in_=r_(self.d_wglu[chunk][:, kh * 1024:(kh + 1) * 1024]),
                             reads=[], writes=[("wu", s_)], cls=f"wu{s_}")
                    wsl.append(s_)
                for th in range(2):
                    bank = half * 2 + th
                    banks[(half, th)] = bank
                    pairs = [(r_(WU[:, wsl[k // 8], (k % 8) * 128:(k % 8 + 1) * 128]), r_(HT[:, k, th * 512:(th + 1) * 512])) for k in range(NCH)]
                    self.mm_group(PS[bank][:, :], pairs, reads=[("wu", wsl[0]), ("wu", wsl[1])] + htkeys, writes=[("ps", bank)])
            for th in range(2):
                bv, bg = banks[(0, th)], banks[(1, th)]
                self.ew("act", lambda e, bg=bg, th=th, m=m: e.activation(out=SG[:, th, 0:512], in_=PS[bg][:, :], func=AF.Sigmoid,
                                                                       bias=SCST[:, 32 + m:33 + m]),
                        reads=[("ps", bg), ("acst",)], writes=[("sg", th)])
                self.ew("dve", lambda e, bv=bv, th=th, m=m: e.scalar_tensor_tensor(out=SG[:, th, 0:512], in0=PS[bv][:, :], scalar=SCST[:, 16 + m:17 + m],
                                                                               in1=SG[:, th, 0:512], op0=ALU.add, op1=ALU.mult),
                        reads=[("ps", bv), ("sg", th), ("acst",)], writes=[("sg", th)])
                self.ew("dve", lambda e, th=th, m=m: e.tensor_tensor(out=X[:, m, HX + th * 512:HX + (th + 1) * 512],
                                                                   in0=X[:, m, HX + th * 512:HX + (th + 1) * 512], in1=SG[:, th, 0:512], op=ALU.add),
                        reads=[("sg", th), ("X", m)], writes=[("X", m)])

    def ffn(self, l):
        X, HT, PS = self.X, self.HT, self.PS
        WU, WD, STG, CG, G, CW = self.WU, self.WD, self.STG, self.CG, self.G, self.CW
        self.dma(out=CW[:, :], in_=self.d_cw[l], reads=[], writes=[("cw",)], cls="c_cw")
        self.rmsnorm(DEPTH + l, HX - 2, [(0, 342), (342, 342), (684, 342)])
        GF = 2
        ngroups = NFP // GF
        htkeys = [("HT", k) for k in range(NCH)]

        def down(g, gs):
            for ch in range(2):
                slots = []
                for jj in range(GF):
                    j = g * GF + jj
                    s = self.wd_ring.next()
                    self.dma(out=r_(WD[:, s, :]), in_=r_(self.d_wd[l][j][:, ch * 1024:(ch + 1) * 1024]),
                             reads=[], writes=[("wd", s)], cls=f"wd{s}")
                    slots.append(s)
                for mm in range(8):
                    m = ch * 8 + mm
                    for th in range(2):
                        bank = 4 + self.dn_ring.next()
                        pairs = [(r_(WD[:, slots[jj], mm * 128:(mm + 1) * 128]),
                                  r_(G[:, gs * GF + jj, th * 512:(th + 1) * 512])) for jj in range(GF)]
                        self.mm_group(PS[bank][:, :], pairs,
                                      reads=[("wd", s) for s in slots] + [("g", gs * GF + jj) for jj in range(GF)],
                                      writes=[("ps", bank)])
                        self.ew("dve", lambda e, bank=bank, m=m, th=th: e.tensor_tensor(
                            out=X[:, m, HX + th * 512:HX + (th + 1) * 512],
                            in0=X[:, m, HX + th * 512:HX + (th + 1) * 512], in1=PS[bank][:, :], op=ALU.add),
                            reads=[("ps", bank), ("X", m)], writes=[("X", m)])

        for g in range(ngroups):
            gs = g % 2
            for jj in range(GF):
                j = g * GF + jj
                gi = gs * GF + jj
                for half in range(2):
                    chunk = j + NFP * half
                    wsl = []
                    for kh in range(2):
                        s = self.wu_ring.next()
                        self.dma(out=r_(WU[:, s, :]), in_=r_(self.d_wu[l][chunk][:, kh * 1024:(kh + 1) * 1024]),
                                 reads=[], writes=[("wu", s)], cls=f"wu{s}")
                        wsl.append(s)
                    for blk in range(3):
                        bank = self.up_ring.next()
                        pairs = [(r_(WU[:, wsl[k // 8], (k % 8) * 128:(k % 8 + 1) * 128]),
                                  r_(HT[:, k, blk * 342:(blk + 1) * 342])) for k in range(NCH)]
                        self.mm_group(PS[bank][:, 0:342], pairs,
                                      reads=[("wu", wsl[0]), ("wu", wsl[1])] + htkeys, writes=[("ps", bank)])
                        self.ew("act", lambda e, bank=bank, half=half, blk=blk: e.activation(
                            out=STG[:, half, blk * 342:(blk + 1) * 342], in_=PS[bank][:, 0:342], func=AF.Copy),
                            reads=[("ps", bank)], writes=[("stg", half, blk)])
                    eng = "dve" if half == 0 else "pool"
                    dst = r_(G[:, gi, :]) if half == 0 else CG[:, :]
                    dsrc = G[:, gi, :] if half == 0 else CG[:, :]
                    dkey = ("g", gi) if half == 0 else ("cg",)
                    cwb = chunk * 4
                    self.ew(eng, lambda e, dst=dst, half=half, cwb=cwb: e.tensor_scalar(
                        out=dst, in0=STG[:, half, 2:2 + T], scalar1=CW[:, cwb + 2:cwb + 3], scalar2=CW[:, cwb + 3:cwb + 4],
                        op0=ALU.mult, op1=ALU.add), reads=[("stg", half, 0), ("stg", half, 1), ("stg", half, 2), ("cw",)], writes=[dkey])
                    for tap in (1, 0):
                        if eng == "dve":
                            self.ew(eng, lambda e, dst=dst, dsrc=dsrc, half=half, cwb=cwb, tap=tap: e.scalar_tensor_tensor(
                                out=dst, in0=STG[:, half, tap:tap + T], scalar=CW[:, cwb + tap:cwb + tap + 1], in1=dsrc,
                                op0=ALU.mult, op1=ALU.add), reads=[("stg", half, 0), ("stg", half, 1), ("stg", half, 2), ("cw",), dkey], writes=[dkey])
                        else:
                            PT = self.RSTD[:, 0:T]
                            self.ew(eng, lambda e, half=half, cwb=cwb, tap=tap, PT=PT: e.tensor_scalar(
                                out=PT, in0=STG[:, half, tap:tap + T], scalar1=CW[:, cwb + tap:cwb + tap + 1], scalar2=self.CCOL[:, 0:1],
                                op0=ALU.mult, op1=ALU.add), reads=[("stg", half, 0), ("stg", half, 1), ("stg", half, 2), ("cw",), ("ccol",)], writes=[("rstd",)])
                            self.ew(eng, lambda e, dst=dst, dsrc=dsrc, PT=PT: e.tensor_tensor(
                                out=dst, in0=dsrc, in1=PT, op=ALU.add), reads=[("rstd",), dkey], writes=[dkey])
                self.ew("act", lambda e: e.activation(out=CG[:, :], in_=CG[:, :], func=AF.Silu),
                        reads=[("cg",)], writes=[("cg",)])
                self.ew("dve", lambda e, gi=gi: e.tensor_tensor(out=r_(G[:, gi, :]), in0=G[:, gi, :], in1=CG[:, :], op=ALU.mult),
                        reads=[("cg",), ("g", gi)], writes=[("g", gi)])
            if g >= 1:
                down(g - 1, (g - 1) % 2)
        down(ngroups - 1, (ngroups - 1) % 2)

    def build(self, ncores):
        nc = self.nc
        self.ncores = ncores
        self.declare_dram()
        P = self.P
        from contextlib import ExitStack
        with ExitStack() as es:
            sb = lambda name, shape: es.enter_context(nc.sbuf_tensor(name, shape, F32))
            self.X = sb("X", [128, NCH, XW])
            self.HT = sb("HT", [128, NCH, XW])
            self.RSTD = sb("RSTD", [128, XW])
            self.SQ = sb("SQ", [128, 2, 512])
            self.ACST = sb("ACST", [128, 128])
            self.SPRM = sb("SPRM", [128, 13, 64])
            self.NEGPI = sb("NEGPI", [128, 1])
            self.CCOL = sb("CCOL", [128, 4])
            self.STATE = sb("STATE", [128, 64, 2])
            self.INIT = sb("INIT", [128, 64, 2])
            self.ONES = sb("ONES", [128, 128])
            self.ONESF = sb("ONESF", [128, 128])
            self.GAIN = sb("GAIN", [128, 2 * DEPTH * NCH])
            self.GATE = sb("GATE", [128, 1])
            self.INVC = sb("INVC", [128, 64])
            self.TB = sb("TB", [128, NCH, 16])
            self.CW = sb("CW", [128, 2 * NFP * 4])
            self.PBS = sb("PBS", [128, 48])
            self.T16 = sb("T16", [128, 4, 16])
            self.WU = sb("WU", [128, 4, 1024])
            self.WD = sb("WD", [128, 4, 1024])
            self.STG = sb("STG", [128, 2, XW])
            self.CG = sb("CG", [128, T])
            self.G = sb("G", [128, 4, T])
            self.PW = self.WU[:, 0:4, :].rearrange("p (a b) n -> p a (b n)", a=2)
            self.PL = self.G
            self.TA = self.STG
            self.PS = [es.enter_context(nc.psum_tensor(f"ps{i}", [128, 512], F32)) for i in range(8)]
            self.TBF = None
            self.TMP = self.CG[:, :].rearrange("p (a n) -> p a n", a=2)
            self.sq_ring = Ring(2)
            self.wu_ring = Ring(4)
            self.wd_ring = Ring(4)
            self.up_ring = Ring(4)
            self.dn_ring = Ring(3)
            self.pw_ring = Ring(2)
            self.tmp_ring = Ring(2)

            xkeys = [("X", c) for c in range(NCH)]
            for c in range(NCH):
                self.dma(out=self.X[:, c, :], in_=self.d_x[:, c, :], reads=[], writes=[("X", c)], cls=f"xin{c}")
            self.dma(out=self.GAIN[:, :], in_=self.d_gain, reads=[], writes=[("gain",)], cls="c_gain")
            self.dma(out=self.GATE[:, :], in_=self.d_gate, reads=[], writes=[("gate",)], cls="c_gate")
            self.dma(out=self.INVC[:, :], in_=self.d_invc, reads=[], writes=[("invc",)], cls="c_invc")
            self.ew("pool", lambda e: e.memset(self.ONESF[:, :], 1.0), reads=[], writes=[("onesf",)])
            self.ew("pool", lambda e: e.memset(self.CCOL[:, 0:1], 0.0), reads=[], writes=[("ccol",)])
            self.ew("pool", lambda e: e.memset(self.CCOL[:, 1:2], 12582912.0), reads=[("ccol",)], writes=[("ccol",)])
            self.ew("pool", lambda e: e.memset(self.CCOL[:, 2:3], 0.25), reads=[("ccol",)], writes=[("ccol",)])
            self.ew("pool", lambda e: e.memset(self.CCOL[:, 3:4], 1.0), reads=[("ccol",)], writes=[("ccol",)])
            self.ew("pool", lambda e: e.memset(self.NEGPI[:, :], 0.0), reads=[], writes=[("negpi",)])
            self.ew("dve", lambda e: e.tensor_copy(out=r_(self.ONES[:, :]), in_=self.ONESF[:, :]),
                    reads=[("onesf",)], writes=[("ones",)])
            for l in self.layers:
                kind = l % 3
                if kind == 0:
                    if l != 0:
                        if l != self.first:
                            self.exchange_tail(16)
                    P.barrier()
                    self.pool_mixer(l)
                elif kind == 1:
                    P.barrier()
                    self.attention(l)
                else:
                    P.barrier()
                    self.s5(l)
                P.barrier()
                self.exchange_tail(2)
                self.ffn(l)
                P.barrier()
            for c in range(NCH):
                self.dma(out=self.d_out[:, c, :], in_=self.X[:, c, HX:XW], reads=[("X", c)], writes=[("out", c)], cls=f"xin{c}")

            clss = P.classes()
            names = list(Prog.ENGINES[:4]) + clss
            sems = {}
            for n in names:
                sems[n] = es.enter_context(nc.semaphore("s_" + n))
            block = es.enter_context(nc.Block())
            P.emit(block, sems)
        return nc


def _feat_major(v):
    n = v.shape[-1] // 128
    return np.ascontiguousarray(v.reshape(n, 128).T)


def prep_shared(inp, layers):
    sh = {}
    gains = np.concatenate([inp["norm_mix_g"], inp["norm_ffn_g"]], axis=0)
    sh["gains"] = np.ascontiguousarray(gains.reshape(2 * DEPTH, NCH, 128).transpose(2, 0, 1).reshape(128, -1))
    for l in layers:
        wu = inp["ffn_w_up"][l]
        sh[f"wu{l}"] = np.ascontiguousarray(wu.reshape(NCH, 128, 2 * NFP, 128).transpose(2, 1, 0, 3).reshape(2 * NFP, 128, NCH * 128))
        sh[f"wd{l}"] = np.ascontiguousarray(inp["ffn_w_down"][l].reshape(NFP, 128, D))
        cw = np.concatenate([inp["ffn_conv_w"][l], inp["ffn_conv_b"][l][None]], axis=0)
        sh[f"cw{l}"] = np.ascontiguousarray(cw.reshape(4, 2 * NFP, 128).transpose(2, 1, 0).reshape(128, -1))
        if l % 3 == 2:
            j = l // 3
            ch = lambda v: np.ascontiguousarray(v.reshape(64, 128).T)
            ls = np.repeat(inp["ssm_log_step"][j][:, None], 64, axis=1)
            sh[f"sprm{l}"] = np.ascontiguousarray(np.concatenate([ch(inp["ssm_lam_re"][j]), ch(inp["ssm_lam_im"][j]), ch(ls)], axis=1))
            sbt = np.zeros((64, 128, 256), np.float32)
            sct = np.zeros((64, 128, 256), np.float32)
            for ct in range(64):
                q = ct % 4
                for gg in range(2):
                    g = 2 * ct + gg
                    rows = slice(q * 32 + gg * 16, q * 32 + gg * 16 + 16)
                    cols = slice(gg * 64, gg * 64 + 64)
                    sbt[ct, rows, 0:128][:, cols] = inp["ssm_b_re"][j][g].T
                    sbt[ct, rows, 128:256][:, cols] = inp["ssm_b_im"][j][g].T
                    sct[ct, cols, 0:128][:, rows] = inp["ssm_c_re"][j][g].T
                    sct[ct, cols, 128:256][:, rows] = inp["ssm_c_im"][j][g].T
            sh[f"sbt{l}"] = sbt
            sh[f"sct{l}"] = sct
            sh[f"scst{l}"] = np.ascontiguousarray(np.concatenate([_feat_major(inp["ssm_d"][j]), _feat_major(inp["ssm_b_glu"][j])], axis=1))
            sh[f"wglu{l}"] = np.ascontiguousarray(inp["ssm_w_glu"][j].reshape(NCH, 128, 32, 128).transpose(2, 1, 0, 3).reshape(32, 128, NCH * 128))
        if l % 3 == 1:
            j = l // 3
            wqkv = inp["sb_w_qkv"][j]
            sh[f"wqk{l}"] = np.ascontiguousarray(wqkv[:, 0:4096].reshape(NCH, 128, 32, 128).transpose(2, 1, 0, 3).reshape(32, 128, NCH * 128))
            sh[f"wv{l}"] = np.ascontiguousarray(wqkv[:, 4096:6144])
            sh[f"wo{l}"] = np.ascontiguousarray(inp["sb_w_o"][j].reshape(NCH, 128, NCH, 128).transpose(2, 1, 0, 3).reshape(NCH, 128, NCH * 128))
            acst = np.zeros((128, 18), np.float32)
            acst[:, 0:16] = np.arange(128, dtype=np.float32)[:, None] + 128.0 * np.arange(16, dtype=np.float32)[None, :]
            acst[:, 16] = inp["sb_q_gain"][j]
            acst[:, 17] = inp["sb_k_gain"][j]
            sh[f"acst{l}"] = acst
            sh["tri"] = np.ascontiguousarray(np.tril(np.ones((128, 128), np.float32)))
        if l % 3 == 0:
            j = l // 3
            pw = inp["pool_w"][j]
            sh[f"pw{l}"] = np.ascontiguousarray(pw.reshape(4, 4, 128, 512).transpose(0, 2, 1, 3).reshape(4, 128, 2048))
            sh[f"pbs{l}"] = np.ascontiguousarray(np.concatenate([_feat_major(inp["pool_b"][j]), _feat_major(inp["pool_scale"][j])], axis=1))
    return sh


def prep_core(xT_full_b, h):
    m = {}
    xs = np.zeros((D, XW), np.float32)
    lo = h * T
    xs[:, HX:] = xT_full_b[:, lo:lo + T]
    if h == 1:
        xs[:, :HX] = xT_full_b[:, lo - HX:lo]
    m["xT"] = np.ascontiguousarray(xs.reshape(NCH, 128, XW).transpose(1, 0, 2))
    m["gate"] = np.full((128, 1), float(h), np.float32)
    invc = np.zeros((4, 16), np.float32)
    for g in range(4):
        w = 2 << g
        for i in range(16):
            t = lo + i
            invc[g, i] = 1.0 / min(t + 1, w)
    m["invc"] = np.ascontiguousarray(np.broadcast_to(invc.reshape(1, 64), (128, 64)))
    m["tpos"] = np.ascontiguousarray(np.broadcast_to((lo + np.arange(T, dtype=np.float32))[None, :], (128, T)))
    return m


_NC_CACHE = {}


def run_layers(xT_all, inp, layers, nb=BATCH):
    ncores = 2 * nb
    key = (tuple(layers), ncores)
    if key not in _NC_CACHE:
        nc = bass.Bass("TRN2", target_bir_lowering=False)
        nc.dge_precook = False
        bld = Builder(nc, list(layers), layers[0], layers[-1])
        bld.build(ncores)
        _NC_CACHE[key] = (nc, list(bld.in_names))
    nc, in_names = _NC_CACHE[key]
    sh = prep_shared(inp, layers)
    in_maps = []
    for b in range(nb):
        for h in range(2):
            m = dict(sh)
            m.update(prep_core(xT_all[b], h))
            in_maps.append({k: m[k] for k in in_names})
    import time as _t, sys as _s
    _t0 = _t.time()
    res = run_bass_kernel_spmd(nc, in_maps, core_ids=list(range(ncores)))
    print("[kernel] launch layers=%s took %.1fs" % (list(layers), _t.time() - _t0), file=_s.stderr)
    out = np.zeros_like(xT_all)
    for b in range(nb):
        for h in range(2):
            o = res.results[2 * b + h]["outT"]
            out[b][:, h * T:(h + 1) * T] = o.transpose(1, 0, 2).reshape(D, T)
    return out


LAUNCH_GROUPS = [[0, 1, 2, 3]]


def kernel(**inputs):
    inp = {k: np.asarray(v) for k, v in inputs.items()}
    x = inp["x"]
    xT = np.ascontiguousarray(x.transpose(0, 2, 1))
    for grp in LAUNCH_GROUPS:
        xT = run_layers(xT, inp, grp)
    return np.ascontiguousarray(xT.transpose(0, 2, 1)).astype(np.float32)
```

```python
import numpy as np
import concourse.bass as bass
import concourse.mybir as mybir
from concourse.bass_utils import run_bass_kernel_spmd

F32 = mybir.dt.float32
F32R = mybir.dt.float32r
AF = mybir.ActivationFunctionType
ALU = mybir.AluOpType

D = 2048
NCH = 16
SEQ = 2048
BATCH = 4
T = 1024
HX = 16
XW = HX + T
DFF = 5632
NFP = 44
EPS = 1e-6
DEPTH = 4
SAME_ENGINE_SYNC = True
S5_ROT_ENGINE = "dve"
SYNC_ENGINES = ("act", "dve", "pool")


def r_(ap):
    return ap.bitcast(F32R)


class Op:
    __slots__ = ("eng", "fn", "deps", "cls", "inc", "count", "needed")

    def __init__(self, eng, fn, deps, cls, inc):
        self.eng = eng
        self.fn = fn
        self.deps = deps
        self.cls = cls
        self.inc = inc
        self.count = 0
        self.needed = False


class Ring:
    def __init__(self, n):
        self.n = n
        self.i = -1

    def next(self):
        self.i = (self.i + 1) % self.n
        return self.i


class Prog:
    ENGINES = ("pe", "act", "dve", "pool", "sp")

    def __init__(self, nc):
        self.nc = nc
        self.ops = []
        self.last_w = {}
        self.readers = {}

    def _deps(self, reads, writes, me):
        deps = set()
        for k in reads:
            w = self.last_w.get(k)
            if w is not None:
                deps.add(w)
        for k in writes:
            w = self.last_w.get(k)
            if w is not None:
                deps.add(w)
            deps.update(self.readers.get(k, ()))
        for k in reads:
            self.readers.setdefault(k, []).append(me)
        for k in writes:
            self.last_w[k] = me
            self.readers[k] = []
        deps.discard(me)
        return deps

    def op(self, eng, fn, reads=(), writes=(), cls=None, inc=16):
        me = len(self.ops)
        deps = self._deps(reads, writes, me)
        self.ops.append(Op(eng, fn, deps, cls, inc))
        return me

    def barrier(self):
        n = len(self.ops)
        last = {}
        for i in range(n - 1, -1, -1):
            o = self.ops[i]
            key = o.cls if o.cls is not None else o.eng
            if key not in last:
                last[key] = i
        deps = set(last.values())
        for e in self.ENGINES:
            self.ops.append(Op(e, None, set(deps), None, 0))

    def emit(self, block, sems):
        ops = self.ops
        self.barrier()
        for o in ops:
            for d in o.deps:
                ops[d].needed = True
        cnt = {}
        for o in ops:
            if o.cls is not None:
                cnt[o.cls] = cnt.get(o.cls, 0) + o.inc
                o.count = cnt[o.cls]
            elif o.needed and o.fn is not None:
                cnt[o.eng] = cnt.get(o.eng, 0) + 1
                o.count = cnt[o.eng]
            elif o.fn is None:
                o.count = -1
        def semkey(o):
            return o.cls if o.cls is not None else o.eng

        streams = {e: [] for e in self.ENGINES}
        for i, o in enumerate(ops):
            streams[o.eng].append(i)

        def make(engname):
            idxs = streams[engname]

            def body(e):
                seen = {}
                for i in idxs:
                    o = ops[i]
                    waits = {}
                    for d in o.deps:
                        od = ops[d]
                        if od.fn is None:
                            continue
                        k = semkey(od)
                        if od.cls is None and od.eng == engname and not SAME_ENGINE_SYNC:
                            continue
                        if od.cls is None and od.eng == engname and engname not in SYNC_ENGINES:
                            continue
                        if od.count > waits.get(k, 0):
                            waits[k] = od.count
                    for k, v in waits.items():
                        if v > seen.get(k, 0):
                            e.wait_ge(sems[k], v)
                            seen[k] = v
                    if o.fn is None:
                        continue
                    ins = o.fn(e)
                    if o.cls is not None:
                        ins.then_inc(sems[o.cls], o.inc)
                    elif o.needed:
                        ins.then_inc(sems[o.eng], 1)

            return body

        block.tensor(make("pe"))
        block.scalar(make("act"))
        block.vector(make("dve"))
        block.gpsimd(make("pool"))
        block.sync(make("sp"))

    def classes(self):
        s = set()
        for o in self.ops:
            if o.cls is not None:
                s.add(o.cls)
        return sorted(s)


class Builder:
    def __init__(self, nc, layers, first, last):
        self.nc = nc
        self.layers = layers
        self.first = first
        self.last = last
        self.P = Prog(nc)

    def declare_dram(self):
        nc = self.nc
        self.in_names = []

        def dt(name, shape, kind="ExternalInput"):
            if kind == "ExternalInput":
                self.in_names.append(name)
            return nc.dram_tensor(name, shape, F32, kind=kind).ap()
        self.d_x = dt("xT", [128, NCH, XW])
        self.d_out = dt("outT", [128, NCH, T], "ExternalOutput")
        self.d_gain = dt("gains", [128, 2 * DEPTH * NCH])
        self.d_gate = dt("gate", [128, 1])
        self.d_invc = dt("invc", [128, 4 * 16])
        self.d_wu = {}
        self.d_wd = {}
        self.d_cw = {}
        self.d_pw = {}
        self.d_pbs = {}
        for l in self.layers:
            self.d_wu[l] = dt(f"wu{l}", [2 * NFP, 128, NCH * 128])
            self.d_wd[l] = dt(f"wd{l}", [NFP, 128, D])
            self.d_cw[l] = dt(f"cw{l}", [128, 2 * NFP * 4])
            if l % 3 == 0:
                self.d_pw[l] = dt(f"pw{l}", [4, 128, 4 * 512])
                self.d_pbs[l] = dt(f"pbs{l}", [128, 2 * NCH])
            if l % 3 == 2:
                self.d_sprm = dt(f"sprm{l}", [128, 3 * 64])
                self.d_sbt = dt(f"sbt{l}", [64, 128, 256])
                self.d_sct = dt(f"sct{l}", [64, 128, 256])
                self.d_scst = dt(f"scst{l}", [128, 16 + 32])
                self.d_wglu = dt(f"wglu{l}", [32, 128, NCH * 128])
                if not hasattr(self, "d_tpos"):
                    self.d_tpos = dt("tpos", [128, T])
                self.yd = nc.dram_tensor("yd", [NCH, 128, T], F32).ap()
                self.c_ins = nc.dram_tensor("c_ins", [128, 128], F32).ap()
                self.c_outs = nc.dram_tensor("c_outs", [256, 128], F32).ap()
            if l % 3 == 1:
                self.d_wqk = dt(f"wqk{l}", [32, 128, NCH * 128])
                self.d_wv = dt(f"wv{l}", [D, D])
                self.d_wo = dt(f"wo{l}", [NCH, 128, NCH * 128])
                self.d_tri = dt("tri", [128, 128])
                self.d_acst = dt(f"acst{l}", [128, 18])
                if not hasattr(self, "d_tpos"):
                    self.d_tpos = dt("tpos", [128, T])
                self.qT_d = nc.dram_tensor("qT_d", [NCH, 128, T], F32).ap()
                self.kin = nc.dram_tensor("kin", [D, T], F32).ap()
                self.kall = nc.dram_tensor("kall", [4, 1024, T], F32).ap()
                self.vin = nc.dram_tensor("vin", [T, D], F32).ap()
                self.vall = nc.dram_tensor("vall", [4, 512, D], F32).ap()
        self.c_in2 = nc.dram_tensor("c_in2", [128, NCH * 2], F32).ap()
        self.c_out2 = nc.dram_tensor("c_out2", [256, NCH * 2], F32).ap()
        self.c_in16 = nc.dram_tensor("c_in16", [128, NCH * 16], F32).ap()
        self.c_out16 = nc.dram_tensor("c_out16", [256, NCH * 16], F32).ap()

    def mm_group(self, ps_ap, pairs, reads, writes):
        def fn(e, ps_ap=ps_ap, pairs=pairs):
            n = len(pairs)
            ins = None
            for i, (l, r) in enumerate(pairs):
                ins = e.matmul(ps_ap, lhsT=l, rhs=r, start=(i == 0), stop=(i == n - 1))
            return ins

        self.P.op("pe", fn, reads, writes)

    def dma(self, out, in_, reads, writes, cls, eng="sp"):
        self.P.op(eng, lambda e, out=out, in_=in_: e.dma_start(out=out, in_=in_), reads, writes, cls=cls, inc=16)

    def allgather(self, cin, cout, reads, writes):
        pairs = [[0, 1], [2, 3], [4, 5], [6, 7]][: self.ncores // 2]

        def fn(e, cin=cin, cout=cout):
            return e.collective_compute("AllGather", ALU.bypass, replica_groups=pairs, ins=[cin], outs=[cout])

        self.P.op("pool", fn, reads, writes, cls="cc", inc=1)

    def ew(self, eng, fn, reads, writes):
        self.P.op(eng, fn, reads, writes)

    def rmsnorm(self, gidx, xc0, blocks):
        X, HT, SQ, RSTD, ONES, GAIN = self.X, self.HT, self.SQ, self.RSTD, self.ONES, self.GAIN
        PS = self.PS
        bank = 7
        for (b0, w) in blocks:
            for c in range(NCH):
                s = self.sq_ring.next()
                self.ew("act", lambda e, s=s, c=c, b0=b0, w=w: e.activation(
                    out=r_(SQ[:, s, 0:w]), in_=X[:, c, xc0 + b0:xc0 + b0 + w], func=AF.Square),
                    reads=[("X", c)], writes=[("sq", s)])
                self.P.op("pe", lambda e, s=s, c=c, w=w: e.matmul(
                    PS[bank][:, 0:w], lhsT=r_(ONES[:, :]), rhs=r_(SQ[:, s, 0:w]),
                    start=(c == 0), stop=(c == NCH - 1)),
                    reads=[("sq", s), ("ones",)], writes=[("ps", bank)])
            self.ew("dve", lambda e, b0=b0, w=w: e.tensor_scalar(
                out=RSTD[:, b0:b0 + w], in0=PS[bank][:, 0:w], scalar1=1.0 / D, scalar2=float(EPS),
                op0=ALU.mult, op1=ALU.add),
                reads=[("ps", bank)], writes=[("rstd",)])
            self.ew("act", lambda e, b0=b0, w=w: e.activation(
                out=RSTD[:, b0:b0 + w], in_=RSTD[:, b0:b0 + w], func=AF.Sqrt),
                reads=[("rstd",)], writes=[("rstd",)])
            self.ew("dve", lambda e, b0=b0, w=w: e.reciprocal(
                out=RSTD[:, b0:b0 + w], in_=RSTD[:, b0:b0 + w]),
                reads=[("rstd",)], writes=[("rstd",)])
            for c in range(NCH):
                if c % 2 == 0:
                    self.ew("dve", lambda e, c=c, b0=b0, w=w: e.scalar_tensor_tensor(
                        out=r_(HT[:, c, b0:b0 + w]), in0=X[:, c, xc0 + b0:xc0 + b0 + w],
                        scalar=GAIN[:, gidx * NCH + c:gidx * NCH + c + 1], in1=RSTD[:, b0:b0 + w],
                        op0=ALU.mult, op1=ALU.mult),
                        reads=[("X", c), ("rstd",), ("gain",)], writes=[("HT", c)])
                else:
                    self.ew("pool", lambda e, c=c, b0=b0, w=w: e.tensor_scalar(
                        out=r_(HT[:, c, b0:b0 + w]), in0=X[:, c, xc0 + b0:xc0 + b0 + w],
                        scalar1=GAIN[:, gidx * NCH + c:gidx * NCH + c + 1], scalar2=self.CCOL[:, 0:1], op0=ALU.mult, op1=ALU.add),
                        reads=[("X", c), ("gain",), ("ccol",)], writes=[("HT", c)])
                    self.ew("pool", lambda e, c=c, b0=b0, w=w: e.tensor_tensor(
                        out=r_(HT[:, c, b0:b0 + w]), in0=HT[:, c, b0:b0 + w], in1=RSTD[:, b0:b0 + w], op=ALU.mult),
                        reads=[("HT", c), ("rstd",)], writes=[("HT", c)])

    def exchange_tail(self, ncols):
        X, TB = self.X, self.TB
        cin = self.c_in2 if ncols == 2 else self.c_in16
        cout = self.c_out2 if ncols == 2 else self.c_out16
        tag = f"t{ncols}"
        xkeys = [("X", c) for c in range(NCH)]
        self.dma(out=cin.rearrange("p (c n) -> p c n", n=ncols), in_=X[:, :, XW - ncols:XW],
                 reads=xkeys, writes=[("cin", tag)], cls="tst")
        self.allgather(cin, cout, reads=[("cin", tag)], writes=[("cout", tag)])
        self.dma(out=TB[:, :, 0:ncols], in_=cout[0:128, :].rearrange("p (c n) -> p c n", n=ncols),
                 reads=[("cout", tag)], writes=[("TB",)], cls="tld")
        GATE = self.GATE
        self.ew("dve", lambda e: e.tensor_scalar(
            out=X[:, :, HX - ncols:HX], in0=TB[:, :, 0:ncols], scalar1=GATE[:, 0:1], scalar2=None,
            op0=ALU.mult), reads=[("TB",), ("gate",)], writes=xkeys)

    def pool_mixer(self, l):
        P = self.P
        X, HT, PS = self.X, self.HT, self.PS
        PW, PL, TA, TBF, T16, TMP, PBS, INVC = self.PW, self.PL, self.TA, self.TBF, self.T16, self.TMP, self.PBS, self.INVC
        self.dma(out=PBS[:, 0:2 * NCH], in_=self.d_pbs[l], reads=[], writes=[("pbs",)], cls="c_pbs")
        self.ew("dve", lambda e: e.tensor_tensor(out=PBS[:, 0:NCH], in0=PBS[:, 0:NCH], in1=PBS[:, NCH:2 * NCH], op=ALU.mult),
                reads=[("pbs",)], writes=[("pbs",)])
        self.rmsnorm(l, 0, [(0, 346), (346, 346), (692, 348)])
        for g in range(4):
            w = 2 << g
            ws = self.pw_ring.next()
            self.dma(out=r_(PW[:, ws, :]), in_=r_(self.d_pw[l][g]), reads=[], writes=[("pw", ws)], cls=f"pw{ws}")
            for kk in range(4):
                c = 4 * g + kk
                eng = "dve" if kk % 2 == 0 else "pool"
                src = HT[:, c, :]
                eng = "dve"
                bufs = [TA[:, 0, :], TA[:, 1, :]]
                bkeys = [("ta", 0), ("ta", 1)]
                skey = ("HT", c)
                sh = 1
                lo = 0
                for st in range(g + 1):
                    dst = bufs[st % 2]
                    dkey = bkeys[st % 2]
                    lo2 = lo + sh
                    self.ew(eng, lambda e, dst=dst, src=src, lo2=lo2, sh=sh: e.tensor_tensor(
                        out=dst[:, lo2:XW], in0=src[:, lo2:XW], in1=src[:, lo2 - sh:XW - sh], op=ALU.add),
                        reads=[skey], writes=[dkey])
                    src, skey = dst, dkey
                    lo = lo2
                    sh *= 2
                self.ew(eng, lambda e, src=src, kk=kk, c=c, w=w: e.scalar_tensor_tensor(
                    out=r_(PL[:, kk, :]), in0=src[:, HX:XW], scalar=1.0 / w, in1=HT[:, c, HX:XW],
                    op0=ALU.mult, op1=ALU.subtract),
                    reads=[skey, ("HT", c)], writes=[("pl", kk)])
                self.ew(eng, lambda e, src=src, kk=kk, g=g: e.tensor_tensor(
                    out=T16[:, kk, :], in0=src[:, HX:HX + 16], in1=INVC[:, g * 16:(g + 1) * 16], op=ALU.mult),
                    reads=[skey, ("invc",)], writes=[("t16", kk)])
                self.ew(eng, lambda e, kk=kk, c=c: e.tensor_tensor(
                    out=r_(PL[:, kk, 0:16]), in0=T16[:, kk, :], in1=HT[:, c, HX:HX + 16], op=ALU.subtract),
                    reads=[("t16", kk), ("HT", c)], writes=[("pl", kk)])
            for mm in range(4):
                m = 4 * g + mm
                for th in range(2):
                    bank = 4 + self.dn_ring.next()
                    pairs = [(r_(PW[:, ws, kk * 512 + mm * 128:kk * 512 + (mm + 1) * 128]),
                              r_(PL[:, kk, th * 512:(th + 1) * 512])) for kk in range(4)]
                    self.mm_group(PS[bank][:, :], pairs,
                                  reads=[("pw", ws)] + [("pl", kk) for kk in range(4)], writes=[("ps", bank)])
                    ts = self.tmp_ring.next()
                    self.ew("act", lambda e, bank=bank, ts=ts, m=m: e.activation(
                        out=TMP[:, ts, :], in_=PS[bank][:, :], func=AF.Identity,
                        bias=PBS[:, m:m + 1], scale=PBS[:, NCH + m:NCH + m + 1]),
                        reads=[("ps", bank), ("pbs",)], writes=[("tmp", ts)])
                    self.ew("dve", lambda e, ts=ts, m=m, th=th: e.tensor_tensor(
                        out=X[:, m, HX + th * 512:HX + (th + 1) * 512], in0=X[:, m, HX + th * 512:HX + (th + 1) * 512],
                        in1=TMP[:, ts, :], op=ALU.add),
                        reads=[("tmp", ts), ("X", m)], writes=[("X", m)])

    def attention(self, l):
        P = self.P
        X, HT, PS, WU, SQ, RSTD, ONES, ONESF = self.X, self.HT, self.PS, self.WU, self.SQ, self.RSTD, self.ONES, self.ONESF
        TRI = self.ACST[:, 0:128]
        KPOS = self.PBS[:, 0:16]
        QKG = self.PBS[:, 16:18]
        htkeys = [("HT", k) for k in range(NCH)]
        self.dma(out=r_(TRI), in_=r_(self.d_tri), reads=[], writes=[("tri",)], cls="c_tri")
        self.dma(out=self.PBS[:, 0:18], in_=self.d_acst, reads=[], writes=[("acst",)], cls="c_pbs")
        self.ew("dve", lambda e: e.tensor_scalar(out=QKG[:, 0:1], in0=QKG[:, 0:1], scalar1=float(128 ** -0.5), scalar2=None,
                                                  op0=ALU.mult), reads=[("acst",)], writes=[("acst",)])
        self.rmsnorm(l, HX, [(0, 512), (512, 512)])
        QKS = self.STG
        VST = self.CG[:, :].rearrange("p (a n) -> p a n", a=2)
        for oc in range(32):
            qk = oc // 16
            hh = oc % 16
            wsl = []
            for kh in range(2):
                s_ = self.wu_ring.next()
                self.dma(out=r_(WU[:, s_, :]), in_=r_(self.d_wqk[oc][:, kh * 1024:(kh + 1) * 1024]),
                         reads=[], writes=[("wu", s_)], cls=f"wu{s_}")
                wsl.append(s_)
            st = oc % 2
            for th in range(2):
                bank = self.up_ring.next()
                pairs = [(r_(WU[:, wsl[k // 8], (k % 8) * 128:(k % 8 + 1) * 128]),
                          r_(HT[:, k, th * 512:(th + 1) * 512])) for k in range(NCH)]
                self.mm_group(PS[bank][:, :], pairs, reads=[("wu", wsl[0]), ("wu", wsl[1])] + htkeys, writes=[("ps", bank)])
                sq = self.sq_ring.next()
                self.ew("act", lambda e, bank=bank, sq=sq: e.activation(out=r_(SQ[:, sq, :]), in_=PS[bank][:, :], func=AF.Square),
                        reads=[("ps", bank)], writes=[("sq", sq)])
                self.P.op("pe", lambda e, sq=sq: e.matmul(PS[7][:, :], lhsT=r_(ONES[:, :]), rhs=r_(SQ[:, sq, :]), start=True, stop=True),
                          reads=[("sq", sq), ("ones",)], writes=[("ps", 7)])
                self.ew("dve", lambda e: e.tensor_scalar(out=RSTD[:, 0:512], in0=PS[7][:, :], scalar1=1.0 / 128, scalar2=float(EPS),
                                                          op0=ALU.mult, op1=ALU.add), reads=[("ps", 7)], writes=[("rstd",)])
                self.ew("act", lambda e: e.activation(out=RSTD[:, 0:512], in_=RSTD[:, 0:512], func=AF.Sqrt),
                        reads=[("rstd",)], writes=[("rstd",)])
                self.ew("dve", lambda e: e.reciprocal(out=RSTD[:, 0:512], in_=RSTD[:, 0:512]), reads=[("rstd",)], writes=[("rstd",)])
                self.ew("dve", lambda e, bank=bank, st=st, th=th, qk=qk: e.scalar_tensor_tensor(
                    out=QKS[:, st, th * 512:(th + 1) * 512], in0=PS[bank][:, :], scalar=QKG[:, qk:qk + 1], in1=RSTD[:, 0:512],
                    op0=ALU.mult, op1=ALU.mult), reads=[("ps", bank), ("rstd",), ("acst",)], writes=[("qks", st)])
            dst = self.qT_d[hh] if qk == 0 else self.kin[hh * 128:(hh + 1) * 128, :]
            self.dma(out=dst, in_=QKS[:, st, 0:T], reads=[("qks", st)], writes=[("qkd", oc)], cls=f"qks{st}")
        for tg in range(2):
            for cb in range(4):
                for k in range(NCH):
                    s_ = self.wu_ring.next()
                    self.dma(out=r_(WU[:, s_, 0:512]), in_=r_(self.d_wv[k * 128:(k + 1) * 128, cb * 512:(cb + 1) * 512]),
                             reads=[], writes=[("wu", s_)], cls=f"wu{s_}")
                    for tb in range(4):
                        self.P.op("pe", lambda e, tb=tb, k=k, s_=s_, tg=tg: e.matmul(
                            PS[tb][:, :], lhsT=r_(HT[:, k, tg * 512 + tb * 128:tg * 512 + (tb + 1) * 128]), rhs=r_(WU[:, s_, 0:512]),
                            start=(k == 0), stop=(k == NCH - 1)), reads=[("wu", s_), ("HT", k)], writes=[("ps", tb)])
                for tb in range(4):
                    vs = self.tmp_ring.next()
                    self.ew("act", lambda e, tb=tb, vs=vs: e.activation(out=VST[:, vs, :], in_=PS[tb][:, :], func=AF.Copy),
                            reads=[("ps", tb)], writes=[("vst", vs)])
                    r0 = tg * 512 + tb * 128
                    self.dma(out=self.vin[r0:r0 + 128, cb * 512:(cb + 1) * 512], in_=VST[:, vs, :],
                             reads=[("vst", vs)], writes=[("vind", tg, cb, tb)], cls=f"vst{vs}")
        allk = [("qkd", oc) for oc in range(16, 32)]
        allv = [("vind", tg, cb, tb) for tg in range(2) for cb in range(4) for tb in range(4)]
        for g in range(4):
            self.allgather(self.kin[g * 512:(g + 1) * 512, :], self.kall[g], reads=allk, writes=[("kall", g)])
        for g in range(4):
            self.allgather(self.vin[g * 256:(g + 1) * 256, :], self.vall[g], reads=allv, writes=[("vall", g)])
        P.barrier()
        TPOS = RSTD[:, 0:T]
        self.dma(out=TPOS, in_=self.d_tpos, reads=[], writes=[("rstd",)], cls="c_tpos")
        HTf = HT[:, :, :].rearrange("p c n -> p (c n)")
        KT = HTf[:, 0:4096].rearrange("p (a n) -> p a n", a=2)
        VS = HTf[:, 4096:8192].rearrange("p (a k d) -> p a k d", a=2, k=16)
        OT = HTf[:, 8192:16384].rearrange("p (h n) -> p h n", h=16)
        Gf = self.G[:, :, :].rearrange("p a n -> p (a n)")
        SPM = Gf[:, 0:1024].rearrange("p (a n) -> p a n", a=2)
        ATT = Gf[:, 1024:2048].rearrange("p (a n) -> p a n", a=2)
        LSUM = Gf[:, 2048:2560]
        WDf = self.WD[:, :, :].rearrange("p a n -> p (a n)")
        QT = WDf[:, 0:1024].rearrange("p (a n) -> p a n", a=2)
        NQT = WDf[:, 1024:2048].rearrange("p (a n) -> p a n", a=2)
        STf = self.STG[:, :, :].rearrange("p a n -> p (a n)")
        EB = STf[:, 0:1024].rearrange("p (a n) -> p a n", a=2)
        AR = STf[:, 1024:2048].rearrange("p (a n) -> p a n", a=2)
        MASK = self.CG[:, :].rearrange("p (a n) -> p a n", a=2)
        z_ring, c_ring = Ring(2), Ring(2)
        for c in range(2):
            kbmax = 12 if c == 0 else 16
            tiles = []
            for h in range(16):
                for i, kb in enumerate(range(kbmax - 1, -1, -1)):
                    tiles.append(dict(h=h, kb=kb, i=i, last=(i == kbmax - 1)))
            n = len(tiles)

            def stage1(tl, tau):
                h, kb = tl["h"], tl["kb"]
                hs = h % 2
                if tl["i"] == 0:
                    for half in range(2):
                        r0 = half * 512 + (h % 4) * 128
                        self.dma(out=r_(KT[:, hs, half * T:(half + 1) * T]),
                                 in_=r_(self.kall[h // 4][r0:r0 + 128, :]),
                                 reads=[("kall", h // 4)], writes=[("kt", hs, half)], cls=f"kt{hs}{half}")
                    for half in range(2):
                        for g in range(4):
                            pc = half * 4 + g
                            if pc * 2 >= kbmax:
                                continue
                            self.dma(out=r_(VS[:, hs, pc * 2:pc * 2 + 2, :]),
                                     in_=r_(self.vall[g][half * 256:(half + 1) * 256, h * 128:(h + 1) * 128].rearrange("(k p) d -> p k d", p=128)),
                                     reads=[("vall", g)], writes=[("vs", hs, pc)], cls=f"vs{hs}{pc}")
                    self.dma(out=r_(QT[:, hs, :]), in_=r_(self.qT_d[h][:, c * 512:(c + 1) * 512]),
                             reads=[("qkd", h)], writes=[("qt", hs)], cls=f"qt{hs}")
                    self.ew("dve", lambda e, hs=hs: e.tensor_scalar(out=r_(NQT[:, hs, :]), in0=QT[:, hs, :], scalar1=-1.0, scalar2=None,
                                                                   op0=ALU.mult), reads=[("qt", hs)], writes=[("nqt", hs)])
                zb = z_ring.next()
                tl["zb"] = zb
                sl = tau % 2
                tl["sl"] = sl
                self.P.op("pe", lambda e, zb=zb, hs=hs, kb=kb: e.matmul(PS[zb][:, :], lhsT=r_(KT[:, hs, kb * 128:(kb + 1) * 128]),
                                                                     rhs=r_(QT[:, hs, :]), start=True, stop=True),
                          reads=[("kt", hs, kb // 8), ("qt", hs)], writes=[("ps", zb)])
                self.ew("act", lambda e, zb=zb, sl=sl: e.activation(out=EB[:, sl, :], in_=PS[zb][:, :], func=AF.Exp),
                        reads=[("ps", zb)], writes=[("eb", sl)])
                self.ew("act", lambda e, sl=sl: e.activation(out=EB[:, sl, :], in_=EB[:, sl, :], func=AF.Ln, bias=ONESF[:, 0:1]),
                        reads=[("eb", sl), ("onesf",)], writes=[("eb", sl)])
                self.ew("dve", lambda e, sl=sl, kb=kb, c=c: e.tensor_scalar(out=MASK[:, sl, :], in0=TPOS[:, c * 512:(c + 1) * 512],
                                                                     scalar1=KPOS[:, kb:kb + 1], scalar2=None, op0=ALU.is_gt),
                        reads=[("rstd",), ("acst",)], writes=[("mask", sl)])
                self.ew("pool", lambda e, sl=sl: e.tensor_tensor(out=r_(SPM[:, sl, :]), in0=EB[:, sl, :], in1=MASK[:, sl, :], op=ALU.mult),
                        reads=[("eb", sl), ("mask", sl)], writes=[("spm", sl)])

            def stage2(tl):
                h, kb, sl = tl["h"], tl["kb"], tl["sl"]
                hs = h % 2
                cb_ = 2 + c_ring.next()
                first = tl["i"] == 0

                def fn(e, cb_=cb_, sl=sl, hs=hs, kb=kb, first=first):
                    e.matmul(PS[cb_][:, :], lhsT=r_(TRI), rhs=r_(SPM[:, sl, :]), start=True, stop=False)
                    if not first:
                        e.matmul(PS[cb_][:, :], lhsT=r_(ONES[:, :]), rhs=r_(LSUM), start=False, stop=False)
                    return e.matmul(PS[cb_][:, :], lhsT=r_(KT[:, hs, kb * 128:(kb + 1) * 128]), rhs=r_(NQT[:, hs, :]),
                                    start=False, stop=True)

                self.P.op("pe", fn, reads=[("spm", sl), ("lsum",), ("kt", hs, kb // 8), ("nqt", hs), ("tri",), ("ones",)], writes=[("ps", cb_)])
                if first:
                    self.ew("pool", lambda e, sl=sl: e.tensor_copy(out=r_(LSUM), in_=SPM[:, sl, :]),
                            reads=[("spm", sl)], writes=[("lsum",)])
                else:
                    self.ew("pool", lambda e, sl=sl: e.tensor_tensor(out=r_(LSUM), in0=LSUM, in1=SPM[:, sl, :], op=ALU.add),
                            reads=[("spm", sl), ("lsum",)], writes=[("lsum",)])
                self.ew("act", lambda e, cb_=cb_, sl=sl: e.activation(out=AR[:, sl, :], in_=PS[cb_][:, :], func=AF.Exp, scale=-1.0),
                        reads=[("ps", cb_)], writes=[("ar", sl)])
                self.ew("dve", lambda e, sl=sl: e.tensor_tensor(out=r_(ATT[:, sl, :]), in0=AR[:, sl, :], in1=MASK[:, sl, :], op=ALU.mult),
                        reads=[("ar", sl), ("mask", sl)], writes=[("att", sl)])

            def stage3(tl):
                h, kb, sl = tl["h"], tl["kb"], tl["sl"]
                hs = h % 2
                ob = 4 + hs
                self.P.op("pe", lambda e, ob=ob, hs=hs, kb=kb, sl=sl, tl=tl: e.matmul(
                    PS[ob][:, :], lhsT=r_(VS[:, hs, kb, :]), rhs=r_(ATT[:, sl, :]), start=(tl["i"] == 0), stop=tl["last"]),
                    reads=[("vs", hs, kb // 2), ("att", sl)], writes=[("ps", ob)])
                if tl["last"]:
                    self.ew("act", lambda e, ob=ob, h=h: e.activation(out=r_(OT[:, h, :]), in_=PS[ob][:, :], func=AF.Copy),
                            reads=[("ps", ob)], writes=[("ot", h)])

            for tau in range(n + 2):
                if tau < n:
                    stage1(tiles[tau], tau)
                if 0 <= tau - 1 < n:
                    stage2(tiles[tau - 1])
                if 0 <= tau - 2 < n:
                    stage3(tiles[tau - 2])
            for m in range(NCH):
                wsl = []
                for kh in range(2):
                    s_ = self.wu_ring.next()
                    self.dma(out=r_(WU[:, s_, :]), in_=r_(self.d_wo[m][:, kh * 1024:(kh + 1) * 1024]),
                             reads=[], writes=[("wu", s_)], cls=f"wu{s_}")
                    wsl.append(s_)
                pairs = [(r_(WU[:, wsl[h // 8], (h % 8) * 128:(h % 8 + 1) * 128]), r_(OT[:, h, :])) for h in range(16)]
                self.mm_group(PS[6][:, :], pairs, reads=[("wu", wsl[0]), ("wu", wsl[1])] + [("ot", h) for h in range(16)],
                              writes=[("ps", 6)])
                self.ew("dve", lambda e, m=m, c=c: e.tensor_tensor(
                    out=X[:, m, HX + c * 512:HX + (c + 1) * 512], in0=X[:, m, HX + c * 512:HX + (c + 1) * 512],
                    in1=PS[6][:, :], op=ALU.add), reads=[("ps", 6), ("X", m)], writes=[("X", m)])

    def s5(self, l):
        P = self.P
        X, HT, PS, WU, RSTD, ONESF = self.X, self.HT, self.PS, self.WU, self.RSTD, self.ONESF
        SP = self.SPRM
        LR, LI, LS = SP[:, 0, :], SP[:, 1, :], SP[:, 2, :]
        STEP, MAG, U, FRE, FIM, NFIM = SP[:, 3, :], SP[:, 4, :], SP[:, 5, :], SP[:, 6, :], SP[:, 7, :], SP[:, 8, :]
        W1, W2, W3, W4 = SP[:, 9, :], SP[:, 10, :], SP[:, 11, :], SP[:, 12, :]
        NEGPI = self.NEGPI
        SCST = self.PBS
        STATE, INIT = self.STATE, self.INIT
        TWO_PI = float(2 * np.pi)
        RC = 12582912.0
        kp = [("sprm",)]
        self.dma(out=SP[:, 0:3, :], in_=self.d_sprm.rearrange("p (a n) -> p a n", a=3), reads=[], writes=kp, cls="c_sprm")
        self.dma(out=SCST[:, 0:48], in_=self.d_scst, reads=[], writes=[("acst",)], cls="c_pbs")
        d = lambda fn: self.ew("dve", fn, reads=kp + [("negpi",)], writes=kp)
        a_ = lambda fn: self.ew("act", fn, reads=kp + [("negpi",)], writes=kp)
        a_(lambda e: e.activation(out=STEP, in_=LS, func=AF.Exp))
        d(lambda e: e.tensor_tensor(out=W1, in0=LR, in1=STEP, op=ALU.mult))
        a_(lambda e: e.activation(out=MAG, in_=W1, func=AF.Exp))
        d(lambda e: e.tensor_tensor(out=U, in0=LI, in1=STEP, op=ALU.mult))
        d(lambda e: e.tensor_scalar(out=U, in0=U, scalar1=float(1.0 / (2 * np.pi)), scalar2=None, op0=ALU.mult))
        d(lambda e: e.tensor_scalar(out=W3, in0=U, scalar1=RC, scalar2=None, op0=ALU.add))
        d(lambda e: e.scalar_tensor_tensor(out=W1, in0=W3, scalar=RC, in1=U, op0=ALU.subtract, op1=ALU.subtract))
        d(lambda e: e.tensor_scalar(out=W4, in0=U, scalar1=0.25, scalar2=None, op0=ALU.add))
        d(lambda e: e.tensor_scalar(out=W3, in0=W4, scalar1=RC, scalar2=None, op0=ALU.add))
        d(lambda e: e.scalar_tensor_tensor(out=W2, in0=W3, scalar=RC, in1=W4, op0=ALU.subtract, op1=ALU.subtract))
        a_(lambda e: e.activation(out=W1, in_=W1, func=AF.Sin, scale=TWO_PI))
        a_(lambda e: e.activation(out=W2, in_=W2, func=AF.Sin, scale=TWO_PI))
        d(lambda e: e.scalar_tensor_tensor(out=W3, in0=W2, scalar=-1.0, in1=MAG, op0=ALU.mult, op1=ALU.mult))
        d(lambda e: e.scalar_tensor_tensor(out=W4, in0=W1, scalar=-1.0, in1=MAG, op0=ALU.mult, op1=ALU.mult))
        d(lambda e: e.tensor_scalar(out=W3, in0=W3, scalar1=-1.0, scalar2=None, op0=ALU.add))
        d(lambda e: e.tensor_tensor(out=W1, in0=LR, in1=LR, op=ALU.mult))
        d(lambda e: e.tensor_tensor(out=W2, in0=LI, in1=LI, op=ALU.mult))
        d(lambda e: e.tensor_tensor(out=W1, in0=W1, in1=W2, op=ALU.add))
        d(lambda e: e.reciprocal(out=W1, in_=W1))
        d(lambda e: e.tensor_tensor(out=FRE, in0=W3, in1=LR, op=ALU.mult))
        d(lambda e: e.tensor_tensor(out=W2, in0=W4, in1=LI, op=ALU.mult))
        d(lambda e: e.tensor_tensor(out=FRE, in0=FRE, in1=W2, op=ALU.add))
        d(lambda e: e.tensor_tensor(out=FRE, in0=FRE, in1=W1, op=ALU.mult))
        d(lambda e: e.tensor_tensor(out=FIM, in0=W4, in1=LR, op=ALU.mult))
        d(lambda e: e.tensor_tensor(out=W2, in0=W3, in1=LI, op=ALU.mult))
        d(lambda e: e.tensor_tensor(out=FIM, in0=FIM, in1=W2, op=ALU.subtract))
        d(lambda e: e.tensor_tensor(out=FIM, in0=FIM, in1=W1, op=ALU.mult))
        d(lambda e: e.tensor_scalar(out=NFIM, in0=FIM, scalar1=-1.0, scalar2=None, op0=ALU.mult))
        self.rmsnorm(l, HX, [(0, 512), (512, 512)])
        P.barrier()
        TG = RSTD[:, 0:T]
        self.dma(out=TG, in_=self.d_tpos, reads=[], writes=[("rstd",)], cls="c_tpos")
        STf = self.STG[:, :, :].rearrange("p a n -> p (a n)")
        COSN, SINN, BUR, BUI = STf[:, 0:512], STf[:, 512:1024], STf[:, 1024:1536], STf[:, 1536:2048]
        T1, T2 = self.CG[:, 0:512], self.CG[:, 512:1024]
        Gf = self.G[:, :, :].rearrange("p a n -> p (a n)")
        XR = Gf[:, 0:1024].rearrange("p (a n) -> p a n", a=2)
        XIN = Gf[:, 1024:2048].rearrange("p (a n) -> p a n", a=2)
        YO = Gf[:, 2048:3072].rearrange("p (a n) -> p a n", a=2)
        WDf = self.WD[:, :, :].rearrange("p a n -> p (a n)")
        BW = WDf[:, 0:512].rearrange("p (a n) -> p a n", a=2)
        CWt = WDf[:, 512:1024].rearrange("p (a n) -> p a n", a=2)
        bw_ring, cw_ring, x_ring, b_ring = Ring(2), Ring(2), Ring(2), Ring(2)
        keys_t = [("cosn",), ("sinn",), ("bur",), ("bui",), ("t1",), ("t2",)]

        def tile_pass(ct, second):
            c = ct // 4
            bs = bw_ring.next()
            self.dma(out=r_(BW[:, bs, :]), in_=r_(self.d_sbt[ct]), reads=[], writes=[("bw", bs)], cls=f"bw{bs}")
            if second:
                cs = cw_ring.next()
                self.dma(out=r_(CWt[:, cs, :]), in_=r_(self.d_sct[ct]), reads=[], writes=[("cwt", cs)], cls=f"cwt{cs}")
            for tb in range(2):
                blk = slice(tb * 512, (tb + 1) * 512)
                pb = 2 * b_ring.next()
                self.P.op("pe", lambda e, pb=pb, bs=bs, c=c, blk=blk: e.matmul(PS[pb][:, :], lhsT=r_(BW[:, bs, 0:128]), rhs=r_(HT[:, c, blk]),
                                                                           start=True, stop=True),
                          reads=[("bw", bs), ("HT", c)], writes=[("ps", pb)])
                self.P.op("pe", lambda e, pb=pb, bs=bs, c=c, blk=blk: e.matmul(PS[pb + 1][:, :], lhsT=r_(BW[:, bs, 128:256]), rhs=r_(HT[:, c, blk]),
                                                                           start=True, stop=True),
                          reads=[("bw", bs), ("HT", c)], writes=[("ps", pb + 1)])
                self.ew("dve", lambda e, pb=pb, ct=ct: e.tensor_scalar(out=BUR, in0=PS[pb][:, :], scalar1=FRE[:, ct:ct + 1], scalar2=None, op0=ALU.mult),
                        reads=[("ps", pb)] + kp, writes=[("bur",)])
                self.ew("dve", lambda e, pb=pb, ct=ct: e.scalar_tensor_tensor(out=BUR, in0=PS[pb + 1][:, :], scalar=NFIM[:, ct:ct + 1], in1=BUR,
                                                                          op0=ALU.mult, op1=ALU.add),
                        reads=[("ps", pb + 1), ("bur",)] + kp, writes=[("bur",)])
                self.ew("dve", lambda e, pb=pb, ct=ct: e.tensor_scalar(out=BUI, in0=PS[pb + 1][:, :], scalar1=FRE[:, ct:ct + 1], scalar2=None, op0=ALU.mult),
                        reads=[("ps", pb + 1)] + kp, writes=[("bui",)])
                self.ew("dve", lambda e, pb=pb, ct=ct: e.scalar_tensor_tensor(out=BUI, in0=PS[pb][:, :], scalar=FIM[:, ct:ct + 1], in1=BUI,
                                                                          op0=ALU.mult, op1=ALU.add),
                        reads=[("ps", pb), ("bui",)] + kp, writes=[("bui",)])
                self.ew("dve", lambda e, ct=ct, blk=blk: e.tensor_scalar(out=T1, in0=TG[:, blk], scalar1=U[:, ct:ct + 1], scalar2=None,
                                                                       op0=ALU.mult), reads=[("rstd",)] + kp, writes=[("t1",)])
                self.ew("dve", lambda e: e.tensor_scalar(out=T2, in0=T1, scalar1=RC, scalar2=None, op0=ALU.add),
                        reads=[("t1",)], writes=[("t2",)])
                self.ew("dve", lambda e: e.scalar_tensor_tensor(out=SINN, in0=T2, scalar=RC, in1=T1, op0=ALU.subtract, op1=ALU.subtract),
                        reads=[("t1",), ("t2",)], writes=[("sinn",)])
                self.ew("dve", lambda e: e.tensor_scalar(out=T1, in0=T1, scalar1=0.25, scalar2=None, op0=ALU.add),
                        reads=[("t1",), ("sinn",)], writes=[("t1",)])
                self.ew("dve", lambda e: e.tensor_scalar(out=T2, in0=T1, scalar1=RC, scalar2=None, op0=ALU.add),
                        reads=[("t1",), ("sinn",)], writes=[("t2",)])
                self.ew("dve", lambda e: e.scalar_tensor_tensor(out=COSN, in0=T2, scalar=RC, in1=T1, op0=ALU.subtract, op1=ALU.subtract),
                        reads=[("t1",), ("t2",)], writes=[("cosn",)])
                self.ew("act", lambda e: e.activation(out=SINN, in_=SINN, func=AF.Sin, scale=TWO_PI),
                        reads=[("sinn",)], writes=[("sinn",)])
                self.ew("act", lambda e: e.activation(out=COSN, in_=COSN, func=AF.Sin, scale=TWO_PI),
                        reads=[("cosn",)], writes=[("cosn",)])
                pl = lambda fn, rd, wr: self.ew(S5_ROT_ENGINE, fn, reads=rd, writes=wr)
                pl(lambda e: e.tensor_tensor(out=T1, in0=BUR, in1=COSN, op=ALU.mult), [("bur",), ("cosn",)], [("t1",)])
                pl(lambda e: e.tensor_tensor(out=T2, in0=BUI, in1=SINN, op=ALU.mult), [("bui",), ("sinn",)], [("t2",)])
                pl(lambda e: e.tensor_tensor(out=T1, in0=T1, in1=T2, op=ALU.add), [("t1",), ("t2",)], [("t1",)])
                pl(lambda e: e.tensor_tensor(out=T2, in0=BUI, in1=COSN, op=ALU.mult), [("bui",), ("cosn",), ("t1",)], [("t2",)])
                pl(lambda e: e.tensor_tensor(out=BUR, in0=BUR, in1=SINN, op=ALU.mult), [("bur",), ("sinn",)], [("bur",)])
                pl(lambda e: e.tensor_tensor(out=T2, in0=T2, in1=BUR, op=ALU.subtract), [("t2",), ("bur",)], [("t2",)])
                for comp, TT, tk in ((0, T1, ("t1",)), (1, T2, ("t2",))):
                    if tb == 0:
                        init = INIT[:, ct, comp:comp + 1] if second else 0.0
                        rd = [tk, ("init",)] + kp
                    else:
                        init = STATE[:, ct, comp:comp + 1]
                        rd = [tk, ("state", ct)] + kp
                    self.ew("dve", lambda e, TT=TT, init=init, ct=ct: e.tensor_tensor_scan(
                        out=TT, data0=MAG[:, ct:ct + 1].to_broadcast([128, 512]), data1=TT, initial=init, op0=ALU.mult, op1=ALU.add),
                        reads=rd, writes=[tk])
                for comp, TT, tk in ((0, T1, ("t1",)), (1, T2, ("t2",))):
                    self.ew("act", lambda e, TT=TT, ct=ct, comp=comp: e.activation(out=STATE[:, ct, comp:comp + 1], in_=TT[:, 511:512], func=AF.Copy),
                            reads=[tk], writes=[("state", ct)])
                if not second:
                    continue
                xs = x_ring.next()
                pl(lambda e: e.tensor_tensor(out=BUR, in0=T1, in1=COSN, op=ALU.mult), [("t1",), ("cosn",)], [("bur",)])
                pl(lambda e: e.tensor_tensor(out=BUI, in0=T2, in1=SINN, op=ALU.mult), [("t2",), ("sinn",)], [("bui",)])
                pl(lambda e, xs=xs: e.tensor_tensor(out=r_(XR[:, xs, :]), in0=BUR, in1=BUI, op=ALU.subtract), [("bur",), ("bui",)], [("xr", xs)])
                self.ew("dve", lambda e: e.tensor_tensor(out=BUR, in0=T1, in1=SINN, op=ALU.mult), reads=[("t1",), ("sinn",), ("xr", xs)], writes=[("bur",)])
                self.ew("dve", lambda e: e.tensor_tensor(out=BUI, in0=T2, in1=COSN, op=ALU.mult), reads=[("t2",), ("cosn",), ("xr", xs)], writes=[("bui",)])
                self.ew("dve", lambda e, xs=xs: e.scalar_tensor_tensor(out=r_(XIN[:, xs, :]), in0=BUR, scalar=-1.0, in1=BUI, op0=ALU.mult, op1=ALU.subtract),
                        reads=[("bur",), ("bui",)], writes=[("xin", xs)])
                yb = 4 + tb
                q = ct % 4
                self.P.op("pe", lambda e, yb=yb, cs=cs, xs=xs, q=q: e.matmul(PS[yb][:, :], lhsT=r_(CWt[:, cs, 0:128]), rhs=r_(XR[:, xs, :]),
                                                                         start=(q == 0), stop=False),
                          reads=[("cwt", cs), ("xr", xs)], writes=[("ps", yb)])
                self.P.op("pe", lambda e, yb=yb, cs=cs, xs=xs, q=q: e.matmul(PS[yb][:, :], lhsT=r_(CWt[:, cs, 128:256]), rhs=r_(XIN[:, xs, :]),
                                                                         start=False, stop=(q == 3)),
                          reads=[("cwt", cs), ("xin", xs)], writes=[("ps", yb)])
                if q == 3:
                    self.ew("dve", lambda e, yb=yb, c=c, blk=blk: e.scalar_tensor_tensor(out=PS[yb][:, :], in0=HT[:, c, blk], scalar=SCST[:, c:c + 1],
                                                                                     in1=PS[yb][:, :], op0=ALU.mult, op1=ALU.add),
                            reads=[("HT", c), ("acst",), ("ps", yb)], writes=[("ps", yb)])
                    self.ew("act", lambda e, yb=yb: e.activation(out=PS[6][:, :], in_=PS[yb][:, :], func=AF.Square),
                            reads=[("ps", yb)], writes=[("ps", 6)])
                    self.ew("dve", lambda e: e.tensor_scalar(out=PS[6][:, :], in0=PS[6][:, :], scalar1=0.044715, scalar2=1.0, op0=ALU.mult, op1=ALU.add),
                            reads=[("ps", 6)], writes=[("ps", 6)])
                    self.ew("dve", lambda e, yb=yb, tb=tb: e.tensor_copy(out=r_(YO[:, tb, :]), in_=PS[yb][:, :]),
                            reads=[("ps", yb)], writes=[("yo", tb)])
                    self.ew("dve", lambda e, tb=tb: e.tensor_tensor(out=PS[6][:, :], in0=PS[6][:, :], in1=YO[:, tb, :], op=ALU.mult),
                            reads=[("ps", 6), ("yo", tb)], writes=[("ps", 6)])
                    self.ew("act", lambda e: e.activation(out=PS[6][:, :], in_=PS[6][:, :], func=AF.Sigmoid, scale=1.5957691216057308),
                            reads=[("ps", 6)], writes=[("ps", 6)])
                    self.ew("dve", lambda e, tb=tb: e.tensor_tensor(out=r_(YO[:, tb, :]), in0=YO[:, tb, :], in1=PS[6][:, :], op=ALU.mult),
                            reads=[("ps", 6), ("yo", tb)], writes=[("yo", tb)])
                    self.dma(out=self.yd[c][:, blk], in_=YO[:, tb, :], reads=[("yo", tb)], writes=[("yd", c, tb)], cls=f"yo{tb}")

        for ct in range(64):
            tile_pass(ct, False)
        skeys = [("state", ct) for ct in range(64)]
        self.dma(out=self.c_ins, in_=STATE[:, :, :].rearrange("p a b -> p (a b)"), reads=skeys, writes=[("cin", "s")], cls="tst")
        self.allgather(self.c_ins, self.c_outs, reads=[("cin", "s")], writes=[("cout", "s")])
        self.dma(out=INIT[:, :, :].rearrange("p a b -> p (a b)"), in_=self.c_outs[0:128, :], reads=[("cout", "s")], writes=[("init",)], cls="tld")
        GATE = self.GATE
        self.ew("dve", lambda e: e.tensor_scalar(out=INIT[:, :, :], in0=INIT[:, :, :], scalar1=GATE[:, 0:1], scalar2=None, op0=ALU.mult),
                reads=[("init",), ("gate",)], writes=[("init",)])
        for ct in range(64):
            tile_pass(ct, True)
        P.barrier()
        for c in range(NCH):
            self.dma(out=r_(HT[:, c, 0:T]), in_=r_(self.yd[c]), reads=[("yd", c, 0), ("yd", c, 1)], writes=[("HT", c)], cls=f"xin{c}")
        htkeys = [("HT", k) for k in range(NCH)]
        SG = self.STG
        for m in range(NCH):
            banks = {}
            for half in range(2):
                chunk = m + NCH * half
                wsl = []
                for kh in range(2):
                    s_ = self.wu_ring.next()
                    self.dma(out=r_(WU[:, s_, :]), in_=r_(self.d_wglu[chunk][:, kh * 1024:(kh + 1) * 1024]),
                             reads=[], writes=[("wu", s_)], cls=f"wu{s_}")
                    wsl.append(s_)
                for th in range(2):
                    bank = half * 2 + th
                    banks[(half, th)] = bank
                    pairs = [(r_(WU[:, wsl[k // 8], (k % 8) * 128:(k % 8 + 1) * 128]), r_(HT[:, k, th * 512:(th + 1) * 512])) for k in range(NCH)]
                    self.mm_group(PS[bank][:, :], pairs, reads=[("wu", wsl[0]), ("wu", wsl[1])] + htkeys, writes=[("ps", bank)])
            for th in range(2):
                bv, bg = banks[(0, th)], banks[(1, th)]
                self.ew("act", lambda e, bg=bg, th=th, m=m: e.activation(out=SG[:, th, 0:512], in_=PS[bg][:, :], func=AF.Sigmoid,
                                                                       bias=SCST[:, 32 + m:33 + m]),
                        reads=[("ps", bg), ("acst",)], writes=[("sg", th)])
                self.ew("dve", lambda e, bv=bv, th=th, m=m: e.scalar_tensor_tensor(out=SG[:, th, 0:512], in0=PS[bv][:, :], scalar=SCST[:, 16 + m:17 + m],
                                                                               in1=SG[:, th, 0:512], op0=ALU.add, op1=ALU.mult),
                        reads=[("ps", bv), ("sg", th), ("acst",)], writes=[("sg", th)])
                self.ew("dve", lambda e, th=th, m=m: e.tensor_tensor(out=X[:, m, HX + th * 512:HX + (th + 1) * 512],
                                                                   in0=X[:, m, HX + th * 512:HX + (th + 1) * 512], in1=SG[:, th, 0:512], op=ALU.add),
                        reads=[("sg", th), ("X", m)], writes=[("X", m)])

    def ffn(self, l):
        X, HT, PS = self.X, self.HT, self.PS
        WU, WD, STG, CG, G, CW = self.WU, self.WD, self.STG, self.CG, self.G, self.CW
        self.dma(out=CW[:, :], in_=self.d_cw[l], reads=[], writes=[("cw",)], cls="c_cw")
        self.rmsnorm(DEPTH + l, HX - 2, [(0, 342), (342, 342), (684, 342)])
        GF = 2
        ngroups = NFP // GF
        htkeys = [("HT", k) for k in range(NCH)]

        def down(g, gs):
            for ch in range(2):
                slots = []
                for jj in range(GF):
                    j = g * GF + jj
                    s = self.wd_ring.next()
                    self.dma(out=r_(WD[:, s, :]), in_=r_(self.d_wd[l][j][:, ch * 1024:(ch + 1) * 1024]),
                             reads=[], writes=[("wd", s)], cls=f"wd{s}")
                    slots.append(s)
                for mm in range(8):
                    m = ch * 8 + mm
                    for th in range(2):
                        bank = 4 + self.dn_ring.next()
                        pairs = [(r_(WD[:, slots[jj], mm * 128:(mm + 1) * 128]),
                                  r_(G[:, gs * GF + jj, th * 512:(th + 1) * 512])) for jj in range(GF)]
                        self.mm_group(PS[bank][:, :], pairs,
                                      reads=[("wd", s) for s in slots] + [("g", gs * GF + jj) for jj in range(GF)],
                                      writes=[("ps", bank)])
                        self.ew("dve", lambda e, bank=bank, m=m, th=th: e.tensor_tensor(
                            out=X[:, m, HX + th * 512:HX + (th + 1) * 512],
                            in0=X[:, m, HX + th * 512:HX + (th + 1) * 512], in1=PS[bank][:, :], op=ALU.add),
                            reads=[("ps", bank), ("X", m)], writes=[("X", m)])

        for g in range(ngroups):
            gs = g % 2
            for jj in range(GF):
                j = g * GF + jj
                gi = gs * GF + jj
                for half in range(2):
                    chunk = j + NFP * half
                    wsl = []
                    for kh in range(2):
                        s = self.wu_ring.next()
                        self.dma(out=r_(WU[:, s, :]), in_=r_(self.d_wu[l][chunk][:, kh * 1024:(kh + 1) * 1024]),
                                 reads=[], writes=[("wu", s)], cls=f"wu{s}")
                        wsl.append(s)
                    for blk in range(3):
                        bank = self.up_ring.next()
                        pairs = [(r_(WU[:, wsl[k // 8], (k % 8) * 128:(k % 8 + 1) * 128]),
                                  r_(HT[:, k, blk * 342:(blk + 1) * 342])) for k in range(NCH)]
                        self.mm_group(PS[bank][:, 0:342], pairs,
                                      reads=[("wu", wsl[0]), ("wu", wsl[1])] + htkeys, writes=[("ps", bank)])
                        self.ew("act", lambda e, bank=bank, half=half, blk=blk: e.activation(
                            out=STG[:, half, blk * 342:(blk + 1) * 342], in_=PS[bank][:, 0:342], func=AF.Copy),
                            reads=[("ps", bank)], writes=[("stg", half, blk)])
                    eng = "dve" if half == 0 else "pool"
                    dst = r_(G[:, gi, :]) if half == 0 else CG[:, :]
                    dsrc = G[:, gi, :] if half == 0 else CG[:, :]
                    dkey = ("g", gi) if half == 0 else ("cg",)
                    cwb = chunk * 4
                    self.ew(eng, lambda e, dst=dst, half=half, cwb=cwb: e.tensor_scalar(
                        out=dst, in0=STG[:, half, 2:2 + T], scalar1=CW[:, cwb + 2:cwb + 3], scalar2=CW[:, cwb + 3:cwb + 4],
                        op0=ALU.mult, op1=ALU.add), reads=[("stg", half, 0), ("stg", half, 1), ("stg", half, 2), ("cw",)], writes=[dkey])
                    for tap in (1, 0):
                        if eng == "dve":
                            self.ew(eng, lambda e, dst=dst, dsrc=dsrc, half=half, cwb=cwb, tap=tap: e.scalar_tensor_tensor(
                                out=dst, in0=STG[:, half, tap:tap + T], scalar=CW[:, cwb + tap:cwb + tap + 1], in1=dsrc,
                                op0=ALU.mult, op1=ALU.add), reads=[("stg", half, 0), ("stg", half, 1), ("stg", half, 2), ("cw",), dkey], writes=[dkey])
                        else:
                            PT = self.RSTD[:, 0:T]
                            self.ew(eng, lambda e, half=half, cwb=cwb, tap=tap, PT=PT: e.tensor_scalar(
                                out=PT, in0=STG[:, half, tap:tap + T], scalar1=CW[:, cwb + tap:cwb + tap + 1], scalar2=self.CCOL[:, 0:1],
                                op0=ALU.mult, op1=ALU.add), reads=[("stg", half, 0), ("stg", half, 1), ("stg", half, 2), ("cw",), ("ccol",)], writes=[("rstd",)])
                            self.ew(eng, lambda e, dst=dst, dsrc=dsrc, PT=PT: e.tensor_tensor(
                                out=dst, in0=dsrc, in1=PT, op=ALU.add), reads=[("rstd",), dkey], writes=[dkey])
                self.ew("act", lambda e: e.activation(out=CG[:, :], in_=CG[:, :], func=AF.Silu),
                        reads=[("cg",)], writes=[("cg",)])
                self.ew("dve", lambda e, gi=gi: e.tensor_tensor(out=r_(G[:, gi, :]), in0=G[:, gi, :], in1=CG[:, :], op=ALU.mult),
                        reads=[("cg",), ("g", gi)], writes=[("g", gi)])
            if g >= 1:
                down(g - 1, (g - 1) % 2)
        down(ngroups - 1, (ngroups - 1) % 2)

    def build(self, ncores):
        nc = self.nc
        self.ncores = ncores
        self.declare_dram()
        P = self.P
        from contextlib import ExitStack
        with ExitStack() as es:
            sb = lambda name, shape: es.enter_context(nc.sbuf_tensor(name, shape, F32))
            self.X = sb("X", [128, NCH, XW])
            self.HT = sb("HT", [128, NCH, XW])
            self.RSTD = sb("RSTD", [128, XW])
            self.SQ = sb("SQ", [128, 2, 512])
            self.ACST = sb("ACST", [128, 128])
            self.SPRM = sb("SPRM", [128, 13, 64])
            self.NEGPI = sb("NEGPI", [128, 1])
            self.CCOL = sb("CCOL", [128, 4])
            self.STATE = sb("STATE", [128, 64, 2])
            self.INIT = sb("INIT", [128, 64, 2])
            self.ONES = sb("ONES", [128, 128])
            self.ONESF = sb("ONESF", [128, 128])
            self.GAIN = sb("GAIN", [128, 2 * DEPTH * NCH])
            self.GATE = sb("GATE", [128, 1])
            self.INVC = sb("INVC", [128, 64])
            self.TB = sb("TB", [128, NCH, 16])
            self.CW = sb("CW", [128, 2 * NFP * 4])
            self.PBS = sb("PBS", [128, 48])
            self.T16 = sb("T16", [128, 4, 16])
            self.WU = sb("WU", [128, 4, 1024])
            self.WD = sb("WD", [128, 4, 1024])
            self.STG = sb("STG", [128, 2, XW])
            self.CG = sb("CG", [128, T])
            self.G = sb("G", [128, 4, T])
            self.PW = self.WU[:, 0:4, :].rearrange("p (a b) n -> p a (b n)", a=2)
            self.PL = self.G
            self.TA = self.STG
            self.PS = [es.enter_context(nc.psum_tensor(f"ps{i}", [128, 512], F32)) for i in range(8)]
            self.TBF = None
            self.TMP = self.CG[:, :].rearrange("p (a n) -> p a n", a=2)
            self.sq_ring = Ring(2)
            self.wu_ring = Ring(4)
            self.wd_ring = Ring(4)
            self.up_ring = Ring(4)
            self.dn_ring = Ring(3)
            self.pw_ring = Ring(2)
            self.tmp_ring = Ring(2)

            xkeys = [("X", c) for c in range(NCH)]
            for c in range(NCH):
                self.dma(out=self.X[:, c, :], in_=self.d_x[:, c, :], reads=[], writes=[("X", c)], cls=f"xin{c}")
            self.dma(out=self.GAIN[:, :], in_=self.d_gain, reads=[], writes=[("gain",)], cls="c_gain")
            self.dma(out=self.GATE[:, :], in_=self.d_gate, reads=[], writes=[("gate",)], cls="c_gate")
            self.dma(out=self.INVC[:, :], in_=self.d_invc, reads=[], writes=[("invc",)], cls="c_invc")
            self.ew("pool", lambda e: e.memset(self.ONESF[:, :], 1.0), reads=[], writes=[("onesf",)])
            self.ew("pool", lambda e: e.memset(self.CCOL[:, 0:1], 0.0), reads=[], writes=[("ccol",)])
            self.ew("pool", lambda e: e.memset(self.CCOL[:, 1:2], 12582912.0), reads=[("ccol",)], writes=[("ccol",)])
            self.ew("pool", lambda e: e.memset(self.CCOL[:, 2:3], 0.25), reads=[("ccol",)], writes=[("ccol",)])
            self.ew("pool", lambda e: e.memset(self.CCOL[:, 3:4], 1.0), reads=[("ccol",)], writes=[("ccol",)])
            self.ew("pool", lambda e: e.memset(self.NEGPI[:, :], 0.0), reads=[], writes=[("negpi",)])
            self.ew("dve", lambda e: e.tensor_copy(out=r_(self.ONES[:, :]), in_=self.ONESF[:, :]),
                    reads=[("onesf",)], writes=[("ones",)])
            for l in self.layers:
                kind = l % 3
                if kind == 0:
                    if l != 0:
                        if l != self.first:
                            self.exchange_tail(16)
                    P.barrier()
                    self.pool_mixer(l)
                elif kind == 1:
                    P.barrier()
                    self.attention(l)
                else:
                    P.barrier()
                    self.s5(l)
                P.barrier()
                self.exchange_tail(2)
                self.ffn(l)
                P.barrier()
            for c in range(NCH):
                self.dma(out=self.d_out[:, c, :], in_=self.X[:, c, HX:XW], reads=[("X", c)], writes=[("out", c)], cls=f"xin{c}")

            clss = P.classes()
            names = list(Prog.ENGINES[:4]) + clss
            sems = {}
            for n in names:
                sems[n] = es.enter_context(nc.semaphore("s_" + n))
            block = es.enter_context(nc.Block())
            P.emit(block, sems)
        return nc


def _feat_major(v):
    n = v.shape[-1] // 128
    return np.ascontiguousarray(v.reshape(n, 128).T)


def prep_shared(inp, layers):
    sh = {}
    gains = np.concatenate([inp["norm_mix_g"], inp["norm_ffn_g"]], axis=0)
    sh["gains"] = np.ascontiguousarray(gains.reshape(2 * DEPTH, NCH, 128).transpose(2, 0, 1).reshape(128, -1))
    for l in layers:
        wu = inp["ffn_w_up"][l]
        sh[f"wu{l}"] = np.ascontiguousarray(wu.reshape(NCH, 128, 2 * NFP, 128).transpose(2, 1, 0, 3).reshape(2 * NFP, 128, NCH * 128))
        sh[f"wd{l}"] = np.ascontiguousarray(inp["ffn_w_down"][l].reshape(NFP, 128, D))
        cw = np.concatenate([inp["ffn_conv_w"][l], inp["ffn_conv_b"][l][None]], axis=0)
        sh[f"cw{l}"] = np.ascontiguousarray(cw.reshape(4, 2 * NFP, 128).transpose(2, 1, 0).reshape(128, -1))
        if l % 3 == 2:
            j = l // 3
            ch = lambda v: np.ascontiguousarray(v.reshape(64, 128).T)
            ls = np.repeat(inp["ssm_log_step"][j][:, None], 64, axis=1)
            sh[f"sprm{l}"] = np.ascontiguousarray(np.concatenate([ch(inp["ssm_lam_re"][j]), ch(inp["ssm_lam_im"][j]), ch(ls)], axis=1))
            sbt = np.zeros((64, 128, 256), np.float32)
            sct = np.zeros((64, 128, 256), np.float32)
            for ct in range(64):
                q = ct % 4
                for gg in range(2):
                    g = 2 * ct + gg
                    rows = slice(q * 32 + gg * 16, q * 32 + gg * 16 + 16)
                    cols = slice(gg * 64, gg * 64 + 64)
                    sbt[ct, rows, 0:128][:, cols] = inp["ssm_b_re"][j][g].T
                    sbt[ct, rows, 128:256][:, cols] = inp["ssm_b_im"][j][g].T
                    sct[ct, cols, 0:128][:, rows] = inp["ssm_c_re"][j][g].T
                    sct[ct, cols, 128:256][:, rows] = inp["ssm_c_im"][j][g].T
            sh[f"sbt{l}"] = sbt
            sh[f"sct{l}"] = sct
            sh[f"scst{l}"] = np.ascontiguousarray(np.concatenate([_feat_major(inp["ssm_d"][j]), _feat_major(inp["ssm_b_glu"][j])], axis=1))
            sh[f"wglu{l}"] = np.ascontiguousarray(inp["ssm_w_glu"][j].reshape(NCH, 128, 32, 128).transpose(2, 1, 0, 3).reshape(32, 128, NCH * 128))
        if l % 3 == 1:
            j = l // 3
            wqkv = inp["sb_w_qkv"][j]
            sh[f"wqk{l}"] = np.ascontiguousarray(wqkv[:, 0:4096].reshape(NCH, 128, 32, 128).transpose(2, 1, 0, 3).reshape(32, 128, NCH * 128))
            sh[f"wv{l}"] = np.ascontiguousarray(wqkv[:, 4096:6144])
            sh[f"wo{l}"] = np.ascontiguousarray(inp["sb_w_o"][j].reshape(NCH, 128, NCH, 128).transpose(2, 1, 0, 3).reshape(NCH, 128, NCH * 128))
            acst = np.zeros((128, 18), np.float32)
            acst[:, 0:16] = np.arange(128, dtype=np.float32)[:, None] + 128.0 * np.arange(16, dtype=np.float32)[None, :]
            acst[:, 16] = inp["sb_q_gain"][j]
            acst[:, 17] = inp["sb_k_gain"][j]
            sh[f"acst{l}"] = acst
            sh["tri"] = np.ascontiguousarray(np.tril(np.ones((128, 128), np.float32)))
        if l % 3 == 0:
            j = l // 3
            pw = inp["pool_w"][j]
            sh[f"pw{l}"] = np.ascontiguousarray(pw.reshape(4, 4, 128, 512).transpose(0, 2, 1, 3).reshape(4, 128, 2048))
            sh[f"pbs{l}"] = np.ascontiguousarray(np.concatenate([_feat_major(inp["pool_b"][j]), _feat_major(inp["pool_scale"][j])], axis=1))
    return sh


def prep_core(xT_full_b, h):
    m = {}
    xs = np.zeros((D, XW), np.float32)
    lo = h * T
    xs[:, HX:] = xT_full_b[:, lo:lo + T]
    if h == 1:
        xs[:, :HX] = xT_full_b[:, lo - HX:lo]
    m["xT"] = np.ascontiguousarray(xs.reshape(NCH, 128, XW).transpose(1, 0, 2))
    m["gate"] = np.full((128, 1), float(h), np.float32)
    invc = np.zeros((4, 16), np.float32)
    for g in range(4):
        w = 2 << g
        for i in range(16):
            t = lo + i
            invc[g, i] = 1.0 / min(t + 1, w)
    m["invc"] = np.ascontiguousarray(np.broadcast_to(invc.reshape(1, 64), (128, 64)))
    m["tpos"] = np.ascontiguousarray(np.broadcast_to((lo + np.arange(T, dtype=np.float32))[None, :], (128, T)))
    return m


_NC_CACHE = {}


def run_layers(xT_all, inp, layers, nb=BATCH):
    ncores = 2 * nb
    key = (tuple(layers), ncores)
    if key not in _NC_CACHE:
        nc = bass.Bass("TRN2", target_bir_lowering=False)
        nc.dge_precook = False
        bld = Builder(nc, list(layers), layers[0], layers[-1])
        bld.build(ncores)
        _NC_CACHE[key] = (nc, list(bld.in_names))
    nc, in_names = _NC_CACHE[key]
    sh = prep_shared(inp, layers)
    in_maps = []
    for b in range(nb):
        for h in range(2):
            m = dict(sh)
            m.update(prep_core(xT_all[b], h))
            in_maps.append({k: m[k] for k in in_names})
    import time as _t, sys as _s
    _t0 = _t.time()
    res = run_bass_kernel_spmd(nc, in_maps, core_ids=list(range(ncores)))
    print("[kernel] launch layers=%s took %.1fs" % (list(layers), _t.time() - _t0), file=_s.stderr)
    out = np.zeros_like(xT_all)
    for b in range(nb):
        for h in range(2):
            o = res.results[2 * b + h]["outT"]
            out[b][:, h * T:(h + 1) * T] = o.transpose(1, 0, 2).reshape(D, T)
    return out


LAUNCH_GROUPS = [[0, 1, 2, 3]]


def kernel(**inputs):
    inp = {k: np.asarray(v) for k, v in inputs.items()}
    x = inp["x"]
    xT = np.ascontiguousarray(x.transpose(0, 2, 1))
    for grp in LAUNCH_GROUPS:
        xT = run_layers(xT, inp, grp)
    return np.ascontiguousarray(xT.transpose(0, 2, 1)).astype(np.float32)
```
